# Optimizing a Trainium2 kernel written in Bass

```python
import jax, jax.numpy as jnp
from jax import lax
import numpy as np

D_MODEL = 2048
BATCH = 4
SEQ = 2048
DEPTH = 1

CHUNK = 64
D_CONV = D_MODEL // 2
CONV_K = 3
D_POOL = D_MODEL // 2
POOL_WINDOWS = (2, 4, 8, 16)
N_POOL_GROUPS = len(POOL_WINDOWS)
POOL_GROUP_W = D_POOL // N_POOL_GROUPS
POOL_GROUP_OUT = D_MODEL // N_POOL_GROUPS
D_FF = ((8 * D_MODEL // 3 + 255) // 256) * 256
D_IN = 3 * D_CONV + D_POOL + 2 * D_MODEL
EPS = 1e-6

kernel_name = "hybrid_shortconv_multipool_gated_block"


def rmsnorm(x, g):
    xf = x.astype(jnp.float32)
    y = xf * lax.rsqrt(jnp.mean(xf * xf, axis=-1, keepdims=True) + EPS)
    return (y * g.astype(jnp.float32)).astype(x.dtype)


def causal_depthwise_conv(u, w, b):
    S = u.shape[1]
    up = jnp.pad(u, ((0, 0), (CONV_K - 1, 0), (0, 0)))
    out = b
    for k in range(CONV_K):
        out = out + w[k] * up[:, k:k + S]
    return out


def multiscale_pool(v):
    B, S, _ = v.shape
    vg = v.reshape(B, S, N_POOL_GROUPS, POOL_GROUP_W)
    vf = vg.astype(jnp.float32)
    cs = jnp.cumsum(vf, axis=1)
    cs = jnp.concatenate([jnp.zeros_like(cs[:, :1]), cs], axis=1)
    t = jnp.arange(S, dtype=jnp.int32)[:, None]
    win = jnp.asarray(POOL_WINDOWS, dtype=jnp.int32)[None, :]
    lo = jnp.maximum(t + 1 - win, 0)
    cnt = (t + 1 - lo).astype(jnp.float32)
    g_idx = jnp.arange(N_POOL_GROUPS, dtype=jnp.int32)[None, :]
    window_sum = cs[:, 1:] - cs[:, lo, g_idx, :]
    mean = window_sum / cnt[None, :, :, None]
    return (mean - vf).astype(v.dtype)


def swiglu(h, w_gate, w_up, w_down):
    return (jax.nn.silu(h @ w_gate) * (h @ w_up)) @ w_down


def setup_inputs(seed: int = 0) -> dict:
    key = jax.random.key(seed)
    ks = jax.random.split(key, 16)
    f32 = jnp.float32
    L = DEPTH
    nrm = lambda k, shape, fan_in: jax.random.normal(k, shape, f32) * (fan_in ** -0.5)
    return {
        "x": jax.random.normal(ks[0], (BATCH, SEQ, D_MODEL), f32),
        "norm1_g": 1.0 + 0.02 * jax.random.normal(ks[1], (L, D_MODEL), f32),
        "w_in": nrm(ks[2], (L, D_MODEL, D_IN), D_MODEL),
        "b_gate": 0.01 * jax.random.normal(ks[3], (L, 2 * D_MODEL), f32),
        "conv_w": nrm(ks[4], (L, CONV_K, D_CONV), CONV_K),
        "conv_b": 0.01 * jax.random.normal(ks[5], (L, D_CONV), f32),
        "w_a_out": nrm(ks[6], (L, D_CONV, D_MODEL), D_CONV),
        "w_pool": nrm(ks[7], (L, N_POOL_GROUPS, POOL_GROUP_W, POOL_GROUP_OUT), POOL_GROUP_W),
        "pool_scale": 1.0 + 0.02 * jax.random.normal(ks[8], (L, D_MODEL), f32),
        "w_o": nrm(ks[9], (L, D_MODEL, D_MODEL), D_MODEL),
        "norm2_g": 1.0 + 0.02 * jax.random.normal(ks[10], (L, D_MODEL), f32),
        "w_ffn_gate": nrm(ks[11], (L, D_MODEL, D_FF), D_MODEL),
        "w_ffn_up": nrm(ks[12], (L, D_MODEL, D_FF), D_MODEL),
        "w_ffn_down": nrm(ks[13], (L, D_FF, D_MODEL), D_FF),
        "final_g": 1.0 + 0.02 * jax.random.normal(ks[14], (D_MODEL,), f32),
    }


def reference(x, norm1_g, w_in, b_gate, conv_w, conv_b, w_a_out, w_pool, pool_scale,
              w_o, norm2_g, w_ffn_gate, w_ffn_up, w_ffn_down, final_g):
    B, S, _ = x.shape
    splits = np.cumsum([D_CONV, D_CONV, D_CONV, D_POOL, D_MODEL]).tolist()
    for l in range(DEPTH):
        h = rmsnorm(x, norm1_g[l])
        proj = h @ w_in[l]
        b_a, c_a, v_a, v_b, g_a, g_b = jnp.split(proj, splits, axis=-1)

        u = causal_depthwise_conv(c_a * v_a, conv_w[l], conv_b[l])
        y_a = (b_a * u) @ w_a_out[l]

        p = multiscale_pool(v_b)
        y_b = jnp.einsum("bsgc,gcd->bsgd", p, w_pool[l]).reshape(B, S, D_MODEL)
        y_b = y_b * pool_scale[l]

        gb = b_gate[l]
        merged = jax.nn.sigmoid(g_a + gb[:D_MODEL]) * y_a + jax.nn.sigmoid(g_b + gb[D_MODEL:]) * y_b
        x = x + merged @ w_o[l]

        h2 = rmsnorm(x, norm2_g[l])
        x = x + swiglu(h2, w_ffn_gate[l], w_ffn_up[l], w_ffn_down[l])
    return rmsnorm(x, final_g)
```

```python
from contextlib import ExitStack

import numpy as np
import concourse.bass as bass
import concourse.mybir as mybir
from concourse.bass_utils import run_bass_kernel_spmd

F32 = mybir.dt.float32
BF16 = mybir.dt.bfloat16
I32 = mybir.dt.int32
AF = mybir.ActivationFunctionType
ALU = mybir.AluOpType

NCORES = 8
D = 2048
KD = 16
T = 1024
H = 16
TT = T + H
NTOK = T // 128
DI = 8192
NF = 44
G = 4
FG = NF // G
SLOT = 6144
NSLOT = 4
EPS = 1e-6
POOL_W = (2, 4, 8, 16)


JOB_SIZES = ([6144] * 8 + [6144, 6144, 4096] + [5376] * 16 + [4096] * 8
             + ([4096] * FG + [FG * 512] * 4) * G)
WTOTAL = sum(JOB_SIZES)


def pack_weights(inputs):
    f = lambda a: np.asarray(a, dtype=np.float32)
    w_in = f(inputs["w_in"]).reshape(D, DI)
    w_a_out = f(inputs["w_a_out"]).reshape(1024, D)
    w_pool = f(inputs["w_pool"]).reshape(1024, 512)
    w_o = f(inputs["w_o"]).reshape(D, D)
    w_gate = f(inputs["w_ffn_gate"]).reshape(D, NF * 128)
    w_up = f(inputs["w_ffn_up"]).reshape(D, NF * 128)
    w_down = f(inputs["w_ffn_down"]).reshape(NF * 128, D)

    def kv(a):
        return a.reshape(a.shape[0] // 128, 128, a.shape[1]).transpose(1, 0, 2)

    blocks = []
    for j in range(8):
        blocks.append(np.stack([kv(w_in[:, wi * 1024 + j * 128:wi * 1024 + (j + 1) * 128]) for wi in range(3)],
                               axis=2).reshape(128, -1))
    for q in range(3):
        n = 3 if q < 2 else 2
        blocks.append(kv(w_in[:, 3072 + q * 384:3072 + q * 384 + n * 128]).reshape(128, -1))
    for j in range(16):
        g = j // 4
        blocks.append(np.concatenate([
            np.stack([kv(w_in[:, 4096 + j * 128:4096 + (j + 1) * 128]),
                      kv(w_in[:, 6144 + j * 128:6144 + (j + 1) * 128])], axis=2).reshape(128, -1),
            kv(w_a_out[:, j * 128:(j + 1) * 128]).reshape(128, -1),
            kv(w_pool[g * 256:(g + 1) * 256, (j % 4) * 128:(j % 4 + 1) * 128]).reshape(128, -1)], axis=1))
    for db in range(4):
        for hh in range(2):
            blocks.append(kv(w_o[hh * 1024:(hh + 1) * 1024, db * 512:(db + 1) * 512]).reshape(128, -1))
    for gi in range(G):
        for fi in range(FG):
            fidx = gi * FG + fi
            blocks.append(np.concatenate([kv(w_gate[:, fidx * 128:(fidx + 1) * 128]).reshape(128, -1),
                                          kv(w_up[:, fidx * 128:(fidx + 1) * 128]).reshape(128, -1)], axis=1))
        for db in range(4):
            blocks.append(kv(w_down[gi * FG * 128:(gi + 1) * FG * 128, db * 512:(db + 1) * 512]).reshape(128, -1))
    assert [b.shape[1] for b in blocks] == JOB_SIZES
    return np.ascontiguousarray(np.concatenate(blocks, axis=1))


class Sem:
    def __init__(self, h):
        self.h = h
        self.count = 0


class Res:
    def __init__(self, name):
        self.name = name
        self.reset()

    def reset(self):
        self.w = []
        self.r = []
        self.pr = []


class Emitter:
    def __init__(self, nc):
        self.nc = nc
        self.sems = []
        self.res = []
        self.cur = None
        self.eng = None
        self.waited = {}
        self.engsem = {}

    def sem(self, h):
        s = Sem(h)
        self.sems.append(s)
        return s

    def resource(self, name):
        r = Res(name)
        self.res.append(r)
        return r

    def begin_pass(self, name, eng):
        self.cur = name
        self.eng = eng
        self.waited = {}
        for s in self.sems:
            s.count = 0
        for r in self.res:
            r.reset()

    def wait_tokens(self, engname, toks):
        if self.cur != engname:
            return
        need = {}
        for (s, v) in toks:
            if need.get(s, 0) < v:
                need[s] = v
        for s, v in need.items():
            if self.waited.get(s, 0) >= v:
                continue
            self.eng.wait_ge(s.h, v)
            self.waited[s] = v

    def op(self, engname, fn, reads=(), writes=(), sem=None, inc=1):
        toks = []
        for r in reads:
            toks += r.w
        for w in writes:
            toks += w.w + w.r + w.pr
        self.wait_tokens(engname, toks)
        ins = fn(self.eng) if self.cur == engname else None
        s = sem if sem is not None else self.engsem[engname]
        s.count += inc
        if ins is not None:
            ins.then_inc(s.h, inc)
        tok = (s, s.count)
        for r in reads:
            r.r.append(tok)
        for w in writes:
            if w.r:
                w.pr = w.r
                w.r = []
                w.w = [tok]
            else:
                w.w.append(tok)
        return tok


def build_nc(debug=False):
    nc = bass.Bass("TRN2", target_bir_lowering=False)

    def din(name, shape, dt=F32):
        return nc.dram_tensor(name, list(shape), dt, kind="ExternalInput").ap()

    x_c = din("x_c", [TT, D])
    pos16 = din("pos16", [128, 16])
    g1_d = din("norm1_g", [1, D])
    wpack = din("wpack", [128, WTOTAL])
    sparams = din("sparams", [80, 128])
    g2_d = din("norm2_g", [1, D])
    gf_d = din("final_g", [1, D])
    out_d = nc.dram_tensor("out", [T, D], F32, kind="ExternalOutput").ap()
    dbg = {}
    if debug:
        for nm, shp, dt in [("dbg_hT", [128, KD * TT], BF16), ("dbg_bu", [128, 8 * T], BF16),
                            ("dbg_pl", [128, 8 * T], BF16), ("dbg_merged", [128, KD * T], BF16),
                            ("dbg_x1", [128, NTOK * D], F32), ("dbg_h2T", [128, KD * T], BF16),
                            ("dbg_prm", [128, 80], F32)]:
            dbg[nm] = nc.dram_tensor(nm, shp, dt, kind="ExternalOutput").ap()

    with ExitStack() as es:
        def sb(name, shape, dt):
            return es.enter_context(nc.sbuf_tensor(name, shape, dt))

        def hsem(name):
            return es.enter_context(nc.semaphore(name))

        arenaA = sb("arenaA", [128, 16512], F32)
        arenaB = sb("arenaB", [128, 8192], F32)
        arenaC = sb("arenaC", [128, 12288], F32)
        wring = sb("wring", [128, NSLOT * SLOT], BF16)
        identf = sb("identf", [128, 128], F32)
        identb = sb("identb", [128, 128], BF16)
        iot = sb("iot", [128, 128], I32)
        prm_rows = sb("prm_rows", [128, 128], F32)
        prm = sb("prm", [128, 80], F32)
        ssb = sb("ssb", [128, 32], F32)
        rsb = sb("rsb", [128, 32], F32)
        rstdb = sb("rstdb", [128, 32], F32)
        pos_sb = sb("pos_sb", [128, 16], F32)
        cnt_sb = sb("cnt_sb", [128, 64], F32)
        inv_sb = sb("inv_sb", [128, 64], F32)
        tmp16 = sb("tmp16", [128, 16], F32)
        junk = sb("junk", [128, 4], F32)
        zeros_bf = sb("zeros_bf", [128, 512], BF16)
        ps = es.enter_context(nc.psum_tensor("ps", [128, 4096], F32))
        h_act = hsem("s_act"); h_dve = hsem("s_dve"); h_pe = hsem("s_pe"); h_pool = hsem("s_pool")
        h_c = hsem("s_c"); h_x0 = hsem("s_x0"); h_x1 = hsem("s_x1"); h_x2 = hsem("s_x2")
        h_w0 = hsem("s_w0"); h_w1 = hsem("s_w1"); h_w2 = hsem("s_w2"); h_w3 = hsem("s_w3")
        h_o0 = hsem("s_o0"); h_o1 = hsem("s_o1"); h_g = hsem("s_g"); h_o2 = hsem("s_o2"); h_o3 = hsem("s_o3")
        h_r0 = hsem("s_r0"); h_r1 = hsem("s_r1"); h_r2 = hsem("s_r2"); h_r3 = hsem("s_r3")
        h_r4 = hsem("s_r4"); h_r5 = hsem("s_r5"); h_r6 = hsem("s_r6"); h_r7 = hsem("s_r7")
        h_dbg = hsem("s_dbg")
        h_p = hsem("s_p")
        block = es.enter_context(nc.Block())
        em = Emitter(nc)
        em.engsem = {"act": em.sem(h_act), "dve": em.sem(h_dve), "pe": em.sem(h_pe),
                     "pool": em.sem(h_pool)}
        s_c = em.sem(h_c)
        s_x = [em.sem(h_x0), em.sem(h_x1), em.sem(h_x2)]
        s_w = [em.sem(h_w0), em.sem(h_w1), em.sem(h_w2), em.sem(h_w3)]
        s_g = em.sem(h_g)
        s_r = [em.sem(h) for h in (h_r0, h_r1, h_r2, h_r3, h_r4, h_r5, h_r6, h_r7)]
        s_oh = [[em.sem(h_)] for h_ in (h_o0, h_o1, h_o2, h_o3)]
        s_dbg = em.sem(h_dbg)
        s_p = em.sem(h_p)

        A = arenaA
        hT = A[:, 0:8320].bitcast(BF16).rearrange("p (k t) -> p k t", t=TT)
        bu = A[:, 8320:12416].bitcast(BF16).rearrange("p (k t) -> p k t", t=T)
        pl = A[:, 12416:16512].bitcast(BF16).rearrange("p (k t) -> p k t", t=T)
        x1 = A[:, 0:16384].rearrange("p (i d) -> p i d", d=D)
        mg = arenaB[:, :].bitcast(BF16).rearrange("p (k t) -> p k t", t=T)
        ost = [arenaB[:, 2048 * q_:2048 * (q_ + 1)] for q_ in range(4)]
        C = arenaC
        gb = C[:, 0:2048]
        hb = [C[:, 2048:3072].bitcast(BF16), C[:, 3072:4096].bitcast(BF16)]
        sq = C[:, 4096:5120].bitcast(BF16)
        FB = 5120
        hb2 = hb + [C[:, FB + 1024 * n_:FB + 1024 * (n_ + 1)].bitcast(BF16) for n_ in range(6)]
        hbc = C[:, FB + 6144:FB + 7168].bitcast(BF16)
        xt = ([C[:, FB:FB + 2048], C[:, FB + 2048:FB + 4096], C[:, FB + 4096:FB + 6144]]
              + [arenaB[:, 2048 * n_:2048 * (n_ + 1)] for n_ in range(4)]
              + [A[:, 8320 + 2048 * n_:8320 + 2048 * (n_ + 1)] for n_ in range(2)])
        R1 = C[:, FB:FB + 1040]
        R2 = C[:, FB + 1040:FB + 2080]
        R3 = C[:, FB + 2080:FB + 3120]
        sga = [C[:, FB + 3120:FB + 3632], C[:, FB + 3632:FB + 4144]]
        sgb = [C[:, FB + 4144:FB + 4656], C[:, FB + 4656:FB + 5168]]
        m1 = C[:, FB + 5168:FB + 5680]
        m2 = C[:, FB + 5680:FB + 6192]
        act = C[:, FB:FB + 5632].bitcast(BF16).rearrange("p (k t) -> p k t", t=T)
        sg = [C[:, FB + 5632:FB + 6144], C[:, FB + 6144:FB + 6656]]

        def slot(s, n=SLOT):
            return wring[:, s * SLOT:s * SLOT + n]

        def bank(b):
            return ps[:, b * 512:(b + 1) * 512]

        def pTb(q):
            return ps[:, q * 1024:(q + 1) * 1024].bitcast(BF16).rearrange("p (k t) -> p k t", t=128)

        R = {}
        for nm in ["zeros", "iot", "identf", "identb", "prm_rows", "prm", "pos", "cnt", "inv", "tmp16",
                   "gb", "sq", "hT", "bu", "pl", "mg", "R1", "R2", "R3", "m1", "m2"]:
            R[nm] = em.resource(nm)
        r_xt = [em.resource("xt%d" % n_) for n_ in range(9)]
        r_hb = [em.resource("hb0"), em.resource("hb1")]
        r_hb2 = r_hb + [em.resource("hb2_%d" % n_) for n_ in range(6)]
        r_hbc = em.resource("hbc")
        r_sga = [em.resource("sga0"), em.resource("sga1")]
        r_sgb = [em.resource("sgb0"), em.resource("sgb1")]
        r_sg = [em.resource("sg0"), em.resource("sg1")]
        r_ost = [em.resource("ost0"), em.resource("ost1")]
        r_osth = [[em.resource("ost%d_%d" % (a_, b_)) for b_ in range(2)] for a_ in range(4)]
        r_bank = [em.resource("bank%d" % b) for b in range(8)]
        r_slot = [em.resource("slot%d" % s) for s in range(NSLOT)]
        r_x1 = [em.resource("x1_%d" % i) for i in range(NTOK)]
        r_hT = [em.resource("hT_a"), em.resource("hT_b")]
        r_mg = [em.resource("mg_%d" % i) for i in range(NTOK)]
        r_act = [em.resource("act0"), em.resource("act1")]
        r_ss = [em.resource("ss%d" % c) for c in range(32)]
        r_rs = [em.resource("rs%d" % c) for c in range(32)]
        r_ss2 = [em.resource("ss2_%d" % c) for c in range(32)]
        r_rstd = [em.resource("rstd%d" % c) for c in range(32)]
        r_out = em.resource("out")
        r_dbg = em.resource("dbg")

        NJOBS = len(JOB_SIZES)
        job_off = [0]
        for n_ in JOB_SIZES:
            job_off.append(job_off[-1] + n_)

        def program():
            state = {"next_job": 0, "bank_rr": 0}

            def issue_job():
                n = state["next_job"]
                if n >= NJOBS:
                    return
                s = n % NSLOT
                sz = JOB_SIZES[n]
                off = job_off[n]
                em.op("pool", lambda e, s=s, sz=sz, off=off: e.dma_start(out=slot(s, sz), in_=wpack[:, off:off + sz]),
                      writes=[r_slot[s]], sem=s_w[s], inc=16)
                state["next_job"] = n + 1

            job_ctr = {"n": 0}

            def take_job():
                n = job_ctr["n"]
                job_ctr["n"] = n + 1
                return n % NSLOT

            def next_bank():
                b = state["bank_rr"]
                state["bank_rr"] = (b + 1) % 8
                return b

            em.op("pool", lambda e: e.iota(iot[:, :], [[1, 128]], base=0, channel_multiplier=-1),
                  writes=[R["iot"]])
            em.op("dve", lambda e: e.tensor_single_scalar(out=identf[:, :], in_=iot[:, :], scalar=0.0,
                                                           op=ALU.is_equal),
                  reads=[R["iot"]], writes=[R["identf"]])
            em.op("dve", lambda e: e.tensor_copy(out=identb[:, :], in_=identf[:, :]),
                  reads=[R["identf"]], writes=[R["identb"]])
            em.op("dve", lambda e: e.memset(zeros_bf[:, :], 0.0), writes=[R["zeros"]])

            def warm(n):
                def f(e):
                    ins = None
                    for _ in range(n):
                        ins = e.matmul(bank(7), identb[:, :], zeros_bf[:, :], start=True, stop=True)
                    return ins
                em.op("pe", f, reads=[R["identb"], R["zeros"]], writes=[r_bank[7]])

            def load_small_params():
                em.op("sp", lambda e: e.dma_start(out=prm_rows[0:80, :], in_=sparams[:, :]),
                      writes=[R["prm_rows"]], sem=s_c, inc=16)
                em.op("sp", lambda e: e.dma_start(out=pos_sb[:, :], in_=pos16[:, :]),
                      writes=[R["pos"]], sem=s_p, inc=16)

            def small_param_compute():
                em.op("pe", lambda e: e.transpose(bank(7)[:, 0:80], prm_rows[0:80, :], identf[0:80, 0:80]),
                      reads=[R["prm_rows"], R["identf"]], writes=[r_bank[7]])
                em.op("act", lambda e: e.copy(out=prm[:, :], in_=bank(7)[:, 0:80]),
                      reads=[r_bank[7]], writes=[R["prm"]])

            def pool_count_compute():
                for g, w in enumerate(POOL_W):
                    em.op("dve", lambda e, g=g, w=w: e.tensor_scalar(out=cnt_sb[:, g * 16:(g + 1) * 16], in0=pos_sb[:, :],
                                                                     scalar1=float(w), scalar2=None, op0=ALU.min),
                          reads=[R["pos"]], writes=[R["cnt"]])
                em.op("dve", lambda e: e.reciprocal(out=inv_sb[:, :], in_=cnt_sb[:, :]),
                      reads=[R["cnt"]], writes=[R["inv"]])

            def norm_stats(col, src_ap, src_res, npart):
                em.op("act", lambda e: e.activation(out=sq[0:npart, :], in_=src_ap, func=AF.Square,
                                                    accum_out=ssb[0:npart, col:col + 1]),
                      reads=src_res, writes=[R["sq"], r_ss[col]])
                em.op("act", lambda e: e.copy(out=junk[0:npart, 0:1], in_=ssb[0:npart, col:col + 1]),
                      reads=[r_ss[col]], writes=[r_ss2[col]])
                em.op("act", lambda e: e.activation(out=rsb[0:npart, col:col + 1], in_=ssb[0:npart, col:col + 1],
                                                    func=AF.Sqrt, scale=1.0 / D, bias=EPS),
                      reads=[r_ss2[col]], writes=[r_rs[col]])

            def norm_rstd(col, npart):
                em.op("dve", lambda e: e.reciprocal(out=rstdb[0:npart, col:col + 1], in_=rsb[0:npart, col:col + 1]),
                      reads=[r_rs[col]], writes=[r_rstd[col]])

            def norm_to_T(col, src_ap, src_res, npart, hbuf, hres, q, dstT, dst_res, c0, evac):
                norm_rstd(col, npart)
                em.op("dve", lambda e: e.scalar_tensor_tensor(out=hbuf[0:npart, :], in0=src_ap,
                                                              scalar=rstdb[0:npart, col:col + 1], in1=gb[0:npart, :],
                                                              op0=ALU.mult, op1=ALU.mult),
                      reads=src_res + [r_rstd[col], R["gb"]], writes=[hres])

                def stage2():
                    pt = pTb(q)

                    def tr(e):
                        ins = None
                        for k in range(KD):
                            ins = e.transpose(pt[:, k, 0:npart], hbuf[0:npart, k * 128:(k + 1) * 128],
                                              identb[0:npart, 0:npart])
                        return ins
                    em.op("pe", tr, reads=[hres, R["identb"]], writes=[r_bank[2 * q], r_bank[2 * q + 1]])
                    if evac == "act":
                        em.op("act", lambda e: e.copy(out=dstT[:, :, c0:c0 + npart], in_=pt[:, :, 0:npart]),
                              reads=[r_bank[2 * q], r_bank[2 * q + 1]], writes=[dst_res])
                    else:
                        em.op("dve", lambda e: e.tensor_copy(out=dstT[:, :, c0:c0 + npart], in_=pt[:, :, 0:npart]),
                              reads=[r_bank[2 * q], r_bank[2 * q + 1]], writes=[dst_res])
                return stage2

            def mm_group(e, out_ap, lhs_fn, rhs_fn, nk):
                ins = None
                for k in range(nk):
                    ins = e.matmul(out_ap, lhs_fn(k), rhs_fn(k), start=(k == 0), stop=(k == nk - 1))
                return ins

            PW = lambda c: prm[:, c:c + 1]

            xsem = s_x + s_r[0:6]
            for ti in range(NTOK + 1):
                npart = H if ti == 0 else 128
                r0 = 0 if ti == 0 else H + (ti - 1) * 128
                em.op("sp", lambda e, ti=ti, npart=npart, r0=r0: e.dma_start(out=xt[ti][0:npart, :], in_=x_c[r0:r0 + npart, :]),
                      writes=[r_xt[ti]], sem=xsem[ti], inc=16)
                if ti == 0:
                    em.op("sp", lambda e: e.dma_start(out=gb, in_=g1_d[0:1, :].partition_broadcast(128)),
                          writes=[R["gb"]], sem=s_g, inc=16)
                if ti == 4:
                    load_small_params()
            em.wait_tokens("pool", [t for r_ in r_xt[0:3] for t in r_.w])
            issue_job()
            em.wait_tokens("pool", [t for r_ in r_xt for t in r_.w])
            for _ in range(NSLOT - 1):
                issue_job()

            def phase1_tiles(tiles, qfn, hbufs, lag, nwarm=0, drain=True):
                pend = []
                for idx, ti in enumerate(tiles):
                    npart = H if ti == 0 else 128
                    r0 = 0 if ti == 0 else H + (ti - 1) * 128
                    hbuf, hres = hbufs[idx % len(hbufs)]
                    norm_stats(ti, xt[ti][0:npart, :], [r_xt[ti]], npart)
                    pend.append(norm_to_T(ti, xt[ti][0:npart, :], [r_xt[ti]], npart, hbuf, hres, qfn(ti),
                                          hT, r_hT[0 if ti <= 4 else 1], r0, "act" if ti % 2 == 1 else "dve"))
                    if len(pend) > lag:
                        pend.pop(0)()
                        if nwarm:
                            warm(nwarm)
                if not drain:
                    return pend
                for st2 in pend:
                    st2()
                    if nwarm:
                        warm(nwarm)
                return []

            def mixA_setup(j):
                s = take_job()
                wv = slot(s).rearrange("p (k w c) -> p k w c", k=16, w=3)
                bH = 3 if j == 0 else 6 + (j % 2)
                return s, wv, bH

            def mixA_pe_halo(j, s, wv, bH):
                def halo(e):
                    mm_group(e, bank(bH)[:, 0:16], lambda k: wv[:, k, 1, :], lambda k: hT[:, k, 0:16], KD)
                    return mm_group(e, bank(bH)[:, 16:32], lambda k: wv[:, k, 2, :], lambda k: hT[:, k, 0:16], KD)
                em.op("pe", halo, reads=[r_slot[s], r_hT[0]], writes=[r_bank[bH]])

            def mixA_ew_halo(j, bH):
                em.op("act", lambda e: e.copy(out=R1[:, 0:16], in_=bank(bH)[:, 0:16]),
                      reads=[r_bank[bH]], writes=[R["R1"]])
                em.op("dve", lambda e: e.tensor_tensor(out=R2[:, 0:16], in0=bank(bH)[:, 16:32], in1=R1[:, 0:16],
                                                       op=ALU.mult),
                      reads=[r_bank[bH], R["R1"]], writes=[R["R2"]])

            def mixA_pe_blk(j, blk, s, wv, which=(1, 2, 0)):
                c0 = H + blk * 512
                bs = 3 * ((2 * j + blk) % 2)
                for (wi, bb) in [(1, bs), (2, bs + 1), (0, bs + 2)]:
                    if wi not in which:
                        continue
                    em.op("pe", lambda e, wi=wi, bb=bb: mm_group(
                        e, bank(bb), lambda k: wv[:, k, wi, :], lambda k: hT[:, k, c0:c0 + 512], KD),
                        reads=[r_slot[s], r_hT[blk]], writes=[r_bank[bb]])

            def mixA_ew_blk(j, blk):
                c0 = H + blk * 512
                o0 = blk * 512
                bs = 3 * ((2 * j + blk) % 2)
                em.op("act", lambda e: e.copy(out=R1[:, c0:c0 + 512], in_=bank(bs)),
                      reads=[r_bank[bs]], writes=[R["R1"]])
                em.op("dve", lambda e: e.tensor_tensor(out=R2[:, c0:c0 + 512], in0=bank(bs + 1),
                                                       in1=R1[:, c0:c0 + 512], op=ALU.mult),
                      reads=[r_bank[bs + 1], R["R1"]], writes=[R["R2"]])
                em.op("act", lambda e: e.activation(out=R3[:, o0:o0 + 512], in_=R2[:, c0:c0 + 512],
                                                    func=AF.Identity, scale=PW(48 + j), bias=PW(56 + j)),
                      reads=[R["R2"], R["prm"]], writes=[R["R3"]])
                for (sh, pc) in [(1, 40 + j), (2, 32 + j)]:
                    em.op("dve", lambda e, sh=sh, pc=pc: e.scalar_tensor_tensor(
                        out=R3[:, o0:o0 + 512], in0=R2[:, c0 - sh:c0 - sh + 512], scalar=PW(pc),
                        in1=R3[:, o0:o0 + 512], op0=ALU.mult, op1=ALU.add),
                        reads=[R["R2"], R["R3"], R["prm"]], writes=[R["R3"]])
                em.op("dve", lambda e: e.tensor_tensor(out=bu[:, j, o0:o0 + 512], in0=bank(bs + 2),
                                                       in1=R3[:, o0:o0 + 512], op=ALU.mult),
                      reads=[r_bank[bs + 2], R["R3"]], writes=[R["bu"]])

            warm(24)
            phase1_tiles(range(0, 5), lambda ti: ti % 3, [(hb[0], r_hb[0]), (hb[1], r_hb[1]), (hbc, r_hbc)], 2, nwarm=4)
            small_param_compute()
            st2s = phase1_tiles(range(5, NTOK + 1), lambda ti: 2 + (ti % 2),
                                [(hb2[n_], r_hb2[n_]) for n_ in range(2, 6)], 4, drain=False)
            s, wv, bH = mixA_setup(0)
            mixA_pe_halo(0, s, wv, bH)
            mixA_pe_blk(0, 0, s, wv, which=(1, 2))
            st2s[0]()
            st2s[1]()
            mixA_pe_blk(0, 0, s, wv, which=(0,))
            st2s[2]()
            st2s[3]()
            em.op("sp", lambda e: e.dma_start(out=gb, in_=g2_d[0:1, :].partition_broadcast(128)),
                  writes=[R["gb"]], sem=s_g, inc=16)
            if debug:
                em.op("sp", lambda e: e.dma_start(out=dbg["dbg_prm"][:, :], in_=prm[:, :]),
                      reads=[R["prm"]], writes=[r_dbg], sem=s_dbg, inc=16)
                em.op("sp", lambda e: e.dma_start(out=dbg["dbg_hT"][:, :], in_=A[:, 0:8320].bitcast(BF16)),
                      reads=r_hT, writes=[r_dbg], sem=s_dbg, inc=16)
            ov = [t for n_ in range(2, 6) for t in r_hb2[n_].r + r_hb2[n_].w] + [t for t in r_hbc.r + r_hbc.w]
            em.wait_tokens("act", ov)
            em.wait_tokens("dve", ov)
            mixA_ew_halo(0, bH)
            mixA_ew_blk(0, 0)
            mixA_pe_blk(0, 1, s, wv)
            issue_job()
            mixA_ew_blk(0, 1)
            for j in range(1, 8):
                s, wv, bH = mixA_setup(j)
                mixA_pe_halo(j, s, wv, bH)
                mixA_ew_halo(j, bH)
                mixA_pe_blk(j, 0, s, wv)
                mixA_ew_blk(j, 0)
                mixA_pe_blk(j, 1, s, wv)
                issue_job()
                mixA_ew_blk(j, 1)

            pool_count_compute()
            for j in range(8):
                if j % 3 == 0:
                    s = take_job()
                n = 3 if j < 6 else 2
                wv = slot(s, 16 * n * 128).rearrange("p (k c) -> p k c", k=16)
                cw = (j % 3) * 128
                g = j // 2
                bH = 6 + (j % 2)
                em.op("pe", lambda e, wv=wv, cw=cw, bH=bH: mm_group(
                    e, bank(bH)[:, 0:16], lambda k: wv[:, k, cw:cw + 128], lambda k: hT[:, k, 0:16], KD),
                    reads=[r_slot[s], r_hT[0]], writes=[r_bank[bH]])
                em.op("act", lambda e, bH=bH: e.copy(out=R1[:, 0:16], in_=bank(bH)[:, 0:16]),
                      reads=[r_bank[bH]], writes=[R["R1"]])
                for blk in range(2):
                    c0 = H + blk * 512
                    bb = 3 * (j % 2) + blk
                    em.op("pe", lambda e, wv=wv, cw=cw, bb=bb, c0=c0: mm_group(
                        e, bank(bb), lambda k: wv[:, k, cw:cw + 128], lambda k: hT[:, k, c0:c0 + 512], KD),
                        reads=[r_slot[s], r_hT[blk]], writes=[r_bank[bb]])
                    if blk == 1 and (j % 3 == 2 or j == 7):
                        issue_job()
                    em.op("act", lambda e, bb=bb, c0=c0: e.copy(out=R1[:, c0:c0 + 512], in_=bank(bb)),
                          reads=[r_bank[bb]], writes=[R["R1"]])
                bufs = [(R2, R["R2"]), (R3, R["R3"])]
                cur, cur_res = R1, R["R1"]
                sh = 1
                for step in range(g + 1):
                    dst, dst_res = bufs[step % 2]
                    lo = 2 * sh - 1
                    em.op("dve", lambda e, dst=dst, cur=cur, lo=lo, sh=sh: e.tensor_tensor(
                        out=dst[:, lo:TT], in0=cur[:, lo:TT], in1=cur[:, lo - sh:TT - sh], op=ALU.add),
                        reads=[cur_res], writes=[dst_res])
                    cur, cur_res = dst, dst_res
                    sh *= 2
                w = POOL_W[g]
                em.op("dve", lambda e, cur=cur, w=w, j=j: e.scalar_tensor_tensor(
                    out=pl[:, j, :], in0=cur[:, H:TT], scalar=1.0 / w, in1=R1[:, H:TT],
                    op0=ALU.mult, op1=ALU.subtract),
                    reads=[cur_res, R["R1"]], writes=[R["pl"]])
                em.op("dve", lambda e, cur=cur, g=g: e.tensor_tensor(out=tmp16[:, :], in0=cur[:, H:H + 16],
                                                                     in1=inv_sb[:, g * 16:(g + 1) * 16], op=ALU.mult),
                      reads=[cur_res, R["inv"]], writes=[R["tmp16"]])
                em.op("dve", lambda e, j=j: e.tensor_tensor(out=pl[:, j, 0:16], in0=tmp16[:, :], in1=R1[:, H:H + 16],
                                                            op=ALU.subtract),
                      reads=[R["tmp16"], R["R1"]], writes=[R["pl"]])

            for j in range(16):
                s = take_job()
                wg_ = slot(s, 4096).rearrange("p (k w c) -> p k w c", k=16, w=2)
                wa = wring[:, s * SLOT + 4096:s * SLOT + 5120].rearrange("p (k c) -> p k c", k=8)
                wp = wring[:, s * SLOT + 5120:s * SLOT + 5376].rearrange("p (k c) -> p k c", k=2)
                g = j // 4
                for blk in range(2):
                    c0 = H + blk * 512
                    o0 = blk * 512
                    par = (2 * j + blk) % 2
                    b0 = 4 * par
                    em.op("pe", lambda e, b0=b0, wg_=wg_, c0=c0: mm_group(
                        e, bank(b0), lambda k: wg_[:, k, 0, :], lambda k: hT[:, k, c0:c0 + 512], KD),
                        reads=[r_slot[s], r_hT[blk]], writes=[r_bank[b0]])
                    em.op("pe", lambda e, b0=b0, wa=wa, o0=o0: mm_group(
                        e, bank(b0 + 1), lambda k: wa[:, k, :], lambda k: bu[:, k, o0:o0 + 512], 8),
                        reads=[r_slot[s], R["bu"]], writes=[r_bank[b0 + 1]])
                    em.op("pe", lambda e, b0=b0, wg_=wg_, c0=c0: mm_group(
                        e, bank(b0 + 2), lambda k: wg_[:, k, 1, :], lambda k: hT[:, k, c0:c0 + 512], KD),
                        reads=[r_slot[s], r_hT[blk]], writes=[r_bank[b0 + 2]])
                    em.op("pe", lambda e, b0=b0, wp=wp, o0=o0, g=g: mm_group(
                        e, bank(b0 + 3), lambda k: wp[:, k, :], lambda k: pl[:, 2 * g + k, o0:o0 + 512], 2),
                        reads=[r_slot[s], R["pl"]], writes=[r_bank[b0 + 3]])
                    if blk == 1:
                        issue_job()
                    em.op("act", lambda e, b0=b0, par=par, j=j: e.activation(out=sga[par], in_=bank(b0), func=AF.Sigmoid,
                                                                             bias=PW(j)),
                          reads=[r_bank[b0], R["prm"]], writes=[r_sga[par]])
                    em.op("act", lambda e, b0=b0, par=par, j=j: e.activation(out=sgb[par], in_=bank(b0 + 2), func=AF.Sigmoid,
                                                                             bias=PW(16 + j)),
                          reads=[r_bank[b0 + 2], R["prm"]], writes=[r_sgb[par]])
                    em.op("dve", lambda e, b0=b0, par=par: e.tensor_tensor(out=m1, in0=bank(b0 + 1), in1=sga[par], op=ALU.mult),
                          reads=[r_bank[b0 + 1], r_sga[par]], writes=[R["m1"]])
                    em.op("dve", lambda e, b0=b0, par=par, j=j: e.scalar_tensor_tensor(
                        out=m2, in0=bank(b0 + 3), scalar=PW(64 + j), in1=sgb[par], op0=ALU.mult, op1=ALU.mult),
                        reads=[r_bank[b0 + 3], r_sgb[par], R["prm"]], writes=[R["m2"]])
                    em.op("dve", lambda e, j=j, o0=o0: e.tensor_tensor(out=mg[:, j, o0:o0 + 512], in0=m1, in1=m2, op=ALU.add),
                          reads=[R["m1"], R["m2"]], writes=r_mg[4 * blk:4 * blk + 4])

            if debug:
                em.op("sp", lambda e: e.dma_start(out=dbg["dbg_bu"][:, :], in_=A[:, 8320:12416].bitcast(BF16)),
                      reads=[R["bu"]], writes=[r_dbg], sem=s_dbg, inc=16)
                em.op("sp", lambda e: e.dma_start(out=dbg["dbg_pl"][:, :], in_=A[:, 12416:16512].bitcast(BF16)),
                      reads=[R["pl"]], writes=[r_dbg], sem=s_dbg, inc=16)
                em.op("sp", lambda e: e.dma_start(out=dbg["dbg_merged"][:, :], in_=arenaB[:, :].bitcast(BF16)),
                      reads=r_mg, writes=[r_dbg], sem=s_dbg, inc=16)

            for i in range(NTOK):
                em.op("sp", lambda e, i=i: e.dma_start(out=x1[:, i, :], in_=x_c[H + i * 128:H + (i + 1) * 128, :]),
                      writes=[r_x1[i], r_hT[0], r_hT[1], R["bu"], R["pl"]] + ([r_dbg] if debug else []), sem=s_r[i], inc=16)
            n2_stage2 = {}

            def n2_step(i):
                if i < 0:
                    return
                n2_stage2[i] = norm_to_T(9 + i, x1[:, i, :], [r_x1[i]], 128, hb2[i], r_hb2[i], 2 + (i % 2),
                                         mg, r_mg[i], i * 128, "act" if i % 2 == 0 else "dve")

            def n2_step2(i):
                if i < 0:
                    return
                n2_stage2[i]()

            for db in range(4):
                sa = take_job()
                sb = take_job()
                wA = slot(sa, 4096).rearrange("p (k c) -> p k c", k=8)
                wB = slot(sb, 4096).rearrange("p (k c) -> p k c", k=8)
                for i in range(NTOK):
                    b = next_bank() if db < 3 else i % 4
                    em.op("pe", lambda e, b=b, i=i, wA=wA, wB=wB: mm_group(
                        e, bank(b), lambda k: mg[:, k, i * 128:(i + 1) * 128],
                        lambda k: (wA if k < 8 else wB)[:, k % 8, :], KD),
                        reads=[r_slot[sa], r_slot[sb], r_mg[i]], writes=[r_bank[b]])
                    if i == NTOK - 1:
                        issue_job()
                        issue_job()
                    em.op("dve", lambda e, b=b, i=i, db=db: e.tensor_tensor(
                        out=x1[:, i, db * 512:(db + 1) * 512], in0=bank(b), in1=x1[:, i, db * 512:(db + 1) * 512], op=ALU.add),
                        reads=[r_bank[b], r_x1[i]], writes=[r_x1[i]])
                    if db == 3:
                        norm_stats(9 + i, x1[:, i, :], [r_x1[i]], 128)
                        n2_step(i - 1)
                        n2_step2(i - 2)
            n2_step(NTOK - 1)
            n2_step2(NTOK - 2)
            n2_step2(NTOK - 1)
            em.op("sp", lambda e: e.dma_start(out=gb, in_=gf_d[0:1, :].partition_broadcast(128)),
                  writes=[R["gb"]], sem=s_g, inc=16)
            if debug:
                em.op("sp", lambda e: e.dma_start(out=dbg["dbg_x1"][:, :], in_=A[:, 0:16384]),
                      reads=r_x1, writes=[r_dbg], sem=s_dbg, inc=16)
                em.op("sp", lambda e: e.dma_start(out=dbg["dbg_h2T"][:, :], in_=arenaB[:, :].bitcast(BF16)),
                      reads=r_mg, writes=[r_dbg], sem=s_dbg, inc=16)

            def final_norm(i):
                if i < 0:
                    return
                par = i % 4
                col = 17 + i
                norm_rstd(col, 128)
                em.op("dve", lambda e: e.scalar_tensor_tensor(
                    out=ost[par], in0=x1[:, i, :], scalar=rstdb[:, col:col + 1], in1=gb, op0=ALU.mult, op1=ALU.mult),
                    reads=[r_x1[i], r_rstd[col], R["gb"]], writes=[r_osth[par][0]] + r_mg)
                q_eng = "sp" if i % 2 == 1 else "pool"
                em.op(q_eng, lambda e: e.dma_start(out=out_d[i * 128:(i + 1) * 128, :], in_=ost[par]),
                      reads=[r_osth[par][0]], writes=[r_out], sem=s_oh[par][0], inc=16)

            state["bank_rr"] = 0
            for gi in range(G):
                for fi in range(FG):
                    s = take_job()
                    wgt = slot(s, 2048).rearrange("p (k c) -> p k c", k=16)
                    wup = wring[:, s * SLOT + 2048:s * SLOT + 4096].rearrange("p (k c) -> p k c", k=16)
                    for blk in range(2):
                        o0 = blk * 512
                        par = blk
                        bg = next_bank()
                        bu_ = next_bank()
                        em.op("pe", lambda e, bg=bg, wgt=wgt, o0=o0: mm_group(
                            e, bank(bg), lambda k: wgt[:, k, :], lambda k: mg[:, k, o0:o0 + 512], KD),
                            reads=[r_slot[s]] + r_mg[4 * blk:4 * blk + 4], writes=[r_bank[bg]])
                        em.op("pe", lambda e, bu_=bu_, wup=wup, o0=o0: mm_group(
                            e, bank(bu_), lambda k: wup[:, k, :], lambda k: mg[:, k, o0:o0 + 512], KD),
                            reads=[r_slot[s]] + r_mg[4 * blk:4 * blk + 4], writes=[r_bank[bu_]])
                        if blk == 1:
                            issue_job()
                        em.op("act", lambda e, bg=bg, par=par: e.activation(out=sg[par], in_=bank(bg), func=AF.Silu),
                              reads=[r_bank[bg]], writes=[r_sg[par]])
                        em.op("dve", lambda e, bu_=bu_, par=par, fi=fi, o0=o0: e.tensor_tensor(
                            out=act[:, fi, o0:o0 + 512], in0=bank(bu_), in1=sg[par], op=ALU.mult),
                            reads=[r_bank[bu_], r_sg[par]], writes=[r_act[blk]])
                def down_group(db, i, s, wd):
                    b = next_bank()
                    em.op("pe", lambda e, b=b, i=i, wd=wd: mm_group(
                        e, bank(b), lambda k: act[:, k, i * 128:(i + 1) * 128], lambda k: wd[:, k, :], FG),
                        reads=[r_slot[s], r_act[i // 4]], writes=[r_bank[b]])
                    return b

                def down_add(db, i, b):
                    em.op("dve", lambda e, b=b, i=i, db=db: e.tensor_tensor(
                        out=x1[:, i, db * 512:(db + 1) * 512], in0=bank(b), in1=x1[:, i, db * 512:(db + 1) * 512],
                        op=ALU.add),
                        reads=[r_bank[b], r_x1[i]], writes=[r_x1[i]])

                last = (gi == G - 1)
                for db in range(2 if last else 4):
                    s = take_job()
                    wd = slot(s, FG * 512).rearrange("p (k c) -> p k c", k=FG)
                    for i in range(NTOK):
                        b = down_group(db, i, s, wd)
                        if i == NTOK - 1:
                            issue_job()
                        down_add(db, i, b)
                if last:
                    s2 = take_job()
                    s3 = take_job()
                    wd2 = slot(s2, FG * 512).rearrange("p (k c) -> p k c", k=FG)
                    wd3 = slot(s3, FG * 512).rearrange("p (k c) -> p k c", k=FG)
                    for i in range(NTOK):
                        b2 = down_group(2, i, s2, wd2)
                        b3 = down_group(3, i, s3, wd3)
                        down_add(2, i, b2)
                        down_add(3, i, b3)
                        norm_stats(17 + i, x1[:, i, :], [r_x1[i]], 128)
                        final_norm(i - 1)

            final_norm(NTOK - 1)
            em.wait_tokens("sp", r_out.w + r_dbg.w)

        @block.sync
        def _(e):
            em.begin_pass("sp", e)
            program()

        @block.scalar
        def _(e):
            em.begin_pass("act", e)
            program()

        @block.vector
        def _(e):
            em.begin_pass("dve", e)
            program()

        @block.gpsimd
        def _(e):
            em.begin_pass("pool", e)
            program()

        @block.tensor
        def _(e):
            em.begin_pass("pe", e)
            program()

    return nc


def make_in_maps(inputs):
    f = lambda a: np.ascontiguousarray(np.asarray(a, dtype=np.float32))
    x = f(inputs["x"])
    B, S, _ = x.shape
    shared = {
        "norm1_g": f(inputs["norm1_g"]).reshape(1, D),
        "wpack": pack_weights(inputs),
        "sparams": np.ascontiguousarray(np.concatenate([
            f(inputs["b_gate"]).reshape(32, 128), f(inputs["conv_w"]).reshape(24, 128),
            f(inputs["conv_b"]).reshape(8, 128), f(inputs["pool_scale"]).reshape(16, 128)], axis=0)),
        "norm2_g": f(inputs["norm2_g"]).reshape(1, D),
        "final_g": f(inputs["final_g"]).reshape(1, D),
    }
    in_maps = []
    per_seq = S // T
    for c in range(NCORES):
        b, hf = divmod(c, per_seq)
        s0 = hf * T
        xc = np.zeros((TT, D), np.float32)
        if s0 > 0:
            xc[0:H] = x[b, s0 - H:s0]
        xc[H:] = x[b, s0:s0 + T]
        pos = np.broadcast_to((s0 + 1 + np.arange(16, dtype=np.float32))[None, :], (128, 16))
        m = dict(shared)
        m["x_c"] = xc
        m["pos16"] = np.ascontiguousarray(pos, dtype=np.float32)
        in_maps.append(m)
    return in_maps, (B, S)


def kernel(**inputs):
    in_maps, (B, S) = make_in_maps(inputs)
    nc = build_nc()
    res = run_bass_kernel_spmd(nc, in_maps, core_ids=list(range(NCORES)))
    out = np.concatenate([np.asarray(r["out"]) for r in res.results], axis=0)
    return out.reshape(B, S, D).astype(np.float32)
```

```python
from contextlib import ExitStack

import numpy as np
import concourse.bass as bass
import concourse.mybir as mybir
from concourse.bass_utils import run_bass_kernel_spmd

F32 = mybir.dt.float32
BF16 = mybir.dt.bfloat16
I32 = mybir.dt.int32
AF = mybir.ActivationFunctionType
ALU = mybir.AluOpType

NCORES = 8
D = 2048
KD = 16
T = 1024
H = 16
TT = T + H
NTOK = T // 128
DI = 8192
NF = 44
G = 4
FG = NF // G
SLOT = 6144
NSLOT = 4
EPS = 1e-6
POOL_W = (2, 4, 8, 16)


JOB_SIZES = ([6144] * 8 + [6144, 6144, 4096] + [5376] * 16 + [4096] * 8
             + ([4096] * FG + [FG * 512] * 4) * G)
WTOTAL = sum(JOB_SIZES)


def pack_weights(inputs):
    f = lambda a: np.asarray(a, dtype=np.float32)
    w_in = f(inputs["w_in"]).reshape(D, DI)
    w_a_out = f(inputs["w_a_out"]).reshape(1024, D)
    w_pool = f(inputs["w_pool"]).reshape(1024, 512)
    w_o = f(inputs["w_o"]).reshape(D, D)
    w_gate = f(inputs["w_ffn_gate"]).reshape(D, NF * 128)
    w_up = f(inputs["w_ffn_up"]).reshape(D, NF * 128)
    w_down = f(inputs["w_ffn_down"]).reshape(NF * 128, D)

    def kv(a):
        return a.reshape(a.shape[0] // 128, 128, a.shape[1]).transpose(1, 0, 2)

    blocks = []
    for j in range(8):
        blocks.append(np.stack([kv(w_in[:, wi * 1024 + j * 128:wi * 1024 + (j + 1) * 128]) for wi in range(3)],
                               axis=2).reshape(128, -1))
    for q in range(3):
        n = 3 if q < 2 else 2
        blocks.append(kv(w_in[:, 3072 + q * 384:3072 + q * 384 + n * 128]).reshape(128, -1))
    for j in range(16):
        g = j // 4
        blocks.append(np.concatenate([
            np.stack([kv(w_in[:, 4096 + j * 128:4096 + (j + 1) * 128]),
                      kv(w_in[:, 6144 + j * 128:6144 + (j + 1) * 128])], axis=2).reshape(128, -1),
            kv(w_a_out[:, j * 128:(j + 1) * 128]).reshape(128, -1),
            kv(w_pool[g * 256:(g + 1) * 256, (j % 4) * 128:(j % 4 + 1) * 128]).reshape(128, -1)], axis=1))
    for db in range(4):
        for hh in range(2):
            blocks.append(kv(w_o[hh * 1024:(hh + 1) * 1024, db * 512:(db + 1) * 512]).reshape(128, -1))
    for gi in range(G):
        for fi in range(FG):
            fidx = gi * FG + fi
            blocks.append(np.concatenate([kv(w_gate[:, fidx * 128:(fidx + 1) * 128]).reshape(128, -1),
                                          kv(w_up[:, fidx * 128:(fidx + 1) * 128]).reshape(128, -1)], axis=1))
        for db in range(4):
            blocks.append(kv(w_down[gi * FG * 128:(gi + 1) * FG * 128, db * 512:(db + 1) * 512]).reshape(128, -1))
    assert [b.shape[1] for b in blocks] == JOB_SIZES
    return np.ascontiguousarray(np.concatenate(blocks, axis=1))


class Sem:
    def __init__(self, h):
        self.h = h
        self.count = 0


class Res:
    def __init__(self, name):
        self.name = name
        self.reset()

    def reset(self):
        self.w = []
        self.r = []
        self.pr = []


class Emitter:
    def __init__(self, nc):
        self.nc = nc
        self.sems = []
        self.res = []
        self.cur = None
        self.eng = None
        self.waited = {}
        self.engsem = {}

    def sem(self, h):
        s = Sem(h)
        self.sems.append(s)
        return s

    def resource(self, name):
        r = Res(name)
        self.res.append(r)
        return r

    def begin_pass(self, name, eng):
        self.cur = name
        self.eng = eng
        self.waited = {}
        for s in self.sems:
            s.count = 0
        for r in self.res:
            r.reset()

    def wait_tokens(self, engname, toks):
        if self.cur != engname:
            return
        need = {}
        for (s, v) in toks:
            if need.get(s, 0) < v:
                need[s] = v
        for s, v in need.items():
            if self.waited.get(s, 0) >= v:
                continue
            self.eng.wait_ge(s.h, v)
            self.waited[s] = v

    def op(self, engname, fn, reads=(), writes=(), sem=None, inc=1):
        toks = []
        for r in reads:
            toks += r.w
        for w in writes:
            toks += w.w + w.r + w.pr
        self.wait_tokens(engname, toks)
        ins = fn(self.eng) if self.cur == engname else None
        s = sem if sem is not None else self.engsem[engname]
        s.count += inc
        if ins is not None:
            ins.then_inc(s.h, inc)
        tok = (s, s.count)
        for r in reads:
            r.r.append(tok)
        for w in writes:
            if w.r:
                w.pr = w.r
                w.r = []
                w.w = [tok]
            else:
                w.w.append(tok)
        return tok


def build_nc(debug=False):
    nc = bass.Bass("TRN2", target_bir_lowering=False)

    def din(name, shape, dt=F32):
        return nc.dram_tensor(name, list(shape), dt, kind="ExternalInput").ap()

    x_c = din("x_c", [TT, D])
    pos16 = din("pos16", [128, 16])
    g1_d = din("norm1_g", [1, D])
    wpack = din("wpack", [128, WTOTAL])
    sparams = din("sparams", [80, 128])
    g2_d = din("norm2_g", [1, D])
    gf_d = din("final_g", [1, D])
    out_d = nc.dram_tensor("out", [T, D], F32, kind="ExternalOutput").ap()
    dbg = {}
    if debug:
        for nm, shp, dt in [("dbg_hT", [128, KD * TT], BF16), ("dbg_bu", [128, 8 * T], BF16),
                            ("dbg_pl", [128, 8 * T], BF16), ("dbg_merged", [128, KD * T], BF16),
                            ("dbg_x1", [128, NTOK * D], F32), ("dbg_h2T", [128, KD * T], BF16),
                            ("dbg_prm", [128, 80], F32)]:
            dbg[nm] = nc.dram_tensor(nm, shp, dt, kind="ExternalOutput").ap()

    with ExitStack() as es:
        def sb(name, shape, dt):
            return es.enter_context(nc.sbuf_tensor(name, shape, dt))

        def hsem(name):
            return es.enter_context(nc.semaphore(name))

        arenaA = sb("arenaA", [128, 16512], F32)
        arenaB = sb("arenaB", [128, 8192], F32)
        arenaC = sb("arenaC", [128, 12288], F32)
        wring = sb("wring", [128, NSLOT * SLOT], BF16)
        identf = sb("identf", [128, 128], F32)
        identb = sb("identb", [128, 128], BF16)
        iot = sb("iot", [128, 128], I32)
        prm_rows = sb("prm_rows", [128, 128], F32)
        prm = sb("prm", [128, 80], F32)
        ssb = sb("ssb", [128, 32], F32)
        rsb = sb("rsb", [128, 32], F32)
        rstdb = sb("rstdb", [128, 32], F32)
        pos_sb = sb("pos_sb", [128, 16], F32)
        cnt_sb = sb("cnt_sb", [128, 64], F32)
        inv_sb = sb("inv_sb", [128, 64], F32)
        tmp16 = sb("tmp16", [128, 16], F32)
        junk = sb("junk", [128, 4], F32)
        zeros_bf = sb("zeros_bf", [128, 512], BF16)
        ps = es.enter_context(nc.psum_tensor("ps", [128, 4096], F32))
        h_act = hsem("s_act"); h_dve = hsem("s_dve"); h_pe = hsem("s_pe"); h_pool = hsem("s_pool")
        h_c = hsem("s_c"); h_x0 = hsem("s_x0"); h_x1 = hsem("s_x1"); h_x2 = hsem("s_x2")
        h_w0 = hsem("s_w0"); h_w1 = hsem("s_w1"); h_w2 = hsem("s_w2"); h_w3 = hsem("s_w3")
        h_o0 = hsem("s_o0"); h_o1 = hsem("s_o1"); h_g = hsem("s_g"); h_o2 = hsem("s_o2"); h_o3 = hsem("s_o3")
        h_r0 = hsem("s_r0"); h_r1 = hsem("s_r1"); h_r2 = hsem("s_r2"); h_r3 = hsem("s_r3")
        h_r4 = hsem("s_r4"); h_r5 = hsem("s_r5"); h_r6 = hsem("s_r6"); h_r7 = hsem("s_r7")
        h_dbg = hsem("s_dbg")
        h_p = hsem("s_p")
        block = es.enter_context(nc.Block())
        em = Emitter(nc)
        em.engsem = {"act": em.sem(h_act), "dve": em.sem(h_dve), "pe": em.sem(h_pe),
                     "pool": em.sem(h_pool)}
        s_c = em.sem(h_c)
        s_x = [em.sem(h_x0), em.sem(h_x1), em.sem(h_x2)]
        s_w = [em.sem(h_w0), em.sem(h_w1), em.sem(h_w2), em.sem(h_w3)]
        s_g = em.sem(h_g)
        s_r = [em.sem(h) for h in (h_r0, h_r1, h_r2, h_r3, h_r4, h_r5, h_r6, h_r7)]
        s_oh = [[em.sem(h_)] for h_ in (h_o0, h_o1, h_o2, h_o3)]
        s_dbg = em.sem(h_dbg)
        s_p = em.sem(h_p)

        A = arenaA
        hT = A[:, 0:8320].bitcast(BF16).rearrange("p (k t) -> p k t", t=TT)
        bu = A[:, 8320:12416].bitcast(BF16).rearrange("p (k t) -> p k t", t=T)
        pl = A[:, 12416:16512].bitcast(BF16).rearrange("p (k t) -> p k t", t=T)
        x1 = A[:, 0:16384].rearrange("p (i d) -> p i d", d=D)
        mg = arenaB[:, :].bitcast(BF16).rearrange("p (k t) -> p k t", t=T)
        ost = [arenaB[:, 2048 * q_:2048 * (q_ + 1)] for q_ in range(4)]
        C = arenaC
        gb = C[:, 0:2048]
        hb = [C[:, 2048:3072].bitcast(BF16), C[:, 3072:4096].bitcast(BF16)]
        sq = C[:, 4096:5120].bitcast(BF16)
        FB = 5120
        hb2 = hb + [C[:, FB + 1024 * n_:FB + 1024 * (n_ + 1)].bitcast(BF16) for n_ in range(6)]
        hbc = C[:, FB + 6144:FB + 7168].bitcast(BF16)
        xt = ([C[:, FB:FB + 2048], C[:, FB + 2048:FB + 4096], C[:, FB + 4096:FB + 6144]]
              + [arenaB[:, 2048 * n_:2048 * (n_ + 1)] for n_ in range(4)]
              + [A[:, 8320 + 2048 * n_:8320 + 2048 * (n_ + 1)] for n_ in range(2)])
        R1 = C[:, FB:FB + 1040]
        R2 = C[:, FB + 1040:FB + 2080]
        R3 = C[:, FB + 2080:FB + 3120]
        sga = [C[:, FB + 3120:FB + 3632], C[:, FB + 3632:FB + 4144]]
        sgb = [C[:, FB + 4144:FB + 4656], C[:, FB + 4656:FB + 5168]]
        m1 = C[:, FB + 5168:FB + 5680]
        m2 = C[:, FB + 5680:FB + 6192]
        act = C[:, FB:FB + 5632].bitcast(BF16).rearrange("p (k t) -> p k t", t=T)
        sg = [C[:, FB + 5632:FB + 6144], C[:, FB + 6144:FB + 6656]]

        def slot(s, n=SLOT):
            return wring[:, s * SLOT:s * SLOT + n]

        def bank(b):
            return ps[:, b * 512:(b + 1) * 512]

        def pTb(q):
            return ps[:, q * 1024:(q + 1) * 1024].bitcast(BF16).rearrange("p (k t) -> p k t", t=128)

        R = {}
        for nm in ["zeros", "iot", "identf", "identb", "prm_rows", "prm", "pos", "cnt", "inv", "tmp16",
                   "gb", "sq", "hT", "bu", "pl", "mg", "R1", "R2", "R3", "m1", "m2"]:
            R[nm] = em.resource(nm)
        r_xt = [em.resource("xt%d" % n_) for n_ in range(9)]
        r_hb = [em.resource("hb0"), em.resource("hb1")]
        r_hb2 = r_hb + [em.resource("hb2_%d" % n_) for n_ in range(6)]
        r_hbc = em.resource("hbc")
        r_sga = [em.resource("sga0"), em.resource("sga1")]
        r_sgb = [em.resource("sgb0"), em.resource("sgb1")]
        r_sg = [em.resource("sg0"), em.resource("sg1")]
        r_ost = [em.resource("ost0"), em.resource("ost1")]
        r_osth = [[em.resource("ost%d_%d" % (a_, b_)) for b_ in range(2)] for a_ in range(4)]
        r_bank = [em.resource("bank%d" % b) for b in range(8)]
        r_slot = [em.resource("slot%d" % s) for s in range(NSLOT)]
        r_x1 = [em.resource("x1_%d" % i) for i in range(NTOK)]
        r_hT = [em.resource("hT_a"), em.resource("hT_b")]
        r_mg = [em.resource("mg_%d" % i) for i in range(NTOK)]
        r_act = [em.resource("act0"), em.resource("act1")]
        r_ss = [em.resource("ss%d" % c) for c in range(32)]
        r_rs = [em.resource("rs%d" % c) for c in range(32)]
        r_ss2 = [em.resource("ss2_%d" % c) for c in range(32)]
        r_rstd = [em.resource("rstd%d" % c) for c in range(32)]
        r_outs = [em.resource("out%d" % i) for i in range(NTOK)]
        r_dbg = em.resource("dbg")

        NJOBS = len(JOB_SIZES)
        job_off = [0]
        for n_ in JOB_SIZES:
            job_off.append(job_off[-1] + n_)

        def program():
            state = {"next_job": 0, "bank_rr": 0}

            def issue_job():
                n = state["next_job"]
                if n >= NJOBS:
                    return
                s = n % NSLOT
                sz = JOB_SIZES[n]
                off = job_off[n]
                em.op("pool", lambda e, s=s, sz=sz, off=off: e.dma_start(out=slot(s, sz), in_=wpack[:, off:off + sz]),
                      writes=[r_slot[s]], sem=s_w[s], inc=16)
                state["next_job"] = n + 1

            job_ctr = {"n": 0}

            def take_job():
                n = job_ctr["n"]
                job_ctr["n"] = n + 1
                return n % NSLOT

            def next_bank():
                b = state["bank_rr"]
                state["bank_rr"] = (b + 1) % 8
                return b

            em.op("pool", lambda e: e.iota(iot[:, :], [[1, 128]], base=0, channel_multiplier=-1),
                  writes=[R["iot"]])
            em.op("dve", lambda e: e.tensor_single_scalar(out=identf[:, :], in_=iot[:, :], scalar=0.0,
                                                           op=ALU.is_equal),
                  reads=[R["iot"]], writes=[R["identf"]])
            em.op("dve", lambda e: e.tensor_copy(out=identb[:, :], in_=identf[:, :]),
                  reads=[R["identf"]], writes=[R["identb"]])
            em.op("dve", lambda e: e.memset(zeros_bf[:, :], 0.0), writes=[R["zeros"]])

            def warm(n):
                def f(e):
                    ins = None
                    for _ in range(n):
                        ins = e.matmul(bank(7), identb[:, :], zeros_bf[:, :], start=True, stop=True)
                    return ins
                em.op("pe", f, reads=[R["identb"], R["zeros"]], writes=[r_bank[7]])

            def load_small_params():
                em.op("sp", lambda e: e.dma_start(out=prm_rows[0:80, :], in_=sparams[:, :]),
                      writes=[R["prm_rows"]], sem=s_c, inc=16)
                em.op("sp", lambda e: e.dma_start(out=pos_sb[:, :], in_=pos16[:, :]),
                      writes=[R["pos"]], sem=s_p, inc=16)

            def small_param_compute():
                em.op("pe", lambda e: e.transpose(bank(7)[:, 0:80], prm_rows[0:80, :], identf[0:80, 0:80]),
                      reads=[R["prm_rows"], R["identf"]], writes=[r_bank[7]])
                em.op("act", lambda e: e.copy(out=prm[:, :], in_=bank(7)[:, 0:80]),
                      reads=[r_bank[7]], writes=[R["prm"]])

            def pool_count_compute():
                for g, w in enumerate(POOL_W):
                    em.op("dve", lambda e, g=g, w=w: e.tensor_scalar(out=cnt_sb[:, g * 16:(g + 1) * 16], in0=pos_sb[:, :],
                                                                     scalar1=float(w), scalar2=None, op0=ALU.min),
                          reads=[R["pos"]], writes=[R["cnt"]])
                em.op("dve", lambda e: e.reciprocal(out=inv_sb[:, :], in_=cnt_sb[:, :]),
                      reads=[R["cnt"]], writes=[R["inv"]])

            def norm_stats(col, src_ap, src_res, npart):
                em.op("act", lambda e: e.activation(out=sq[0:npart, :], in_=src_ap, func=AF.Square,
                                                    accum_out=ssb[0:npart, col:col + 1]),
                      reads=src_res, writes=[R["sq"], r_ss[col]])
                em.op("act", lambda e: e.copy(out=junk[0:npart, 0:1], in_=ssb[0:npart, col:col + 1]),
                      reads=[r_ss[col]], writes=[r_ss2[col]])
                em.op("act", lambda e: e.activation(out=rsb[0:npart, col:col + 1], in_=ssb[0:npart, col:col + 1],
                                                    func=AF.Sqrt, scale=1.0 / D, bias=EPS),
                      reads=[r_ss2[col]], writes=[r_rs[col]])

            def norm_rstd(col, npart):
                em.op("dve", lambda e: e.reciprocal(out=rstdb[0:npart, col:col + 1], in_=rsb[0:npart, col:col + 1]),
                      reads=[r_rs[col]], writes=[r_rstd[col]])

            def norm_to_T(col, src_ap, src_res, npart, hbuf, hres, q, dstT, dst_res, c0, evac):
                norm_rstd(col, npart)
                em.op("dve", lambda e: e.scalar_tensor_tensor(out=hbuf[0:npart, :], in0=src_ap,
                                                              scalar=rstdb[0:npart, col:col + 1], in1=gb[0:npart, :],
                                                              op0=ALU.mult, op1=ALU.mult),
                      reads=src_res + [r_rstd[col], R["gb"]], writes=[hres])

                def stage2():
                    pt = pTb(q)

                    def tr(e):
                        ins = None
                        for k in range(KD):
                            ins = e.transpose(pt[:, k, 0:npart], hbuf[0:npart, k * 128:(k + 1) * 128],
                                              identb[0:npart, 0:npart])
                        return ins
                    em.op("pe", tr, reads=[hres, R["identb"]], writes=[r_bank[2 * q], r_bank[2 * q + 1]])
                    if evac == "act":
                        em.op("act", lambda e: e.copy(out=dstT[:, :, c0:c0 + npart], in_=pt[:, :, 0:npart]),
                              reads=[r_bank[2 * q], r_bank[2 * q + 1]], writes=[dst_res])
                    else:
                        em.op("dve", lambda e: e.tensor_copy(out=dstT[:, :, c0:c0 + npart], in_=pt[:, :, 0:npart]),
                              reads=[r_bank[2 * q], r_bank[2 * q + 1]], writes=[dst_res])
                return stage2

            def mm_group(e, out_ap, lhs_fn, rhs_fn, nk):
                ins = None
                for k in range(nk):
                    ins = e.matmul(out_ap, lhs_fn(k), rhs_fn(k), start=(k == 0), stop=(k == nk - 1))
                return ins

            PW = lambda c: prm[:, c:c + 1]

            xsem = s_x + s_r[0:6]
            for ti in range(NTOK + 1):
                npart = H if ti == 0 else 128
                r0 = 0 if ti == 0 else H + (ti - 1) * 128
                em.op("sp", lambda e, ti=ti, npart=npart, r0=r0: e.dma_start(out=xt[ti][0:npart, :], in_=x_c[r0:r0 + npart, :]),
                      writes=[r_xt[ti]], sem=xsem[ti], inc=16)
                if ti == 0:
                    em.op("sp", lambda e: e.dma_start(out=gb, in_=g1_d[0:1, :].partition_broadcast(128)),
                          writes=[R["gb"]], sem=s_g, inc=16)
                if ti == 4:
                    load_small_params()
            em.wait_tokens("pool", [t for r_ in r_xt[0:3] for t in r_.w])
            issue_job()
            em.wait_tokens("pool", [t for r_ in r_xt for t in r_.w])
            for _ in range(NSLOT - 1):
                issue_job()

            def phase1_tiles(tiles, qfn, hbufs, lag, nwarm=0, drain=True):
                pend = []
                for idx, ti in enumerate(tiles):
                    npart = H if ti == 0 else 128
                    r0 = 0 if ti == 0 else H + (ti - 1) * 128
                    hbuf, hres = hbufs[idx % len(hbufs)]
                    norm_stats(ti, xt[ti][0:npart, :], [r_xt[ti]], npart)
                    pend.append(norm_to_T(ti, xt[ti][0:npart, :], [r_xt[ti]], npart, hbuf, hres, qfn(ti),
                                          hT, r_hT[0 if ti <= 4 else 1], r0, "act" if ti % 2 == 1 else "dve"))
                    if len(pend) > lag:
                        pend.pop(0)()
                        if nwarm:
                            warm(nwarm)
                if not drain:
                    return pend
                for st2 in pend:
                    st2()
                    if nwarm:
                        warm(nwarm)
                return []

            def mixA_setup(j):
                s = take_job()
                wv = slot(s).rearrange("p (k w c) -> p k w c", k=16, w=3)
                bH = 3 if j == 0 else 6 + (j % 2)
                return s, wv, bH

            def mixA_pe_halo(j, s, wv, bH):
                def halo(e):
                    mm_group(e, bank(bH)[:, 0:16], lambda k: wv[:, k, 1, :], lambda k: hT[:, k, 0:16], KD)
                    return mm_group(e, bank(bH)[:, 16:32], lambda k: wv[:, k, 2, :], lambda k: hT[:, k, 0:16], KD)
                em.op("pe", halo, reads=[r_slot[s], r_hT[0]], writes=[r_bank[bH]])

            def mixA_ew_halo(j, bH):
                em.op("act", lambda e: e.copy(out=R1[:, 0:16], in_=bank(bH)[:, 0:16]),
                      reads=[r_bank[bH]], writes=[R["R1"]])
                em.op("dve", lambda e: e.tensor_tensor(out=R2[:, 0:16], in0=bank(bH)[:, 16:32], in1=R1[:, 0:16],
                                                       op=ALU.mult),
                      reads=[r_bank[bH], R["R1"]], writes=[R["R2"]])

            def mixA_pe_blk(j, blk, s, wv, which=(1, 2, 0)):
                c0 = H + blk * 512
                bs = 3 * ((2 * j + blk) % 2)
                for (wi, bb) in [(1, bs), (2, bs + 1), (0, bs + 2)]:
                    if wi not in which:
                        continue
                    em.op("pe", lambda e, wi=wi, bb=bb: mm_group(
                        e, bank(bb), lambda k: wv[:, k, wi, :], lambda k: hT[:, k, c0:c0 + 512], KD),
                        reads=[r_slot[s], r_hT[blk]], writes=[r_bank[bb]])

            def mixA_ew_blk(j, blk):
                c0 = H + blk * 512
                o0 = blk * 512
                bs = 3 * ((2 * j + blk) % 2)
                em.op("act", lambda e: e.copy(out=R1[:, c0:c0 + 512], in_=bank(bs)),
                      reads=[r_bank[bs]], writes=[R["R1"]])
                em.op("dve", lambda e: e.tensor_tensor(out=R2[:, c0:c0 + 512], in0=bank(bs + 1),
                                                       in1=R1[:, c0:c0 + 512], op=ALU.mult),
                      reads=[r_bank[bs + 1], R["R1"]], writes=[R["R2"]])
                em.op("act", lambda e: e.activation(out=R3[:, o0:o0 + 512], in_=R2[:, c0:c0 + 512],
                                                    func=AF.Identity, scale=PW(48 + j), bias=PW(56 + j)),
                      reads=[R["R2"], R["prm"]], writes=[R["R3"]])
                for (sh, pc) in [(1, 40 + j), (2, 32 + j)]:
                    em.op("dve", lambda e, sh=sh, pc=pc: e.scalar_tensor_tensor(
                        out=R3[:, o0:o0 + 512], in0=R2[:, c0 - sh:c0 - sh + 512], scalar=PW(pc),
                        in1=R3[:, o0:o0 + 512], op0=ALU.mult, op1=ALU.add),
                        reads=[R["R2"], R["R3"], R["prm"]], writes=[R["R3"]])
                em.op("dve", lambda e: e.tensor_tensor(out=bu[:, j, o0:o0 + 512], in0=bank(bs + 2),
                                                       in1=R3[:, o0:o0 + 512], op=ALU.mult),
                      reads=[r_bank[bs + 2], R["R3"]], writes=[R["bu"]])

            warm(24)
            phase1_tiles(range(0, 5), lambda ti: ti % 3, [(hb[0], r_hb[0]), (hb[1], r_hb[1]), (hbc, r_hbc)], 2, nwarm=4)
            small_param_compute()
            st2s = phase1_tiles(range(5, NTOK + 1), lambda ti: 2 + (ti % 2),
                                [(hb2[n_], r_hb2[n_]) for n_ in range(2, 6)], 4, drain=False)
            s, wv, bH = mixA_setup(0)
            mixA_pe_halo(0, s, wv, bH)
            mixA_pe_blk(0, 0, s, wv, which=(1, 2))
            st2s[0]()
            st2s[1]()
            mixA_pe_blk(0, 0, s, wv, which=(0,))
            st2s[2]()
            st2s[3]()
            em.op("sp", lambda e: e.dma_start(out=gb, in_=g2_d[0:1, :].partition_broadcast(128)),
                  writes=[R["gb"]], sem=s_g, inc=16)
            if debug:
                em.op("sp", lambda e: e.dma_start(out=dbg["dbg_prm"][:, :], in_=prm[:, :]),
                      reads=[R["prm"]], writes=[r_dbg], sem=s_dbg, inc=16)
                em.op("sp", lambda e: e.dma_start(out=dbg["dbg_hT"][:, :], in_=A[:, 0:8320].bitcast(BF16)),
                      reads=r_hT, writes=[r_dbg], sem=s_dbg, inc=16)
            ov = [t for n_ in range(2, 6) for t in r_hb2[n_].r + r_hb2[n_].w] + [t for t in r_hbc.r + r_hbc.w]
            em.wait_tokens("act", ov)
            em.wait_tokens("dve", ov)
            mixA_ew_halo(0, bH)
            mixA_ew_blk(0, 0)
            mixA_pe_blk(0, 1, s, wv)
            issue_job()
            mixA_ew_blk(0, 1)
            for j in range(1, 8):
                s, wv, bH = mixA_setup(j)
                mixA_pe_halo(j, s, wv, bH)
                mixA_ew_halo(j, bH)
                mixA_pe_blk(j, 0, s, wv)
                mixA_ew_blk(j, 0)
                mixA_pe_blk(j, 1, s, wv)
                issue_job()
                mixA_ew_blk(j, 1)

            pool_count_compute()
            for j in range(8):
                if j % 3 == 0:
                    s = take_job()
                n = 3 if j < 6 else 2
                wv = slot(s, 16 * n * 128).rearrange("p (k c) -> p k c", k=16)
                cw = (j % 3) * 128
                g = j // 2
                bH = 6 + (j % 2)
                em.op("pe", lambda e, wv=wv, cw=cw, bH=bH: mm_group(
                    e, bank(bH)[:, 0:16], lambda k: wv[:, k, cw:cw + 128], lambda k: hT[:, k, 0:16], KD),
                    reads=[r_slot[s], r_hT[0]], writes=[r_bank[bH]])
                em.op("act", lambda e, bH=bH: e.copy(out=R1[:, 0:16], in_=bank(bH)[:, 0:16]),
                      reads=[r_bank[bH]], writes=[R["R1"]])
                for blk in range(2):
                    c0 = H + blk * 512
                    bb = 3 * (j % 2) + blk
                    em.op("pe", lambda e, wv=wv, cw=cw, bb=bb, c0=c0: mm_group(
                        e, bank(bb), lambda k: wv[:, k, cw:cw + 128], lambda k: hT[:, k, c0:c0 + 512], KD),
                        reads=[r_slot[s], r_hT[blk]], writes=[r_bank[bb]])
                    if blk == 1 and (j % 3 == 2 or j == 7):
                        issue_job()
                    em.op("act", lambda e, bb=bb, c0=c0: e.copy(out=R1[:, c0:c0 + 512], in_=bank(bb)),
                          reads=[r_bank[bb]], writes=[R["R1"]])
                bufs = [(R2, R["R2"]), (R3, R["R3"])]
                cur, cur_res = R1, R["R1"]
                sh = 1
                for step in range(g + 1):
                    dst, dst_res = bufs[step % 2]
                    lo = 2 * sh - 1
                    em.op("dve", lambda e, dst=dst, cur=cur, lo=lo, sh=sh: e.tensor_tensor(
                        out=dst[:, lo:TT], in0=cur[:, lo:TT], in1=cur[:, lo - sh:TT - sh], op=ALU.add),
                        reads=[cur_res], writes=[dst_res])
                    cur, cur_res = dst, dst_res
                    sh *= 2
                w = POOL_W[g]
                em.op("dve", lambda e, cur=cur, w=w, j=j: e.scalar_tensor_tensor(
                    out=pl[:, j, :], in0=cur[:, H:TT], scalar=1.0 / w, in1=R1[:, H:TT],
                    op0=ALU.mult, op1=ALU.subtract),
                    reads=[cur_res, R["R1"]], writes=[R["pl"]])
                em.op("dve", lambda e, cur=cur, g=g: e.tensor_tensor(out=tmp16[:, :], in0=cur[:, H:H + 16],
                                                                     in1=inv_sb[:, g * 16:(g + 1) * 16], op=ALU.mult),
                      reads=[cur_res, R["inv"]], writes=[R["tmp16"]])
                em.op("dve", lambda e, j=j: e.tensor_tensor(out=pl[:, j, 0:16], in0=tmp16[:, :], in1=R1[:, H:H + 16],
                                                            op=ALU.subtract),
                      reads=[R["tmp16"], R["R1"]], writes=[R["pl"]])

            for j in range(16):
                s = take_job()
                wg_ = slot(s, 4096).rearrange("p (k w c) -> p k w c", k=16, w=2)
                wa = wring[:, s * SLOT + 4096:s * SLOT + 5120].rearrange("p (k c) -> p k c", k=8)
                wp = wring[:, s * SLOT + 5120:s * SLOT + 5376].rearrange("p (k c) -> p k c", k=2)
                g = j // 4
                for blk in range(2):
                    c0 = H + blk * 512
                    o0 = blk * 512
                    par = (2 * j + blk) % 2
                    b0 = 4 * par
                    em.op("pe", lambda e, b0=b0, wg_=wg_, c0=c0: mm_group(
                        e, bank(b0), lambda k: wg_[:, k, 0, :], lambda k: hT[:, k, c0:c0 + 512], KD),
                        reads=[r_slot[s], r_hT[blk]], writes=[r_bank[b0]])
                    em.op("pe", lambda e, b0=b0, wa=wa, o0=o0: mm_group(
                        e, bank(b0 + 1), lambda k: wa[:, k, :], lambda k: bu[:, k, o0:o0 + 512], 8),
                        reads=[r_slot[s], R["bu"]], writes=[r_bank[b0 + 1]])
                    em.op("pe", lambda e, b0=b0, wg_=wg_, c0=c0: mm_group(
                        e, bank(b0 + 2), lambda k: wg_[:, k, 1, :], lambda k: hT[:, k, c0:c0 + 512], KD),
                        reads=[r_slot[s], r_hT[blk]], writes=[r_bank[b0 + 2]])
                    em.op("pe", lambda e, b0=b0, wp=wp, o0=o0, g=g: mm_group(
                        e, bank(b0 + 3), lambda k: wp[:, k, :], lambda k: pl[:, 2 * g + k, o0:o0 + 512], 2),
                        reads=[r_slot[s], R["pl"]], writes=[r_bank[b0 + 3]])
                    if blk == 1:
                        issue_job()
                    em.op("act", lambda e, b0=b0, par=par, j=j: e.activation(out=sga[par], in_=bank(b0), func=AF.Sigmoid,
                                                                             bias=PW(j)),
                          reads=[r_bank[b0], R["prm"]], writes=[r_sga[par]])
                    em.op("act", lambda e, b0=b0, par=par, j=j: e.activation(out=sgb[par], in_=bank(b0 + 2), func=AF.Sigmoid,
                                                                             bias=PW(16 + j)),
                          reads=[r_bank[b0 + 2], R["prm"]], writes=[r_sgb[par]])
                    em.op("dve", lambda e, b0=b0, par=par: e.tensor_tensor(out=m1, in0=bank(b0 + 1), in1=sga[par], op=ALU.mult),
                          reads=[r_bank[b0 + 1], r_sga[par]], writes=[R["m1"]])
                    em.op("dve", lambda e, b0=b0, par=par, j=j: e.scalar_tensor_tensor(
                        out=m2, in0=bank(b0 + 3), scalar=PW(64 + j), in1=sgb[par], op0=ALU.mult, op1=ALU.mult),
                        reads=[r_bank[b0 + 3], r_sgb[par], R["prm"]], writes=[R["m2"]])
                    em.op("dve", lambda e, j=j, o0=o0: e.tensor_tensor(out=mg[:, j, o0:o0 + 512], in0=m1, in1=m2, op=ALU.add),
                          reads=[R["m1"], R["m2"]], writes=r_mg[4 * blk:4 * blk + 4])

            if debug:
                em.op("sp", lambda e: e.dma_start(out=dbg["dbg_bu"][:, :], in_=A[:, 8320:12416].bitcast(BF16)),
                      reads=[R["bu"]], writes=[r_dbg], sem=s_dbg, inc=16)
                em.op("sp", lambda e: e.dma_start(out=dbg["dbg_pl"][:, :], in_=A[:, 12416:16512].bitcast(BF16)),
                      reads=[R["pl"]], writes=[r_dbg], sem=s_dbg, inc=16)
                em.op("sp", lambda e: e.dma_start(out=dbg["dbg_merged"][:, :], in_=arenaB[:, :].bitcast(BF16)),
                      reads=r_mg, writes=[r_dbg], sem=s_dbg, inc=16)

            ovl = [r_hT[0], r_hT[1], R["bu"], R["pl"]] + ([r_dbg] if debug else [])
            em.wait_tokens("sp", [t for r_ in ovl for t in r_.w + r_.r + r_.pr])
            for i in range(NTOK):
                em.op("sp", lambda e, i=i: e.dma_start(out=x1[:, i, :], in_=x_c[H + i * 128:H + (i + 1) * 128, :]),
                      writes=[r_x1[i]], sem=s_r[i], inc=16)
            n2_stage2 = {}

            def n2_step(i):
                if i < 0:
                    return
                n2_stage2[i] = norm_to_T(9 + i, x1[:, i, :], [r_x1[i]], 128, hb2[i], r_hb2[i], 2 + (i % 2),
                                         mg, r_mg[i], i * 128, "act" if i % 2 == 0 else "dve")

            def n2_step2(i):
                if i < 0:
                    return
                n2_stage2[i]()

            for db in range(4):
                sa = take_job()
                sb = take_job()
                wA = slot(sa, 4096).rearrange("p (k c) -> p k c", k=8)
                wB = slot(sb, 4096).rearrange("p (k c) -> p k c", k=8)
                for i in range(NTOK):
                    b = next_bank() if db < 3 else i % 4
                    em.op("pe", lambda e, b=b, i=i, wA=wA, wB=wB: mm_group(
                        e, bank(b), lambda k: mg[:, k, i * 128:(i + 1) * 128],
                        lambda k: (wA if k < 8 else wB)[:, k % 8, :], KD),
                        reads=[r_slot[sa], r_slot[sb], r_mg[i]], writes=[r_bank[b]])
                    if i == NTOK - 1:
                        issue_job()
                        issue_job()
                    em.op("dve", lambda e, b=b, i=i, db=db: e.tensor_tensor(
                        out=x1[:, i, db * 512:(db + 1) * 512], in0=bank(b), in1=x1[:, i, db * 512:(db + 1) * 512], op=ALU.add),
                        reads=[r_bank[b], r_x1[i]], writes=[r_x1[i]])
                    if db == 3:
                        norm_stats(9 + i, x1[:, i, :], [r_x1[i]], 128)
                        n2_step(i - 1)
                        n2_step2(i - 2)
            n2_step(NTOK - 1)
            n2_step2(NTOK - 2)
            n2_step2(NTOK - 1)
            em.op("sp", lambda e: e.dma_start(out=gb, in_=gf_d[0:1, :].partition_broadcast(128)),
                  writes=[R["gb"]], sem=s_g, inc=16)
            if debug:
                em.op("sp", lambda e: e.dma_start(out=dbg["dbg_x1"][:, :], in_=A[:, 0:16384]),
                      reads=r_x1, writes=[r_dbg], sem=s_dbg, inc=16)
                em.op("sp", lambda e: e.dma_start(out=dbg["dbg_h2T"][:, :], in_=arenaB[:, :].bitcast(BF16)),
                      reads=r_mg, writes=[r_dbg], sem=s_dbg, inc=16)

            def final_norm(i):
                if i < 0:
                    return
                par = i % 4
                col = 17 + i
                norm_rstd(col, 128)
                em.op("dve", lambda e: e.scalar_tensor_tensor(
                    out=ost[par], in0=x1[:, i, :], scalar=rstdb[:, col:col + 1], in1=gb, op0=ALU.mult, op1=ALU.mult),
                    reads=[r_x1[i], r_rstd[col], R["gb"]], writes=[r_osth[par][0]] + r_mg)
                q_eng = "sp" if i % 2 == 1 else "pool"
                em.op(q_eng, lambda e: e.dma_start(out=out_d[i * 128:(i + 1) * 128, :], in_=ost[par]),
                      reads=[r_osth[par][0]], writes=[r_outs[i]], sem=s_oh[par][0], inc=16)

            state["bank_rr"] = 0
            for gi in range(G):
                for fi in range(FG):
                    s = take_job()
                    wgt = slot(s, 2048).rearrange("p (k c) -> p k c", k=16)
                    wup = wring[:, s * SLOT + 2048:s * SLOT + 4096].rearrange("p (k c) -> p k c", k=16)
                    for blk in range(2):
                        o0 = blk * 512
                        par = blk
                        bg = next_bank()
                        bu_ = next_bank()
                        em.op("pe", lambda e, bg=bg, wgt=wgt, o0=o0: mm_group(
                            e, bank(bg), lambda k: wgt[:, k, :], lambda k: mg[:, k, o0:o0 + 512], KD),
                            reads=[r_slot[s]] + r_mg[4 * blk:4 * blk + 4], writes=[r_bank[bg]])
                        em.op("pe", lambda e, bu_=bu_, wup=wup, o0=o0: mm_group(
                            e, bank(bu_), lambda k: wup[:, k, :], lambda k: mg[:, k, o0:o0 + 512], KD),
                            reads=[r_slot[s]] + r_mg[4 * blk:4 * blk + 4], writes=[r_bank[bu_]])
                        if blk == 1:
                            issue_job()
                        em.op("act", lambda e, bg=bg, par=par: e.activation(out=sg[par], in_=bank(bg), func=AF.Silu),
                              reads=[r_bank[bg]], writes=[r_sg[par]])
                        em.op("dve", lambda e, bu_=bu_, par=par, fi=fi, o0=o0: e.tensor_tensor(
                            out=act[:, fi, o0:o0 + 512], in0=bank(bu_), in1=sg[par], op=ALU.mult),
                            reads=[r_bank[bu_], r_sg[par]], writes=[r_act[blk]])
                def down_group(db, i, s, wd):
                    b = next_bank()
                    em.op("pe", lambda e, b=b, i=i, wd=wd: mm_group(
                        e, bank(b), lambda k: act[:, k, i * 128:(i + 1) * 128], lambda k: wd[:, k, :], FG),
                        reads=[r_slot[s], r_act[i // 4]], writes=[r_bank[b]])
                    return b

                def down_add(db, i, b):
                    em.op("dve", lambda e, b=b, i=i, db=db: e.tensor_tensor(
                        out=x1[:, i, db * 512:(db + 1) * 512], in0=bank(b), in1=x1[:, i, db * 512:(db + 1) * 512],
                        op=ALU.add),
                        reads=[r_bank[b], r_x1[i]], writes=[r_x1[i]])

                last = (gi == G - 1)
                for db in range(2 if last else 4):
                    s = take_job()
                    wd = slot(s, FG * 512).rearrange("p (k c) -> p k c", k=FG)
                    for i in range(NTOK):
                        b = down_group(db, i, s, wd)
                        if i == NTOK - 1:
                            issue_job()
                        down_add(db, i, b)
                if last:
                    s2 = take_job()
                    s3 = take_job()
                    wd2 = slot(s2, FG * 512).rearrange("p (k c) -> p k c", k=FG)
                    wd3 = slot(s3, FG * 512).rearrange("p (k c) -> p k c", k=FG)
                    for i in range(NTOK):
                        b2 = down_group(2, i, s2, wd2)
                        b3 = down_group(3, i, s3, wd3)
                        down_add(2, i, b2)
                        down_add(3, i, b3)
                        norm_stats(17 + i, x1[:, i, :], [r_x1[i]], 128)
                        final_norm(i - 1)

            final_norm(NTOK - 1)
            em.wait_tokens("sp", [t for r_ in r_outs for t in r_.w] + r_dbg.w)

        @block.sync
        def _(e):
            em.begin_pass("sp", e)
            program()

        @block.scalar
        def _(e):
            em.begin_pass("act", e)
            program()

        @block.vector
        def _(e):
            em.begin_pass("dve", e)
            program()

        @block.gpsimd
        def _(e):
            em.begin_pass("pool", e)
            program()

        @block.tensor
        def _(e):
            em.begin_pass("pe", e)
            program()

    return nc


def make_in_maps(inputs):
    f = lambda a: np.ascontiguousarray(np.asarray(a, dtype=np.float32))
    x = f(inputs["x"])
    B, S, _ = x.shape
    shared = {
        "norm1_g": f(inputs["norm1_g"]).reshape(1, D),
        "wpack": pack_weights(inputs),
        "sparams": np.ascontiguousarray(np.concatenate([
            f(inputs["b_gate"]).reshape(32, 128), f(inputs["conv_w"]).reshape(24, 128),
            f(inputs["conv_b"]).reshape(8, 128), f(inputs["pool_scale"]).reshape(16, 128)], axis=0)),
        "norm2_g": f(inputs["norm2_g"]).reshape(1, D),
        "final_g": f(inputs["final_g"]).reshape(1, D),
    }
    in_maps = []
    per_seq = S // T
    for c in range(NCORES):
        b, hf = divmod(c, per_seq)
        s0 = hf * T
        xc = np.zeros((TT, D), np.float32)
        if s0 > 0:
            xc[0:H] = x[b, s0 - H:s0]
        xc[H:] = x[b, s0:s0 + T]
        pos = np.broadcast_to((s0 + 1 + np.arange(16, dtype=np.float32))[None, :], (128, 16))
        m = dict(shared)
        m["x_c"] = xc
        m["pos16"] = np.ascontiguousarray(pos, dtype=np.float32)
        in_maps.append(m)
    return in_maps, (B, S)


def kernel(**inputs):
    in_maps, (B, S) = make_in_maps(inputs)
    nc = build_nc()
    res = run_bass_kernel_spmd(nc, in_maps, core_ids=list(range(NCORES)))
    out = np.concatenate([np.asarray(r["out"]) for r in res.results], axis=0)
    return out.reshape(B, S, D).astype(np.float32)
```

```python
from contextlib import ExitStack

import numpy as np
import concourse.bass as bass
import concourse.mybir as mybir
from concourse.bass_utils import run_bass_kernel_spmd

F32 = mybir.dt.float32
BF16 = mybir.dt.bfloat16
I32 = mybir.dt.int32
AF = mybir.ActivationFunctionType
ALU = mybir.AluOpType

NCORES = 8
D = 2048
KD = 16
T = 1024
H = 16
TT = T + H
NTOK = T // 128
DI = 8192
NF = 44
G = 4
FG = NF // G
SLOT = 6144
NSLOT = 4
EPS = 1e-6
POOL_W = (2, 4, 8, 16)


JOB_SIZES = ([6144] * 8 + [6144, 6144, 4096] + [5376] * 16 + [4096] * 8
             + ([4096] * FG + [FG * 512] * 4) * G)
WTOTAL = sum(JOB_SIZES)


def pack_weights(inputs):
    f = lambda a: np.asarray(a, dtype=np.float32)
    w_in = f(inputs["w_in"]).reshape(D, DI)
    w_a_out = f(inputs["w_a_out"]).reshape(1024, D)
    w_pool = f(inputs["w_pool"]).reshape(1024, 512)
    w_o = f(inputs["w_o"]).reshape(D, D)
    w_gate = f(inputs["w_ffn_gate"]).reshape(D, NF * 128)
    w_up = f(inputs["w_ffn_up"]).reshape(D, NF * 128)
    w_down = f(inputs["w_ffn_down"]).reshape(NF * 128, D)

    def kv(a):
        return a.reshape(a.shape[0] // 128, 128, a.shape[1]).transpose(1, 0, 2)

    blocks = []
    for j in range(8):
        blocks.append(np.stack([kv(w_in[:, wi * 1024 + j * 128:wi * 1024 + (j + 1) * 128]) for wi in range(3)],
                               axis=2).reshape(128, -1))
    for q in range(3):
        n = 3 if q < 2 else 2
        blocks.append(kv(w_in[:, 3072 + q * 384:3072 + q * 384 + n * 128]).reshape(128, -1))
    for j in range(16):
        g = j // 4
        blocks.append(np.concatenate([
            np.stack([kv(w_in[:, 4096 + j * 128:4096 + (j + 1) * 128]),
                      kv(w_in[:, 6144 + j * 128:6144 + (j + 1) * 128])], axis=2).reshape(128, -1),
            kv(w_a_out[:, j * 128:(j + 1) * 128]).reshape(128, -1),
            kv(w_pool[g * 256:(g + 1) * 256, (j % 4) * 128:(j % 4 + 1) * 128]).reshape(128, -1)], axis=1))
    for db in range(4):
        for hh in range(2):
            blocks.append(kv(w_o[hh * 1024:(hh + 1) * 1024, db * 512:(db + 1) * 512]).reshape(128, -1))
    for gi in range(G):
        for fi in range(FG):
            fidx = gi * FG + fi
            blocks.append(np.concatenate([kv(w_gate[:, fidx * 128:(fidx + 1) * 128]).reshape(128, -1),
                                          kv(w_up[:, fidx * 128:(fidx + 1) * 128]).reshape(128, -1)], axis=1))
        for db in range(4):
            blocks.append(kv(w_down[gi * FG * 128:(gi + 1) * FG * 128, db * 512:(db + 1) * 512]).reshape(128, -1))
    assert [b.shape[1] for b in blocks] == JOB_SIZES
    return np.ascontiguousarray(np.concatenate(blocks, axis=1))


class Sem:
    def __init__(self, h):
        self.h = h
        self.count = 0


class Res:
    def __init__(self, name):
        self.name = name
        self.reset()

    def reset(self):
        self.w = []
        self.r = []
        self.pr = []


class Emitter:
    def __init__(self, nc):
        self.nc = nc
        self.sems = []
        self.res = []
        self.cur = None
        self.eng = None
        self.waited = {}
        self.engsem = {}

    def sem(self, h):
        s = Sem(h)
        self.sems.append(s)
        return s

    def resource(self, name):
        r = Res(name)
        self.res.append(r)
        return r

    def begin_pass(self, name, eng):
        self.cur = name
        self.eng = eng
        self.waited = {}
        for s in self.sems:
            s.count = 0
        for r in self.res:
            r.reset()

    def wait_tokens(self, engname, toks):
        if self.cur != engname:
            return
        need = {}
        for (s, v) in toks:
            if need.get(s, 0) < v:
                need[s] = v
        for s, v in need.items():
            if self.waited.get(s, 0) >= v:
                continue
            self.eng.wait_ge(s.h, v)
            self.waited[s] = v

    def op(self, engname, fn, reads=(), writes=(), sem=None, inc=1):
        toks = []
        for r in reads:
            toks += r.w
        for w in writes:
            toks += w.w + w.r + w.pr
        self.wait_tokens(engname, toks)
        ins = fn(self.eng) if self.cur == engname else None
        s = sem if sem is not None else self.engsem[engname]
        s.count += inc
        if ins is not None:
            ins.then_inc(s.h, inc)
        tok = (s, s.count)
        for r in reads:
            r.r.append(tok)
        for w in writes:
            if w.r:
                w.pr = w.r
                w.r = []
                w.w = [tok]
            else:
                w.w.append(tok)
        return tok


def build_nc(debug=False):
    nc = bass.Bass("TRN2", target_bir_lowering=False)

    def din(name, shape, dt=F32):
        return nc.dram_tensor(name, list(shape), dt, kind="ExternalInput").ap()

    x_c = din("x_c", [TT, D])
    pos16 = din("pos16", [128, 16])
    g1_d = din("norm1_g", [1, D])
    wpack = din("wpack", [128, WTOTAL])
    sparams = din("sparams", [80, 128])
    g2_d = din("norm2_g", [1, D])
    gf_d = din("final_g", [1, D])
    out_d = nc.dram_tensor("out", [T, D], F32, kind="ExternalOutput").ap()
    dbg = {}
    if debug:
        for nm, shp, dt in [("dbg_hT", [128, KD * TT], BF16), ("dbg_bu", [128, 8 * T], BF16),
                            ("dbg_pl", [128, 8 * T], BF16), ("dbg_merged", [128, KD * T], BF16),
                            ("dbg_x1", [128, NTOK * D], F32), ("dbg_h2T", [128, KD * T], BF16),
                            ("dbg_prm", [128, 80], F32)]:
            dbg[nm] = nc.dram_tensor(nm, shp, dt, kind="ExternalOutput").ap()

    with ExitStack() as es:
        def sb(name, shape, dt):
            return es.enter_context(nc.sbuf_tensor(name, shape, dt))

        def hsem(name):
            return es.enter_context(nc.semaphore(name))

        arenaA = sb("arenaA", [128, 16512], F32)
        arenaB = sb("arenaB", [128, 8192], F32)
        arenaC = sb("arenaC", [128, 12288], F32)
        wring = sb("wring", [128, NSLOT * SLOT], BF16)
        identf = sb("identf", [128, 128], F32)
        identb = sb("identb", [128, 128], BF16)
        iot = sb("iot", [128, 128], I32)
        prm_rows = sb("prm_rows", [128, 128], F32)
        prm = sb("prm", [128, 80], F32)
        ssb = sb("ssb", [128, 32], F32)
        rsb = sb("rsb", [128, 32], F32)
        rstdb = sb("rstdb", [128, 32], F32)
        pos_sb = sb("pos_sb", [128, 16], F32)
        cnt_sb = sb("cnt_sb", [128, 64], F32)
        inv_sb = sb("inv_sb", [128, 64], F32)
        tmp16 = sb("tmp16", [128, 16], F32)
        junk = sb("junk", [128, 4], F32)
        zeros_bf = sb("zeros_bf", [128, 512], BF16)
        ps = es.enter_context(nc.psum_tensor("ps", [128, 4096], F32))
        h_act = hsem("s_act"); h_dve = hsem("s_dve"); h_pe = hsem("s_pe"); h_pool = hsem("s_pool")
        h_c = hsem("s_c"); h_x0 = hsem("s_x0"); h_x1 = hsem("s_x1"); h_x2 = hsem("s_x2")
        h_w0 = hsem("s_w0"); h_w1 = hsem("s_w1"); h_w2 = hsem("s_w2"); h_w3 = hsem("s_w3")
        h_o0 = hsem("s_o0"); h_o1 = hsem("s_o1"); h_g = hsem("s_g"); h_o2 = hsem("s_o2"); h_o3 = hsem("s_o3")
        h_r0 = hsem("s_r0"); h_r1 = hsem("s_r1"); h_r2 = hsem("s_r2"); h_r3 = hsem("s_r3")
        h_r4 = hsem("s_r4"); h_r5 = hsem("s_r5"); h_r6 = hsem("s_r6"); h_r7 = hsem("s_r7")
        h_dbg = hsem("s_dbg")
        h_p = hsem("s_p")
        block = es.enter_context(nc.Block())
        em = Emitter(nc)
        em.engsem = {"act": em.sem(h_act), "dve": em.sem(h_dve), "pe": em.sem(h_pe),
                     "pool": em.sem(h_pool)}
        s_c = em.sem(h_c)
        s_x = [em.sem(h_x0), em.sem(h_x1), em.sem(h_x2)]
        s_w = [em.sem(h_w0), em.sem(h_w1), em.sem(h_w2), em.sem(h_w3)]
        s_g = em.sem(h_g)
        s_r = [em.sem(h) for h in (h_r0, h_r1, h_r2, h_r3, h_r4, h_r5, h_r6, h_r7)]
        s_oh = [[em.sem(h_)] for h_ in (h_o0, h_o1, h_o2, h_o3)]
        s_dbg = em.sem(h_dbg)
        s_p = em.sem(h_p)

        A = arenaA
        hT = A[:, 0:8320].bitcast(BF16).rearrange("p (k t) -> p k t", t=TT)
        bu = A[:, 8320:12416].bitcast(BF16).rearrange("p (k t) -> p k t", t=T)
        pl = A[:, 12416:16512].bitcast(BF16).rearrange("p (k t) -> p k t", t=T)
        x1 = A[:, 0:16384].rearrange("p (i d) -> p i d", d=D)
        mg = arenaB[:, :].bitcast(BF16).rearrange("p (k t) -> p k t", t=T)
        ost = [arenaB[:, 2048 * q_:2048 * (q_ + 1)] for q_ in range(4)]
        C = arenaC
        gb = C[:, 0:2048]
        hb = [C[:, 2048:3072].bitcast(BF16), C[:, 3072:4096].bitcast(BF16)]
        sq = C[:, 4096:5120].bitcast(BF16)
        FB = 5120
        hb2 = hb + [C[:, FB + 1024 * n_:FB + 1024 * (n_ + 1)].bitcast(BF16) for n_ in range(6)]
        hbc = C[:, FB + 6144:FB + 7168].bitcast(BF16)
        xt = ([C[:, FB:FB + 2048], C[:, FB + 2048:FB + 4096], C[:, FB + 4096:FB + 6144]]
              + [arenaB[:, 2048 * n_:2048 * (n_ + 1)] for n_ in range(4)]
              + [A[:, 8320 + 2048 * n_:8320 + 2048 * (n_ + 1)] for n_ in range(2)])
        R1 = C[:, FB:FB + 1040]
        R2 = C[:, FB + 1040:FB + 2080]
        R3 = C[:, FB + 2080:FB + 3120]
        sga = [C[:, FB + 3120:FB + 3632], C[:, FB + 3632:FB + 4144]]
        sgb = [C[:, FB + 4144:FB + 4656], C[:, FB + 4656:FB + 5168]]
        m1 = C[:, FB + 5168:FB + 5680]
        m2 = C[:, FB + 5680:FB + 6192]
        act = C[:, FB:FB + 5632].bitcast(BF16).rearrange("p (k t) -> p k t", t=T)
        sg = [C[:, FB + 5632:FB + 6144], C[:, FB + 6144:FB + 6656]]

        def slot(s, n=SLOT):
            return wring[:, s * SLOT:s * SLOT + n]

        def bank(b):
            return ps[:, b * 512:(b + 1) * 512]

        def pTb(q):
            return ps[:, q * 1024:(q + 1) * 1024].bitcast(BF16).rearrange("p (k t) -> p k t", t=128)

        R = {}
        for nm in ["zeros", "iot", "identf", "identb", "prm_rows", "prm", "pos", "cnt", "inv", "tmp16",
                   "gb", "sq", "hT", "bu", "pl", "mg", "R1", "R2", "R3", "m1", "m2"]:
            R[nm] = em.resource(nm)
        r_xt = [em.resource("xt%d" % n_) for n_ in range(9)]
        r_hb = [em.resource("hb0"), em.resource("hb1")]
        r_hb2 = r_hb + [em.resource("hb2_%d" % n_) for n_ in range(6)]
        r_hbc = em.resource("hbc")
        r_sga = [em.resource("sga0"), em.resource("sga1")]
        r_sgb = [em.resource("sgb0"), em.resource("sgb1")]
        r_sg = [em.resource("sg0"), em.resource("sg1")]
        r_ost = [em.resource("ost0"), em.resource("ost1")]
        r_osth = [[em.resource("ost%d_%d" % (a_, b_)) for b_ in range(2)] for a_ in range(4)]
        r_bank = [em.resource("bank%d" % b) for b in range(8)]
        r_slot = [em.resource("slot%d" % s) for s in range(NSLOT)]
        r_x1 = [em.resource("x1_%d" % i) for i in range(NTOK)]
        r_hT = [em.resource("hT_a"), em.resource("hT_b")]
        r_mg = [em.resource("mg_%d" % i) for i in range(NTOK)]
        r_act = [em.resource("act0"), em.resource("act1")]
        r_ss = [em.resource("ss%d" % c) for c in range(32)]
        r_rs = [em.resource("rs%d" % c) for c in range(32)]
        r_ss2 = [em.resource("ss2_%d" % c) for c in range(32)]
        r_rstd = [em.resource("rstd%d" % c) for c in range(32)]
        r_outs = [em.resource("out%d" % i) for i in range(NTOK)]
        r_dbg = em.resource("dbg")

        NJOBS = len(JOB_SIZES)
        job_off = [0]
        for n_ in JOB_SIZES:
            job_off.append(job_off[-1] + n_)

        def program():
            state = {"next_job": 0, "bank_rr": 0}

            def issue_job():
                n = state["next_job"]
                if n >= NJOBS:
                    return
                s = n % NSLOT
                sz = JOB_SIZES[n]
                off = job_off[n]
                em.op("pool", lambda e, s=s, sz=sz, off=off: e.dma_start(out=slot(s, sz), in_=wpack[:, off:off + sz]),
                      writes=[r_slot[s]], sem=s_w[s], inc=16)
                state["next_job"] = n + 1

            job_ctr = {"n": 0}

            def take_job():
                n = job_ctr["n"]
                job_ctr["n"] = n + 1
                return n % NSLOT

            def next_bank():
                b = state["bank_rr"]
                state["bank_rr"] = (b + 1) % 8
                return b

            em.op("pool", lambda e: e.iota(iot[:, :], [[1, 128]], base=0, channel_multiplier=-1),
                  writes=[R["iot"]])
            em.op("dve", lambda e: e.tensor_single_scalar(out=identf[:, :], in_=iot[:, :], scalar=0.0,
                                                           op=ALU.is_equal),
                  reads=[R["iot"]], writes=[R["identf"]])
            em.op("dve", lambda e: e.tensor_copy(out=identb[:, :], in_=identf[:, :]),
                  reads=[R["identf"]], writes=[R["identb"]])
            em.op("dve", lambda e: e.memset(zeros_bf[:, :], 0.0), writes=[R["zeros"]])

            def warm(n):
                def f(e):
                    ins = None
                    for _ in range(n):
                        ins = e.matmul(bank(7), identb[:, :], zeros_bf[:, :], start=True, stop=True)
                    return ins
                em.op("pe", f, reads=[R["identb"], R["zeros"]], writes=[r_bank[7]])

            def load_small_params():
                em.op("sp", lambda e: e.dma_start(out=prm_rows[0:80, :], in_=sparams[:, :]),
                      writes=[R["prm_rows"]], sem=s_c, inc=16)
                em.op("sp", lambda e: e.dma_start(out=pos_sb[:, :], in_=pos16[:, :]),
                      writes=[R["pos"]], sem=s_p, inc=16)

            def small_param_compute():
                em.op("pe", lambda e: e.transpose(bank(7)[:, 0:80], prm_rows[0:80, :], identf[0:80, 0:80]),
                      reads=[R["prm_rows"], R["identf"]], writes=[r_bank[7]])
                em.op("act", lambda e: e.copy(out=prm[:, :], in_=bank(7)[:, 0:80]),
                      reads=[r_bank[7]], writes=[R["prm"]])

            def pool_count_compute():
                for g, w in enumerate(POOL_W):
                    em.op("dve", lambda e, g=g, w=w: e.tensor_scalar(out=cnt_sb[:, g * 16:(g + 1) * 16], in0=pos_sb[:, :],
                                                                     scalar1=float(w), scalar2=None, op0=ALU.min),
                          reads=[R["pos"]], writes=[R["cnt"]])
                em.op("dve", lambda e: e.reciprocal(out=inv_sb[:, :], in_=cnt_sb[:, :]),
                      reads=[R["cnt"]], writes=[R["inv"]])

            def norm_stats(col, src_ap, src_res, npart):
                em.op("act", lambda e: e.activation(out=sq[0:npart, :], in_=src_ap, func=AF.Square,
                                                    accum_out=ssb[0:npart, col:col + 1]),
                      reads=src_res, writes=[R["sq"], r_ss[col]])
                em.op("act", lambda e: e.copy(out=junk[0:npart, 0:1], in_=ssb[0:npart, col:col + 1]),
                      reads=[r_ss[col]], writes=[r_ss2[col]])
                em.op("act", lambda e: e.activation(out=rsb[0:npart, col:col + 1], in_=ssb[0:npart, col:col + 1],
                                                    func=AF.Sqrt, scale=1.0 / D, bias=EPS),
                      reads=[r_ss2[col]], writes=[r_rs[col]])

            def norm_rstd(col, npart):
                em.op("dve", lambda e: e.reciprocal(out=rstdb[0:npart, col:col + 1], in_=rsb[0:npart, col:col + 1]),
                      reads=[r_rs[col]], writes=[r_rstd[col]])

            def norm_to_T(col, src_ap, src_res, npart, hbuf, hres, q, dstT, dst_res, c0, evac):
                norm_rstd(col, npart)
                em.op("dve", lambda e: e.scalar_tensor_tensor(out=hbuf[0:npart, :], in0=src_ap,
                                                              scalar=rstdb[0:npart, col:col + 1], in1=gb[0:npart, :],
                                                              op0=ALU.mult, op1=ALU.mult),
                      reads=src_res + [r_rstd[col], R["gb"]], writes=[hres])

                def stage2():
                    pt = pTb(q)

                    def tr(e):
                        ins = None
                        for k in range(KD):
                            ins = e.transpose(pt[:, k, 0:npart], hbuf[0:npart, k * 128:(k + 1) * 128],
                                              identb[0:npart, 0:npart])
                        return ins
                    em.op("pe", tr, reads=[hres, R["identb"]], writes=[r_bank[2 * q], r_bank[2 * q + 1]])
                    if evac == "act":
                        em.op("act", lambda e: e.copy(out=dstT[:, :, c0:c0 + npart], in_=pt[:, :, 0:npart]),
                              reads=[r_bank[2 * q], r_bank[2 * q + 1]], writes=[dst_res])
                    else:
                        em.op("dve", lambda e: e.tensor_copy(out=dstT[:, :, c0:c0 + npart], in_=pt[:, :, 0:npart]),
                              reads=[r_bank[2 * q], r_bank[2 * q + 1]], writes=[dst_res])
                return stage2

            def mm_group(e, out_ap, lhs_fn, rhs_fn, nk):
                ins = None
                for k in range(nk):
                    ins = e.matmul(out_ap, lhs_fn(k), rhs_fn(k), start=(k == 0), stop=(k == nk - 1))
                return ins

            PW = lambda c: prm[:, c:c + 1]

            xsem = s_x + s_r[0:6]
            for ti in range(NTOK + 1):
                npart = H if ti == 0 else 128
                r0 = 0 if ti == 0 else H + (ti - 1) * 128
                em.op("sp", lambda e, ti=ti, npart=npart, r0=r0: e.dma_start(out=xt[ti][0:npart, :], in_=x_c[r0:r0 + npart, :]),
                      writes=[r_xt[ti]], sem=xsem[ti], inc=16)
                if ti == 0:
                    em.op("sp", lambda e: e.dma_start(out=gb, in_=g1_d[0:1, :].partition_broadcast(128)),
                          writes=[R["gb"]], sem=s_g, inc=16)
                if ti == 4:
                    load_small_params()
            em.wait_tokens("pool", [t for r_ in r_xt[0:3] for t in r_.w])
            issue_job()
            em.wait_tokens("pool", [t for r_ in r_xt for t in r_.w])
            for _ in range(NSLOT - 1):
                issue_job()

            def phase1_tiles(tiles, qfn, hbufs, lag, nwarm=0, drain=True):
                pend = []
                for idx, ti in enumerate(tiles):
                    npart = H if ti == 0 else 128
                    r0 = 0 if ti == 0 else H + (ti - 1) * 128
                    hbuf, hres = hbufs[idx % len(hbufs)]
                    norm_stats(ti, xt[ti][0:npart, :], [r_xt[ti]], npart)
                    pend.append(norm_to_T(ti, xt[ti][0:npart, :], [r_xt[ti]], npart, hbuf, hres, qfn(ti),
                                          hT, r_hT[0 if ti <= 4 else 1], r0, "act" if ti % 2 == 1 else "dve"))
                    if len(pend) > lag:
                        pend.pop(0)()
                        if nwarm:
                            warm(nwarm)
                if not drain:
                    return pend
                for st2 in pend:
                    st2()
                    if nwarm:
                        warm(nwarm)
                return []

            def mixA_setup(j):
                s = take_job()
                wv = slot(s).rearrange("p (k w c) -> p k w c", k=16, w=3)
                bH = 3 if j == 0 else 6 + (j % 2)
                return s, wv, bH

            def mixA_pe_halo(j, s, wv, bH):
                def halo(e):
                    mm_group(e, bank(bH)[:, 0:16], lambda k: wv[:, k, 1, :], lambda k: hT[:, k, 0:16], KD)
                    return mm_group(e, bank(bH)[:, 16:32], lambda k: wv[:, k, 2, :], lambda k: hT[:, k, 0:16], KD)
                em.op("pe", halo, reads=[r_slot[s], r_hT[0]], writes=[r_bank[bH]])

            def mixA_ew_halo(j, bH):
                em.op("act", lambda e: e.copy(out=R1[:, 0:16], in_=bank(bH)[:, 0:16]),
                      reads=[r_bank[bH]], writes=[R["R1"]])
                em.op("dve", lambda e: e.tensor_tensor(out=R2[:, 0:16], in0=bank(bH)[:, 16:32], in1=R1[:, 0:16],
                                                       op=ALU.mult),
                      reads=[r_bank[bH], R["R1"]], writes=[R["R2"]])

            def mixA_pe_blk(j, blk, s, wv, which=(1, 2, 0)):
                c0 = H + blk * 512
                bs = 3 * ((2 * j + blk) % 2)
                for (wi, bb) in [(1, bs), (2, bs + 1), (0, bs + 2)]:
                    if wi not in which:
                        continue
                    em.op("pe", lambda e, wi=wi, bb=bb: mm_group(
                        e, bank(bb), lambda k: wv[:, k, wi, :], lambda k: hT[:, k, c0:c0 + 512], KD),
                        reads=[r_slot[s], r_hT[blk]], writes=[r_bank[bb]])

            def mixA_ew_blk(j, blk):
                c0 = H + blk * 512
                o0 = blk * 512
                bs = 3 * ((2 * j + blk) % 2)
                em.op("act", lambda e: e.copy(out=R1[:, c0:c0 + 512], in_=bank(bs)),
                      reads=[r_bank[bs]], writes=[R["R1"]])
                em.op("dve", lambda e: e.tensor_tensor(out=R2[:, c0:c0 + 512], in0=bank(bs + 1),
                                                       in1=R1[:, c0:c0 + 512], op=ALU.mult),
                      reads=[r_bank[bs + 1], R["R1"]], writes=[R["R2"]])
                em.op("act", lambda e: e.activation(out=R3[:, o0:o0 + 512], in_=R2[:, c0:c0 + 512],
                                                    func=AF.Identity, scale=PW(48 + j), bias=PW(56 + j)),
                      reads=[R["R2"], R["prm"]], writes=[R["R3"]])
                for (sh, pc) in [(1, 40 + j), (2, 32 + j)]:
                    em.op("dve", lambda e, sh=sh, pc=pc: e.scalar_tensor_tensor(
                        out=R3[:, o0:o0 + 512], in0=R2[:, c0 - sh:c0 - sh + 512], scalar=PW(pc),
                        in1=R3[:, o0:o0 + 512], op0=ALU.mult, op1=ALU.add),
                        reads=[R["R2"], R["R3"], R["prm"]], writes=[R["R3"]])
                em.op("dve", lambda e: e.tensor_tensor(out=bu[:, j, o0:o0 + 512], in0=bank(bs + 2),
                                                       in1=R3[:, o0:o0 + 512], op=ALU.mult),
                      reads=[r_bank[bs + 2], R["R3"]], writes=[R["bu"]])

            warm(24)
            phase1_tiles(range(0, 5), lambda ti: ti % 3, [(hb[0], r_hb[0]), (hb[1], r_hb[1]), (hbc, r_hbc)], 2, nwarm=4)
            small_param_compute()
            st2s = phase1_tiles(range(5, NTOK + 1), lambda ti: 2 + (ti % 2),
                                [(hb2[n_], r_hb2[n_]) for n_ in range(2, 6)], 4, drain=False)
            s, wv, bH = mixA_setup(0)
            mixA_pe_halo(0, s, wv, bH)
            mixA_pe_blk(0, 0, s, wv, which=(1, 2))
            st2s[0]()
            st2s[1]()
            mixA_pe_blk(0, 0, s, wv, which=(0,))
            st2s[2]()
            st2s[3]()
            em.op("sp", lambda e: e.dma_start(out=gb, in_=g2_d[0:1, :].partition_broadcast(128)),
                  writes=[R["gb"]], sem=s_g, inc=16)
            if debug:
                em.op("sp", lambda e: e.dma_start(out=dbg["dbg_prm"][:, :], in_=prm[:, :]),
                      reads=[R["prm"]], writes=[r_dbg], sem=s_dbg, inc=16)
                em.op("sp", lambda e: e.dma_start(out=dbg["dbg_hT"][:, :], in_=A[:, 0:8320].bitcast(BF16)),
                      reads=r_hT, writes=[r_dbg], sem=s_dbg, inc=16)
            ov = [t for n_ in range(2, 6) for t in r_hb2[n_].r + r_hb2[n_].w] + [t for t in r_hbc.r + r_hbc.w]
            em.wait_tokens("act", ov)
            em.wait_tokens("dve", ov)
            mixA_ew_halo(0, bH)
            mixA_ew_blk(0, 0)
            mixA_pe_blk(0, 1, s, wv)
            issue_job()
            mixA_ew_blk(0, 1)
            for j in range(1, 8):
                s, wv, bH = mixA_setup(j)
                mixA_pe_halo(j, s, wv, bH)
                mixA_ew_halo(j, bH)
                mixA_pe_blk(j, 0, s, wv)
                mixA_ew_blk(j, 0)
                mixA_pe_blk(j, 1, s, wv)
                issue_job()
                mixA_ew_blk(j, 1)

            pool_count_compute()
            for j in range(8):
                if j % 3 == 0:
                    s = take_job()
                n = 3 if j < 6 else 2
                wv = slot(s, 16 * n * 128).rearrange("p (k c) -> p k c", k=16)
                cw = (j % 3) * 128
                g = j // 2
                bH = 6 + (j % 2)
                em.op("pe", lambda e, wv=wv, cw=cw, bH=bH: mm_group(
                    e, bank(bH)[:, 0:16], lambda k: wv[:, k, cw:cw + 128], lambda k: hT[:, k, 0:16], KD),
                    reads=[r_slot[s], r_hT[0]], writes=[r_bank[bH]])
                em.op("act", lambda e, bH=bH: e.copy(out=R1[:, 0:16], in_=bank(bH)[:, 0:16]),
                      reads=[r_bank[bH]], writes=[R["R1"]])
                for blk in range(2):
                    c0 = H + blk * 512
                    bb = 3 * (j % 2) + blk
                    em.op("pe", lambda e, wv=wv, cw=cw, bb=bb, c0=c0: mm_group(
                        e, bank(bb), lambda k: wv[:, k, cw:cw + 128], lambda k: hT[:, k, c0:c0 + 512], KD),
                        reads=[r_slot[s], r_hT[blk]], writes=[r_bank[bb]])
                    if blk == 1 and (j % 3 == 2 or j == 7):
                        issue_job()
                    em.op("act", lambda e, bb=bb, c0=c0: e.copy(out=R1[:, c0:c0 + 512], in_=bank(bb)),
                          reads=[r_bank[bb]], writes=[R["R1"]])
                bufs = [(R2, R["R2"]), (R3, R["R3"])]
                cur, cur_res = R1, R["R1"]
                sh = 1
                for step in range(g + 1):
                    dst, dst_res = bufs[step % 2]
                    lo = 2 * sh - 1
                    em.op("dve", lambda e, dst=dst, cur=cur, lo=lo, sh=sh: e.tensor_tensor(
                        out=dst[:, lo:TT], in0=cur[:, lo:TT], in1=cur[:, lo - sh:TT - sh], op=ALU.add),
                        reads=[cur_res], writes=[dst_res])
                    cur, cur_res = dst, dst_res
                    sh *= 2
                w = POOL_W[g]
                em.op("dve", lambda e, cur=cur, w=w, j=j: e.scalar_tensor_tensor(
                    out=pl[:, j, :], in0=cur[:, H:TT], scalar=1.0 / w, in1=R1[:, H:TT],
                    op0=ALU.mult, op1=ALU.subtract),
                    reads=[cur_res, R["R1"]], writes=[R["pl"]])
                em.op("dve", lambda e, cur=cur, g=g: e.tensor_tensor(out=tmp16[:, :], in0=cur[:, H:H + 16],
                                                                     in1=inv_sb[:, g * 16:(g + 1) * 16], op=ALU.mult),
                      reads=[cur_res, R["inv"]], writes=[R["tmp16"]])
                em.op("dve", lambda e, j=j: e.tensor_tensor(out=pl[:, j, 0:16], in0=tmp16[:, :], in1=R1[:, H:H + 16],
                                                            op=ALU.subtract),
                      reads=[R["tmp16"], R["R1"]], writes=[R["pl"]])

            for j in range(16):
                s = take_job()
                wg_ = slot(s, 4096).rearrange("p (k w c) -> p k w c", k=16, w=2)
                wa = wring[:, s * SLOT + 4096:s * SLOT + 5120].rearrange("p (k c) -> p k c", k=8)
                wp = wring[:, s * SLOT + 5120:s * SLOT + 5376].rearrange("p (k c) -> p k c", k=2)
                g = j // 4
                for blk in range(2):
                    c0 = H + blk * 512
                    o0 = blk * 512
                    par = (2 * j + blk) % 2
                    b0 = 4 * par
                    em.op("pe", lambda e, b0=b0, wg_=wg_, c0=c0: mm_group(
                        e, bank(b0), lambda k: wg_[:, k, 0, :], lambda k: hT[:, k, c0:c0 + 512], KD),
                        reads=[r_slot[s], r_hT[blk]], writes=[r_bank[b0]])
                    em.op("pe", lambda e, b0=b0, wa=wa, o0=o0: mm_group(
                        e, bank(b0 + 1), lambda k: wa[:, k, :], lambda k: bu[:, k, o0:o0 + 512], 8),
                        reads=[r_slot[s], R["bu"]], writes=[r_bank[b0 + 1]])
                    em.op("pe", lambda e, b0=b0, wg_=wg_, c0=c0: mm_group(
                        e, bank(b0 + 2), lambda k: wg_[:, k, 1, :], lambda k: hT[:, k, c0:c0 + 512], KD),
                        reads=[r_slot[s], r_hT[blk]], writes=[r_bank[b0 + 2]])
                    em.op("pe", lambda e, b0=b0, wp=wp, o0=o0, g=g: mm_group(
                        e, bank(b0 + 3), lambda k: wp[:, k, :], lambda k: pl[:, 2 * g + k, o0:o0 + 512], 2),
                        reads=[r_slot[s], R["pl"]], writes=[r_bank[b0 + 3]])
                    if blk == 1:
                        issue_job()
                    em.op("act", lambda e, b0=b0, par=par, j=j: e.activation(out=sga[par], in_=bank(b0), func=AF.Sigmoid,
                                                                             bias=PW(j)),
                          reads=[r_bank[b0], R["prm"]], writes=[r_sga[par]])
                    em.op("act", lambda e, b0=b0, par=par, j=j: e.activation(out=sgb[par], in_=bank(b0 + 2), func=AF.Sigmoid,
                                                                             bias=PW(16 + j)),
                          reads=[r_bank[b0 + 2], R["prm"]], writes=[r_sgb[par]])
                    em.op("dve", lambda e, b0=b0, par=par: e.tensor_tensor(out=m1, in0=bank(b0 + 1), in1=sga[par], op=ALU.mult),
                          reads=[r_bank[b0 + 1], r_sga[par]], writes=[R["m1"]])
                    em.op("dve", lambda e, b0=b0, par=par, j=j: e.scalar_tensor_tensor(
                        out=m2, in0=bank(b0 + 3), scalar=PW(64 + j), in1=sgb[par], op0=ALU.mult, op1=ALU.mult),
                        reads=[r_bank[b0 + 3], r_sgb[par], R["prm"]], writes=[R["m2"]])
                    em.op("dve", lambda e, j=j, o0=o0: e.tensor_tensor(out=mg[:, j, o0:o0 + 512], in0=m1, in1=m2, op=ALU.add),
                          reads=[R["m1"], R["m2"]], writes=r_mg[4 * blk:4 * blk + 4])

            if debug:
                em.op("sp", lambda e: e.dma_start(out=dbg["dbg_bu"][:, :], in_=A[:, 8320:12416].bitcast(BF16)),
                      reads=[R["bu"]], writes=[r_dbg], sem=s_dbg, inc=16)
                em.op("sp", lambda e: e.dma_start(out=dbg["dbg_pl"][:, :], in_=A[:, 12416:16512].bitcast(BF16)),
                      reads=[R["pl"]], writes=[r_dbg], sem=s_dbg, inc=16)
                em.op("sp", lambda e: e.dma_start(out=dbg["dbg_merged"][:, :], in_=arenaB[:, :].bitcast(BF16)),
                      reads=r_mg, writes=[r_dbg], sem=s_dbg, inc=16)

            ovl = [r_hT[0], r_hT[1], R["bu"], R["pl"]] + ([r_dbg] if debug else [])
            em.wait_tokens("sp", [t for r_ in ovl for t in r_.w + r_.r + r_.pr])
            for i in range(NTOK):
                em.op("sp", lambda e, i=i: e.dma_start(out=x1[:, i, :], in_=x_c[H + i * 128:H + (i + 1) * 128, :]),
                      writes=[r_x1[i]], sem=s_r[i], inc=16)
            n2_stage2 = {}

            def n2_step(i):
                if i < 0:
                    return
                n2_stage2[i] = norm_to_T(9 + i, x1[:, i, :], [r_x1[i]], 128, hb2[i], r_hb2[i], 2 + (i % 2),
                                         mg, r_mg[i], i * 128, "act" if i % 2 == 0 else "dve")

            def n2_step2(i):
                if i < 0:
                    return
                n2_stage2[i]()

            for db in range(4):
                sa = take_job()
                sb = take_job()
                wA = slot(sa, 4096).rearrange("p (k c) -> p k c", k=8)
                wB = slot(sb, 4096).rearrange("p (k c) -> p k c", k=8)
                for i in range(NTOK):
                    b = next_bank() if db < 3 else i % 4
                    em.op("pe", lambda e, b=b, i=i, wA=wA, wB=wB: mm_group(
                        e, bank(b), lambda k: mg[:, k, i * 128:(i + 1) * 128],
                        lambda k: (wA if k < 8 else wB)[:, k % 8, :], KD),
                        reads=[r_slot[sa], r_slot[sb], r_mg[i]], writes=[r_bank[b]])
                    if i == NTOK - 1:
                        issue_job()
                        issue_job()
                    em.op("dve", lambda e, b=b, i=i, db=db: e.tensor_tensor(
                        out=x1[:, i, db * 512:(db + 1) * 512], in0=bank(b), in1=x1[:, i, db * 512:(db + 1) * 512], op=ALU.add),
                        reads=[r_bank[b], r_x1[i]], writes=[r_x1[i]])
                    if db == 3:
                        norm_stats(9 + i, x1[:, i, :], [r_x1[i]], 128)
                        n2_step(i - 1)
                        n2_step2(i - 2)
            n2_step(NTOK - 1)
            n2_step2(NTOK - 2)
            n2_step2(NTOK - 1)
            em.op("sp", lambda e: e.dma_start(out=gb, in_=gf_d[0:1, :].partition_broadcast(128)),
                  writes=[R["gb"]], sem=s_g, inc=16)
            if debug:
                em.op("sp", lambda e: e.dma_start(out=dbg["dbg_x1"][:, :], in_=A[:, 0:16384]),
                      reads=r_x1, writes=[r_dbg], sem=s_dbg, inc=16)
                em.op("sp", lambda e: e.dma_start(out=dbg["dbg_h2T"][:, :], in_=arenaB[:, :].bitcast(BF16)),
                      reads=r_mg, writes=[r_dbg], sem=s_dbg, inc=16)

            def final_norm(i):
                if i < 0:
                    return
                par = i % 4
                col = 17 + i
                norm_rstd(col, 128)
                em.op("dve", lambda e: e.scalar_tensor_tensor(
                    out=ost[par], in0=x1[:, i, :], scalar=rstdb[:, col:col + 1], in1=gb, op0=ALU.mult, op1=ALU.mult),
                    reads=[r_x1[i], r_rstd[col], R["gb"]], writes=[r_osth[par][0]] + r_mg)
                em.op("sp", lambda e: e.dma_start(out=out_d[i * 128:(i + 1) * 128, :], in_=ost[par]),
                      reads=[r_osth[par][0]], writes=[r_outs[i]], sem=s_oh[par][0], inc=16)

            state["bank_rr"] = 0
            for gi in range(G):
                for fi in range(FG):
                    s = take_job()
                    wgt = slot(s, 2048).rearrange("p (k c) -> p k c", k=16)
                    wup = wring[:, s * SLOT + 2048:s * SLOT + 4096].rearrange("p (k c) -> p k c", k=16)
                    for blk in range(2):
                        o0 = blk * 512
                        par = blk
                        bg = next_bank()
                        bu_ = next_bank()
                        em.op("pe", lambda e, bg=bg, wgt=wgt, o0=o0: mm_group(
                            e, bank(bg), lambda k: wgt[:, k, :], lambda k: mg[:, k, o0:o0 + 512], KD),
                            reads=[r_slot[s]] + r_mg[4 * blk:4 * blk + 4], writes=[r_bank[bg]])
                        em.op("pe", lambda e, bu_=bu_, wup=wup, o0=o0: mm_group(
                            e, bank(bu_), lambda k: wup[:, k, :], lambda k: mg[:, k, o0:o0 + 512], KD),
                            reads=[r_slot[s]] + r_mg[4 * blk:4 * blk + 4], writes=[r_bank[bu_]])
                        if blk == 1:
                            issue_job()
                        em.op("act", lambda e, bg=bg, par=par: e.activation(out=sg[par], in_=bank(bg), func=AF.Silu),
                              reads=[r_bank[bg]], writes=[r_sg[par]])
                        em.op("dve", lambda e, bu_=bu_, par=par, fi=fi, o0=o0: e.tensor_tensor(
                            out=act[:, fi, o0:o0 + 512], in0=bank(bu_), in1=sg[par], op=ALU.mult),
                            reads=[r_bank[bu_], r_sg[par]], writes=[r_act[blk]])
                def down_group(db, i, s, wd):
                    b = next_bank()
                    em.op("pe", lambda e, b=b, i=i, wd=wd: mm_group(
                        e, bank(b), lambda k: act[:, k, i * 128:(i + 1) * 128], lambda k: wd[:, k, :], FG),
                        reads=[r_slot[s], r_act[i // 4]], writes=[r_bank[b]])
                    return b

                def down_add(db, i, b):
                    em.op("dve", lambda e, b=b, i=i, db=db: e.tensor_tensor(
                        out=x1[:, i, db * 512:(db + 1) * 512], in0=bank(b), in1=x1[:, i, db * 512:(db + 1) * 512],
                        op=ALU.add),
                        reads=[r_bank[b], r_x1[i]], writes=[r_x1[i]])

                last = (gi == G - 1)
                for db in range(2 if last else 4):
                    s = take_job()
                    wd = slot(s, FG * 512).rearrange("p (k c) -> p k c", k=FG)
                    for i in range(NTOK):
                        b = down_group(db, i, s, wd)
                        if i == NTOK - 1:
                            issue_job()
                        down_add(db, i, b)
                if last:
                    s2 = take_job()
                    s3 = take_job()
                    wd2 = slot(s2, FG * 512).rearrange("p (k c) -> p k c", k=FG)
                    wd3 = slot(s3, FG * 512).rearrange("p (k c) -> p k c", k=FG)
                    for i in range(NTOK):
                        b2 = down_group(2, i, s2, wd2)
                        b3 = down_group(3, i, s3, wd3)
                        down_add(2, i, b2)
                        down_add(3, i, b3)
                        norm_stats(17 + i, x1[:, i, :], [r_x1[i]], 128)
                        final_norm(i - 1)

            final_norm(NTOK - 1)
            em.wait_tokens("sp", [t for r_ in r_outs for t in r_.w] + r_dbg.w)

        @block.sync
        def _(e):
            em.begin_pass("sp", e)
            program()

        @block.scalar
        def _(e):
            em.begin_pass("act", e)
            program()

        @block.vector
        def _(e):
            em.begin_pass("dve", e)
            program()

        @block.gpsimd
        def _(e):
            em.begin_pass("pool", e)
            program()

        @block.tensor
        def _(e):
            em.begin_pass("pe", e)
            program()

    return nc


def make_in_maps(inputs):
    f = lambda a: np.ascontiguousarray(np.asarray(a, dtype=np.float32))
    x = f(inputs["x"])
    B, S, _ = x.shape
    shared = {
        "norm1_g": f(inputs["norm1_g"]).reshape(1, D),
        "wpack": pack_weights(inputs),
        "sparams": np.ascontiguousarray(np.concatenate([
            f(inputs["b_gate"]).reshape(32, 128), f(inputs["conv_w"]).reshape(24, 128),
            f(inputs["conv_b"]).reshape(8, 128), f(inputs["pool_scale"]).reshape(16, 128)], axis=0)),
        "norm2_g": f(inputs["norm2_g"]).reshape(1, D),
        "final_g": f(inputs["final_g"]).reshape(1, D),
    }
    in_maps = []
    per_seq = S // T
    for c in range(NCORES):
        b, hf = divmod(c, per_seq)
        s0 = hf * T
        xc = np.zeros((TT, D), np.float32)
        if s0 > 0:
            xc[0:H] = x[b, s0 - H:s0]
        xc[H:] = x[b, s0:s0 + T]
        pos = np.broadcast_to((s0 + 1 + np.arange(16, dtype=np.float32))[None, :], (128, 16))
        m = dict(shared)
        m["x_c"] = xc
        m["pos16"] = np.ascontiguousarray(pos, dtype=np.float32)
        in_maps.append(m)
    return in_maps, (B, S)


def kernel(**inputs):
    in_maps, (B, S) = make_in_maps(inputs)
    nc = build_nc()
    res = run_bass_kernel_spmd(nc, in_maps, core_ids=list(range(NCORES)))
    out = np.concatenate([np.asarray(r["out"]) for r in res.results], axis=0)
    return out.reshape(B, S, D).astype(np.float32)
```

```python
from contextlib import ExitStack

import numpy as np
import concourse.bass as bass
import concourse.mybir as mybir
from concourse.bass_utils import run_bass_kernel_spmd

F32 = mybir.dt.float32
BF16 = mybir.dt.bfloat16
I32 = mybir.dt.int32
AF = mybir.ActivationFunctionType
ALU = mybir.AluOpType

NCORES = 8
D = 2048
KD = 16
T = 1024
H = 16
TT = T + H
NTOK = T // 128
DI = 8192
NF = 44
G = 4
FG = NF // G
SLOT = 6144
NSLOT = 4
EPS = 1e-6
POOL_W = (2, 4, 8, 16)


JOB_SIZES = ([6144] * 8 + [6144, 6144, 4096] + [5376] * 16 + [4096] * 8
             + ([4096] * FG + [FG * 512] * 4) * G)
WTOTAL = sum(JOB_SIZES)


def pack_weights(inputs):
    f = lambda a: np.asarray(a, dtype=np.float32)
    w_in = f(inputs["w_in"]).reshape(D, DI)
    w_a_out = f(inputs["w_a_out"]).reshape(1024, D)
    w_pool = f(inputs["w_pool"]).reshape(1024, 512)
    w_o = f(inputs["w_o"]).reshape(D, D)
    w_gate = f(inputs["w_ffn_gate"]).reshape(D, NF * 128)
    w_up = f(inputs["w_ffn_up"]).reshape(D, NF * 128)
    w_down = f(inputs["w_ffn_down"]).reshape(NF * 128, D)

    def kv(a):
        return a.reshape(a.shape[0] // 128, 128, a.shape[1]).transpose(1, 0, 2)

    blocks = []
    for j in range(8):
        blocks.append(np.stack([kv(w_in[:, wi * 1024 + j * 128:wi * 1024 + (j + 1) * 128]) for wi in range(3)],
                               axis=2).reshape(128, -1))
    for q in range(3):
        n = 3 if q < 2 else 2
        blocks.append(kv(w_in[:, 3072 + q * 384:3072 + q * 384 + n * 128]).reshape(128, -1))
    for j in range(16):
        g = j // 4
        blocks.append(np.concatenate([
            np.stack([kv(w_in[:, 4096 + j * 128:4096 + (j + 1) * 128]),
                      kv(w_in[:, 6144 + j * 128:6144 + (j + 1) * 128])], axis=2).reshape(128, -1),
            kv(w_a_out[:, j * 128:(j + 1) * 128]).reshape(128, -1),
            kv(w_pool[g * 256:(g + 1) * 256, (j % 4) * 128:(j % 4 + 1) * 128]).reshape(128, -1)], axis=1))
    for db in range(4):
        for hh in range(2):
            blocks.append(kv(w_o[hh * 1024:(hh + 1) * 1024, db * 512:(db + 1) * 512]).reshape(128, -1))
    for gi in range(G):
        for fi in range(FG):
            fidx = gi * FG + fi
            blocks.append(np.concatenate([kv(w_gate[:, fidx * 128:(fidx + 1) * 128]).reshape(128, -1),
                                          kv(w_up[:, fidx * 128:(fidx + 1) * 128]).reshape(128, -1)], axis=1))
        for db in range(4):
            blocks.append(kv(w_down[gi * FG * 128:(gi + 1) * FG * 128, db * 512:(db + 1) * 512]).reshape(128, -1))
    assert [b.shape[1] for b in blocks] == JOB_SIZES
    return np.ascontiguousarray(np.concatenate(blocks, axis=1))


class Sem:
    def __init__(self, h):
        self.h = h
        self.count = 0


class Res:
    def __init__(self, name):
        self.name = name
        self.reset()

    def reset(self):
        self.w = []
        self.r = []
        self.pr = []


class Emitter:
    def __init__(self, nc):
        self.nc = nc
        self.sems = []
        self.res = []
        self.cur = None
        self.eng = None
        self.waited = {}
        self.engsem = {}

    def sem(self, h):
        s = Sem(h)
        self.sems.append(s)
        return s

    def resource(self, name):
        r = Res(name)
        self.res.append(r)
        return r

    def begin_pass(self, name, eng):
        self.cur = name
        self.eng = eng
        self.waited = {}
        for s in self.sems:
            s.count = 0
        for r in self.res:
            r.reset()

    def wait_tokens(self, engname, toks):
        if self.cur != engname:
            return
        need = {}
        for (s, v) in toks:
            if need.get(s, 0) < v:
                need[s] = v
        for s, v in need.items():
            if self.waited.get(s, 0) >= v:
                continue
            self.eng.wait_ge(s.h, v)
            self.waited[s] = v

    def op(self, engname, fn, reads=(), writes=(), sem=None, inc=1):
        toks = []
        for r in reads:
            toks += r.w
        for w in writes:
            toks += w.w + w.r + w.pr
        self.wait_tokens(engname, toks)
        ins = fn(self.eng) if self.cur == engname else None
        s = sem if sem is not None else self.engsem[engname]
        s.count += inc
        if ins is not None:
            ins.then_inc(s.h, inc)
        tok = (s, s.count)
        for r in reads:
            r.r.append(tok)
        for w in writes:
            if w.r:
                w.pr = w.r
                w.r = []
                w.w = [tok]
            else:
                w.w.append(tok)
        return tok


def build_nc(debug=False):
    nc = bass.Bass("TRN2", target_bir_lowering=False)

    def din(name, shape, dt=F32):
        return nc.dram_tensor(name, list(shape), dt, kind="ExternalInput").ap()

    x_c = din("x_c", [TT, D])
    pos16 = din("pos16", [128, 16])
    g1_d = din("norm1_g", [1, D])
    wpack = din("wpack", [128, WTOTAL])
    sparams = din("sparams", [80, 128])
    g2_d = din("norm2_g", [1, D])
    gf_d = din("final_g", [1, D])
    out_d = nc.dram_tensor("out", [T, D], F32, kind="ExternalOutput").ap()
    dbg = {}
    if debug:
        for nm, shp, dt in [("dbg_hT", [128, KD * TT], BF16), ("dbg_bu", [128, 8 * T], BF16),
                            ("dbg_pl", [128, 8 * T], BF16), ("dbg_merged", [128, KD * T], BF16),
                            ("dbg_x1", [128, NTOK * D], F32), ("dbg_h2T", [128, KD * T], BF16),
                            ("dbg_prm", [128, 80], F32)]:
            dbg[nm] = nc.dram_tensor(nm, shp, dt, kind="ExternalOutput").ap()

    with ExitStack() as es:
        def sb(name, shape, dt):
            return es.enter_context(nc.sbuf_tensor(name, shape, dt))

        def hsem(name):
            return es.enter_context(nc.semaphore(name))

        arenaA = sb("arenaA", [128, 16512], F32)
        arenaB = sb("arenaB", [128, 8192], F32)
        arenaC = sb("arenaC", [128, 12288], F32)
        wring = sb("wring", [128, NSLOT * SLOT], BF16)
        identf = sb("identf", [128, 128], F32)
        identb = sb("identb", [128, 128], BF16)
        iot = sb("iot", [128, 128], I32)
        prm_rows = sb("prm_rows", [128, 128], F32)
        prm = sb("prm", [128, 80], F32)
        ssb = sb("ssb", [128, 32], F32)
        rsb = sb("rsb", [128, 32], F32)
        rstdb = sb("rstdb", [128, 32], F32)
        pos_sb = sb("pos_sb", [128, 16], F32)
        cnt_sb = sb("cnt_sb", [128, 64], F32)
        inv_sb = sb("inv_sb", [128, 64], F32)
        tmp16 = sb("tmp16", [128, 16], F32)
        junk = sb("junk", [128, 4], F32)
        zeros_bf = sb("zeros_bf", [128, 512], BF16)
        ps = es.enter_context(nc.psum_tensor("ps", [128, 4096], F32))
        h_act = hsem("s_act"); h_dve = hsem("s_dve"); h_pe = hsem("s_pe"); h_pool = hsem("s_pool")
        h_c = hsem("s_c"); h_x0 = hsem("s_x0"); h_x1 = hsem("s_x1"); h_x2 = hsem("s_x2")
        h_w0 = hsem("s_w0"); h_w1 = hsem("s_w1"); h_w2 = hsem("s_w2"); h_w3 = hsem("s_w3")
        h_o0 = hsem("s_o0"); h_o1 = hsem("s_o1"); h_g = hsem("s_g"); h_o2 = hsem("s_o2"); h_o3 = hsem("s_o3")
        h_r0 = hsem("s_r0"); h_r1 = hsem("s_r1"); h_r2 = hsem("s_r2"); h_r3 = hsem("s_r3")
        h_r4 = hsem("s_r4"); h_r5 = hsem("s_r5"); h_r6 = hsem("s_r6"); h_r7 = hsem("s_r7")
        h_dbg = hsem("s_dbg")
        h_p = hsem("s_p")
        block = es.enter_context(nc.Block())
        em = Emitter(nc)
        em.engsem = {"act": em.sem(h_act), "dve": em.sem(h_dve), "pe": em.sem(h_pe),
                     "pool": em.sem(h_pool)}
        s_c = em.sem(h_c)
        s_x = [em.sem(h_x0), em.sem(h_x1), em.sem(h_x2)]
        s_w = [em.sem(h_w0), em.sem(h_w1), em.sem(h_w2), em.sem(h_w3)]
        s_g = em.sem(h_g)
        s_r = [em.sem(h) for h in (h_r0, h_r1, h_r2, h_r3, h_r4, h_r5, h_r6, h_r7)]
        s_oh = [[em.sem(h_)] for h_ in (h_o0, h_o1, h_o2, h_o3)]
        s_dbg = em.sem(h_dbg)
        s_p = em.sem(h_p)

        A = arenaA
        hT = A[:, 0:8320].bitcast(BF16).rearrange("p (k t) -> p k t", t=TT)
        bu = A[:, 8320:12416].bitcast(BF16).rearrange("p (k t) -> p k t", t=T)
        pl = A[:, 12416:16512].bitcast(BF16).rearrange("p (k t) -> p k t", t=T)
        x1 = A[:, 0:16384].rearrange("p (i d) -> p i d", d=D)
        mg = arenaB[:, :].bitcast(BF16).rearrange("p (k t) -> p k t", t=T)
        ost = [arenaB[:, 2048 * q_:2048 * (q_ + 1)] for q_ in range(4)]
        C = arenaC
        gb = C[:, 0:2048]
        hb = [C[:, 2048:3072].bitcast(BF16), C[:, 3072:4096].bitcast(BF16)]
        sq = C[:, 4096:5120].bitcast(BF16)
        FB = 5120
        hb2 = hb + [C[:, FB + 1024 * n_:FB + 1024 * (n_ + 1)].bitcast(BF16) for n_ in range(6)]
        hbc = C[:, FB + 6144:FB + 7168].bitcast(BF16)
        xt = ([C[:, FB:FB + 2048], C[:, FB + 2048:FB + 4096], C[:, FB + 4096:FB + 6144]]
              + [arenaB[:, 2048 * n_:2048 * (n_ + 1)] for n_ in range(4)]
              + [A[:, 8320 + 2048 * n_:8320 + 2048 * (n_ + 1)] for n_ in range(2)])
        R1 = C[:, FB:FB + 1040]
        R2 = C[:, FB + 1040:FB + 2080]
        R3 = C[:, FB + 2080:FB + 3120]
        sga = [C[:, FB + 3120:FB + 3632], C[:, FB + 3632:FB + 4144]]
        sgb = [C[:, FB + 4144:FB + 4656], C[:, FB + 4656:FB + 5168]]
        m1 = C[:, FB + 5168:FB + 5680]
        m2 = C[:, FB + 5680:FB + 6192]
        act = C[:, FB:FB + 5632].bitcast(BF16).rearrange("p (k t) -> p k t", t=T)
        sg = [C[:, FB + 5632:FB + 6144], C[:, FB + 6144:FB + 6656]]

        def slot(s, n=SLOT):
            return wring[:, s * SLOT:s * SLOT + n]

        def bank(b):
            return ps[:, b * 512:(b + 1) * 512]

        def pTb(q):
            return ps[:, q * 1024:(q + 1) * 1024].bitcast(BF16).rearrange("p (k t) -> p k t", t=128)

        R = {}
        for nm in ["zeros", "iot", "identf", "identb", "prm_rows", "prm", "pos", "cnt", "inv", "tmp16",
                   "gb", "sq", "hT", "bu", "pl", "mg", "R1", "R2", "R3", "m1", "m2"]:
            R[nm] = em.resource(nm)
        r_xt = [em.resource("xt%d" % n_) for n_ in range(9)]
        r_hb = [em.resource("hb0"), em.resource("hb1")]
        r_hb2 = r_hb + [em.resource("hb2_%d" % n_) for n_ in range(6)]
        r_hbc = em.resource("hbc")
        r_sga = [em.resource("sga0"), em.resource("sga1")]
        r_sgb = [em.resource("sgb0"), em.resource("sgb1")]
        r_sg = [em.resource("sg0"), em.resource("sg1")]
        r_ost = [em.resource("ost0"), em.resource("ost1")]
        r_osth = [[em.resource("ost%d_%d" % (a_, b_)) for b_ in range(2)] for a_ in range(4)]
        r_bank = [em.resource("bank%d" % b) for b in range(8)]
        r_slot = [em.resource("slot%d" % s) for s in range(NSLOT)]
        r_x1 = [em.resource("x1_%d" % i) for i in range(NTOK)]
        r_hT = [em.resource("hT_a"), em.resource("hT_b")]
        r_mg = [em.resource("mg_%d" % i) for i in range(NTOK)]
        r_act = [em.resource("act0"), em.resource("act1")]
        r_ss = [em.resource("ss%d" % c) for c in range(32)]
        r_rs = [em.resource("rs%d" % c) for c in range(32)]
        r_ss2 = [em.resource("ss2_%d" % c) for c in range(32)]
        r_rstd = [em.resource("rstd%d" % c) for c in range(32)]
        r_outs = [em.resource("out%d" % i) for i in range(NTOK)]
        r_dbg = em.resource("dbg")

        NJOBS = len(JOB_SIZES)
        job_off = [0]
        for n_ in JOB_SIZES:
            job_off.append(job_off[-1] + n_)

        def program():
            state = {"next_job": 0, "bank_rr": 0}

            def issue_job():
                n = state["next_job"]
                if n >= NJOBS:
                    return
                s = n % NSLOT
                sz = JOB_SIZES[n]
                off = job_off[n]
                em.op("pool", lambda e, s=s, sz=sz, off=off: e.dma_start(out=slot(s, sz), in_=wpack[:, off:off + sz]),
                      writes=[r_slot[s]], sem=s_w[s], inc=16)
                state["next_job"] = n + 1

            job_ctr = {"n": 0}

            def take_job():
                n = job_ctr["n"]
                job_ctr["n"] = n + 1
                return n % NSLOT

            def next_bank():
                b = state["bank_rr"]
                state["bank_rr"] = (b + 1) % 8
                return b

            em.op("pool", lambda e: e.iota(iot[:, :], [[1, 128]], base=0, channel_multiplier=-1),
                  writes=[R["iot"]])
            em.op("dve", lambda e: e.tensor_single_scalar(out=identf[:, :], in_=iot[:, :], scalar=0.0,
                                                           op=ALU.is_equal),
                  reads=[R["iot"]], writes=[R["identf"]])
            em.op("dve", lambda e: e.tensor_copy(out=identb[:, :], in_=identf[:, :]),
                  reads=[R["identf"]], writes=[R["identb"]])
            em.op("dve", lambda e: e.memset(zeros_bf[:, :], 0.0), writes=[R["zeros"]])

            def warm(n):
                def f(e):
                    ins = None
                    for _ in range(n):
                        ins = e.matmul(bank(7), identb[:, :], zeros_bf[:, :], start=True, stop=True)
                    return ins
                em.op("pe", f, reads=[R["identb"], R["zeros"]], writes=[r_bank[7]])

            def load_small_params():
                em.op("sp", lambda e: e.dma_start(out=prm_rows[0:80, :], in_=sparams[:, :]),
                      writes=[R["prm_rows"]], sem=s_c, inc=16)
                em.op("sp", lambda e: e.dma_start(out=pos_sb[:, :], in_=pos16[:, :]),
                      writes=[R["pos"]], sem=s_p, inc=16)

            def small_param_compute():
                em.op("pe", lambda e: e.transpose(bank(7)[:, 0:80], prm_rows[0:80, :], identf[0:80, 0:80]),
                      reads=[R["prm_rows"], R["identf"]], writes=[r_bank[7]])
                em.op("act", lambda e: e.copy(out=prm[:, :], in_=bank(7)[:, 0:80]),
                      reads=[r_bank[7]], writes=[R["prm"]])

            def pool_count_compute():
                for g, w in enumerate(POOL_W):
                    em.op("dve", lambda e, g=g, w=w: e.tensor_scalar(out=cnt_sb[:, g * 16:(g + 1) * 16], in0=pos_sb[:, :],
                                                                     scalar1=float(w), scalar2=None, op0=ALU.min),
                          reads=[R["pos"]], writes=[R["cnt"]])
                em.op("dve", lambda e: e.reciprocal(out=inv_sb[:, :], in_=cnt_sb[:, :]),
                      reads=[R["cnt"]], writes=[R["inv"]])

            def norm_stats(col, src_ap, src_res, npart):
                em.op("act", lambda e: e.activation(out=sq[0:npart, :], in_=src_ap, func=AF.Square,
                                                    accum_out=ssb[0:npart, col:col + 1]),
                      reads=src_res, writes=[R["sq"], r_ss[col]])
                em.op("act", lambda e: e.copy(out=junk[0:npart, 0:1], in_=ssb[0:npart, col:col + 1]),
                      reads=[r_ss[col]], writes=[r_ss2[col]])
                em.op("act", lambda e: e.activation(out=rsb[0:npart, col:col + 1], in_=ssb[0:npart, col:col + 1],
                                                    func=AF.Sqrt, scale=1.0 / D, bias=EPS),
                      reads=[r_ss2[col]], writes=[r_rs[col]])

            def norm_rstd(col, npart):
                em.op("dve", lambda e: e.reciprocal(out=rstdb[0:npart, col:col + 1], in_=rsb[0:npart, col:col + 1]),
                      reads=[r_rs[col]], writes=[r_rstd[col]])

            def norm_to_T(col, src_ap, src_res, npart, hbuf, hres, q, dstT, dst_res, c0, evac):
                norm_rstd(col, npart)
                em.op("dve", lambda e: e.scalar_tensor_tensor(out=hbuf[0:npart, :], in0=src_ap,
                                                              scalar=rstdb[0:npart, col:col + 1], in1=gb[0:npart, :],
                                                              op0=ALU.mult, op1=ALU.mult),
                      reads=src_res + [r_rstd[col], R["gb"]], writes=[hres])

                def stage2():
                    pt = pTb(q)

                    def tr(e):
                        ins = None
                        for k in range(KD):
                            ins = e.transpose(pt[:, k, 0:npart], hbuf[0:npart, k * 128:(k + 1) * 128],
                                              identb[0:npart, 0:npart])
                        return ins
                    em.op("pe", tr, reads=[hres, R["identb"]], writes=[r_bank[2 * q], r_bank[2 * q + 1]])
                    if evac == "act":
                        em.op("act", lambda e: e.copy(out=dstT[:, :, c0:c0 + npart], in_=pt[:, :, 0:npart]),
                              reads=[r_bank[2 * q], r_bank[2 * q + 1]], writes=[dst_res])
                    else:
                        em.op("dve", lambda e: e.tensor_copy(out=dstT[:, :, c0:c0 + npart], in_=pt[:, :, 0:npart]),
                              reads=[r_bank[2 * q], r_bank[2 * q + 1]], writes=[dst_res])
                return stage2

            def mm_group(e, out_ap, lhs_fn, rhs_fn, nk):
                ins = None
                for k in range(nk):
                    ins = e.matmul(out_ap, lhs_fn(k), rhs_fn(k), start=(k == 0), stop=(k == nk - 1))
                return ins

            PW = lambda c: prm[:, c:c + 1]

            xsem = s_x + s_r[0:6]
            for ti in range(NTOK + 1):
                npart = H if ti == 0 else 128
                r0 = 0 if ti == 0 else H + (ti - 1) * 128
                em.op("sp", lambda e, ti=ti, npart=npart, r0=r0: e.dma_start(out=xt[ti][0:npart, :], in_=x_c[r0:r0 + npart, :]),
                      writes=[r_xt[ti]], sem=xsem[ti], inc=16)
                if ti == 0:
                    em.op("sp", lambda e: e.dma_start(out=gb, in_=g1_d[0:1, :].partition_broadcast(128)),
                          writes=[R["gb"]], sem=s_g, inc=16)
                if ti == 4:
                    load_small_params()
            em.wait_tokens("pool", [t for r_ in r_xt[0:3] for t in r_.w])
            issue_job()
            em.wait_tokens("pool", [t for r_ in r_xt for t in r_.w])
            for _ in range(NSLOT - 1):
                issue_job()

            def phase1_tiles(tiles, qfn, hbufs, lag, nwarm=0, drain=True):
                pend = []
                for idx, ti in enumerate(tiles):
                    npart = H if ti == 0 else 128
                    r0 = 0 if ti == 0 else H + (ti - 1) * 128
                    hbuf, hres = hbufs[idx % len(hbufs)]
                    norm_stats(ti, xt[ti][0:npart, :], [r_xt[ti]], npart)
                    pend.append(norm_to_T(ti, xt[ti][0:npart, :], [r_xt[ti]], npart, hbuf, hres, qfn(ti),
                                          hT, r_hT[0 if ti <= 4 else 1], r0, "act" if ti % 2 == 1 else "dve"))
                    if len(pend) > lag:
                        pend.pop(0)()
                        if nwarm:
                            warm(nwarm)
                if not drain:
                    return pend
                for st2 in pend:
                    st2()
                    if nwarm:
                        warm(nwarm)
                return []

            def mixA_setup(j):
                s = take_job()
                wv = slot(s).rearrange("p (k w c) -> p k w c", k=16, w=3)
                bH = 3 if j == 0 else 6 + (j % 2)
                return s, wv, bH

            def mixA_pe_halo(j, s, wv, bH):
                def halo(e):
                    mm_group(e, bank(bH)[:, 0:16], lambda k: wv[:, k, 1, :], lambda k: hT[:, k, 0:16], KD)
                    return mm_group(e, bank(bH)[:, 16:32], lambda k: wv[:, k, 2, :], lambda k: hT[:, k, 0:16], KD)
                em.op("pe", halo, reads=[r_slot[s], r_hT[0]], writes=[r_bank[bH]])

            def mixA_ew_halo(j, bH):
                em.op("act", lambda e: e.copy(out=R1[:, 0:16], in_=bank(bH)[:, 0:16]),
                      reads=[r_bank[bH]], writes=[R["R1"]])
                em.op("dve", lambda e: e.tensor_tensor(out=R2[:, 0:16], in0=bank(bH)[:, 16:32], in1=R1[:, 0:16],
                                                       op=ALU.mult),
                      reads=[r_bank[bH], R["R1"]], writes=[R["R2"]])

            def mixA_pe_blk(j, blk, s, wv, which=(1, 2, 0)):
                c0 = H + blk * 512
                bs = 3 * ((2 * j + blk) % 2)
                for (wi, bb) in [(1, bs), (2, bs + 1), (0, bs + 2)]:
                    if wi not in which:
                        continue
                    em.op("pe", lambda e, wi=wi, bb=bb: mm_group(
                        e, bank(bb), lambda k: wv[:, k, wi, :], lambda k: hT[:, k, c0:c0 + 512], KD),
                        reads=[r_slot[s], r_hT[blk]], writes=[r_bank[bb]])

            def mixA_ew_blk(j, blk):
                c0 = H + blk * 512
                o0 = blk * 512
                bs = 3 * ((2 * j + blk) % 2)
                em.op("act", lambda e: e.copy(out=R1[:, c0:c0 + 512], in_=bank(bs)),
                      reads=[r_bank[bs]], writes=[R["R1"]])
                em.op("dve", lambda e: e.tensor_tensor(out=R2[:, c0:c0 + 512], in0=bank(bs + 1),
                                                       in1=R1[:, c0:c0 + 512], op=ALU.mult),
                      reads=[r_bank[bs + 1], R["R1"]], writes=[R["R2"]])
                em.op("act", lambda e: e.activation(out=R3[:, o0:o0 + 512], in_=R2[:, c0:c0 + 512],
                                                    func=AF.Identity, scale=PW(48 + j), bias=PW(56 + j)),
                      reads=[R["R2"], R["prm"]], writes=[R["R3"]])
                for (sh, pc) in [(1, 40 + j), (2, 32 + j)]:
                    em.op("dve", lambda e, sh=sh, pc=pc: e.scalar_tensor_tensor(
                        out=R3[:, o0:o0 + 512], in0=R2[:, c0 - sh:c0 - sh + 512], scalar=PW(pc),
                        in1=R3[:, o0:o0 + 512], op0=ALU.mult, op1=ALU.add),
                        reads=[R["R2"], R["R3"], R["prm"]], writes=[R["R3"]])
                em.op("dve", lambda e: e.tensor_tensor(out=bu[:, j, o0:o0 + 512], in0=bank(bs + 2),
                                                       in1=R3[:, o0:o0 + 512], op=ALU.mult),
                      reads=[r_bank[bs + 2], R["R3"]], writes=[R["bu"]])

            warm(24)
            phase1_tiles(range(0, 5), lambda ti: ti % 3, [(hb[0], r_hb[0]), (hb[1], r_hb[1]), (hbc, r_hbc)], 2, nwarm=4)
            small_param_compute()
            st2s = phase1_tiles(range(5, NTOK + 1), lambda ti: 2 + (ti % 2),
                                [(hb2[n_], r_hb2[n_]) for n_ in range(2, 6)], 4, drain=False)
            s, wv, bH = mixA_setup(0)
            mixA_pe_halo(0, s, wv, bH)
            mixA_pe_blk(0, 0, s, wv, which=(1, 2))
            st2s[0]()
            st2s[1]()
            mixA_pe_blk(0, 0, s, wv, which=(0,))
            st2s[2]()
            st2s[3]()
            em.op("sp", lambda e: e.dma_start(out=gb, in_=g2_d[0:1, :].partition_broadcast(128)),
                  writes=[R["gb"]], sem=s_g, inc=16)
            if debug:
                em.op("sp", lambda e: e.dma_start(out=dbg["dbg_prm"][:, :], in_=prm[:, :]),
                      reads=[R["prm"]], writes=[r_dbg], sem=s_dbg, inc=16)
                em.op("sp", lambda e: e.dma_start(out=dbg["dbg_hT"][:, :], in_=A[:, 0:8320].bitcast(BF16)),
                      reads=r_hT, writes=[r_dbg], sem=s_dbg, inc=16)
            ov = [t for n_ in range(2, 6) for t in r_hb2[n_].r + r_hb2[n_].w] + [t for t in r_hbc.r + r_hbc.w]
            em.wait_tokens("act", ov)
            em.wait_tokens("dve", ov)
            mixA_ew_halo(0, bH)
            mixA_ew_blk(0, 0)
            mixA_pe_blk(0, 1, s, wv)
            issue_job()
            mixA_ew_blk(0, 1)
            for j in range(1, 8):
                s, wv, bH = mixA_setup(j)
                mixA_pe_halo(j, s, wv, bH)
                mixA_ew_halo(j, bH)
                mixA_pe_blk(j, 0, s, wv)
                mixA_ew_blk(j, 0)
                mixA_pe_blk(j, 1, s, wv)
                issue_job()
                mixA_ew_blk(j, 1)

            pool_count_compute()
            for j in range(8):
                if j % 3 == 0:
                    s = take_job()
                n = 3 if j < 6 else 2
                wv = slot(s, 16 * n * 128).rearrange("p (k c) -> p k c", k=16)
                cw = (j % 3) * 128
                g = j // 2
                bH = 6 + (j % 2)
                em.op("pe", lambda e, wv=wv, cw=cw, bH=bH: mm_group(
                    e, bank(bH)[:, 0:16], lambda k: wv[:, k, cw:cw + 128], lambda k: hT[:, k, 0:16], KD),
                    reads=[r_slot[s], r_hT[0]], writes=[r_bank[bH]])
                em.op("act", lambda e, bH=bH: e.copy(out=R1[:, 0:16], in_=bank(bH)[:, 0:16]),
                      reads=[r_bank[bH]], writes=[R["R1"]])
                for blk in range(2):
                    c0 = H + blk * 512
                    bb = 3 * (j % 2) + blk
                    em.op("pe", lambda e, wv=wv, cw=cw, bb=bb, c0=c0: mm_group(
                        e, bank(bb), lambda k: wv[:, k, cw:cw + 128], lambda k: hT[:, k, c0:c0 + 512], KD),
                        reads=[r_slot[s], r_hT[blk]], writes=[r_bank[bb]])
                    if blk == 1 and (j % 3 == 2 or j == 7):
                        issue_job()
                    em.op("act", lambda e, bb=bb, c0=c0: e.copy(out=R1[:, c0:c0 + 512], in_=bank(bb)),
                          reads=[r_bank[bb]], writes=[R["R1"]])
                bufs = [(R2, R["R2"]), (R3, R["R3"])]
                cur, cur_res = R1, R["R1"]
                sh = 1
                for step in range(g + 1):
                    dst, dst_res = bufs[step % 2]
                    lo = 2 * sh - 1
                    em.op("dve", lambda e, dst=dst, cur=cur, lo=lo, sh=sh: e.tensor_tensor(
                        out=dst[:, lo:TT], in0=cur[:, lo:TT], in1=cur[:, lo - sh:TT - sh], op=ALU.add),
                        reads=[cur_res], writes=[dst_res])
                    cur, cur_res = dst, dst_res
                    sh *= 2
                w = POOL_W[g]
                em.op("dve", lambda e, cur=cur, w=w, j=j: e.scalar_tensor_tensor(
                    out=pl[:, j, :], in0=cur[:, H:TT], scalar=1.0 / w, in1=R1[:, H:TT],
                    op0=ALU.mult, op1=ALU.subtract),
                    reads=[cur_res, R["R1"]], writes=[R["pl"]])
                em.op("dve", lambda e, cur=cur, g=g: e.tensor_tensor(out=tmp16[:, :], in0=cur[:, H:H + 16],
                                                                     in1=inv_sb[:, g * 16:(g + 1) * 16], op=ALU.mult),
                      reads=[cur_res, R["inv"]], writes=[R["tmp16"]])
                em.op("dve", lambda e, j=j: e.tensor_tensor(out=pl[:, j, 0:16], in0=tmp16[:, :], in1=R1[:, H:H + 16],
                                                            op=ALU.subtract),
                      reads=[R["tmp16"], R["R1"]], writes=[R["pl"]])

            for j in range(16):
                s = take_job()
                wg_ = slot(s, 4096).rearrange("p (k w c) -> p k w c", k=16, w=2)
                wa = wring[:, s * SLOT + 4096:s * SLOT + 5120].rearrange("p (k c) -> p k c", k=8)
                wp = wring[:, s * SLOT + 5120:s * SLOT + 5376].rearrange("p (k c) -> p k c", k=2)
                g = j // 4
                for blk in range(2):
                    c0 = H + blk * 512
                    o0 = blk * 512
                    par = (2 * j + blk) % 2
                    b0 = 4 * par
                    em.op("pe", lambda e, b0=b0, wg_=wg_, c0=c0: mm_group(
                        e, bank(b0), lambda k: wg_[:, k, 0, :], lambda k: hT[:, k, c0:c0 + 512], KD),
                        reads=[r_slot[s], r_hT[blk]], writes=[r_bank[b0]])
                    em.op("pe", lambda e, b0=b0, wa=wa, o0=o0: mm_group(
                        e, bank(b0 + 1), lambda k: wa[:, k, :], lambda k: bu[:, k, o0:o0 + 512], 8),
                        reads=[r_slot[s], R["bu"]], writes=[r_bank[b0 + 1]])
                    em.op("pe", lambda e, b0=b0, wg_=wg_, c0=c0: mm_group(
                        e, bank(b0 + 2), lambda k: wg_[:, k, 1, :], lambda k: hT[:, k, c0:c0 + 512], KD),
                        reads=[r_slot[s], r_hT[blk]], writes=[r_bank[b0 + 2]])
                    em.op("pe", lambda e, b0=b0, wp=wp, o0=o0, g=g: mm_group(
                        e, bank(b0 + 3), lambda k: wp[:, k, :], lambda k: pl[:, 2 * g + k, o0:o0 + 512], 2),
                        reads=[r_slot[s], R["pl"]], writes=[r_bank[b0 + 3]])
                    if blk == 1:
                        issue_job()
                    em.op("act", lambda e, b0=b0, par=par, j=j: e.activation(out=sga[par], in_=bank(b0), func=AF.Sigmoid,
                                                                             bias=PW(j)),
                          reads=[r_bank[b0], R["prm"]], writes=[r_sga[par]])
                    em.op("act", lambda e, b0=b0, par=par, j=j: e.activation(out=sgb[par], in_=bank(b0 + 2), func=AF.Sigmoid,
                                                                             bias=PW(16 + j)),
                          reads=[r_bank[b0 + 2], R["prm"]], writes=[r_sgb[par]])
                    em.op("dve", lambda e, b0=b0, par=par: e.tensor_tensor(out=m1, in0=bank(b0 + 1), in1=sga[par], op=ALU.mult),
                          reads=[r_bank[b0 + 1], r_sga[par]], writes=[R["m1"]])
                    em.op("dve", lambda e, b0=b0, par=par, j=j: e.scalar_tensor_tensor(
                        out=m2, in0=bank(b0 + 3), scalar=PW(64 + j), in1=sgb[par], op0=ALU.mult, op1=ALU.mult),
                        reads=[r_bank[b0 + 3], r_sgb[par], R["prm"]], writes=[R["m2"]])
                    em.op("dve", lambda e, j=j, o0=o0: e.tensor_tensor(out=mg[:, j, o0:o0 + 512], in0=m1, in1=m2, op=ALU.add),
                          reads=[R["m1"], R["m2"]], writes=r_mg[4 * blk:4 * blk + 4])

            if debug:
                em.op("sp", lambda e: e.dma_start(out=dbg["dbg_bu"][:, :], in_=A[:, 8320:12416].bitcast(BF16)),
                      reads=[R["bu"]], writes=[r_dbg], sem=s_dbg, inc=16)
                em.op("sp", lambda e: e.dma_start(out=dbg["dbg_pl"][:, :], in_=A[:, 12416:16512].bitcast(BF16)),
                      reads=[R["pl"]], writes=[r_dbg], sem=s_dbg, inc=16)
                em.op("sp", lambda e: e.dma_start(out=dbg["dbg_merged"][:, :], in_=arenaB[:, :].bitcast(BF16)),
                      reads=r_mg, writes=[r_dbg], sem=s_dbg, inc=16)

            ovl = [r_hT[0], r_hT[1], R["bu"], R["pl"]] + ([r_dbg] if debug else [])
            em.wait_tokens("sp", [t for r_ in ovl for t in r_.w + r_.r + r_.pr])
            for i in range(NTOK):
                em.op("sp", lambda e, i=i: e.dma_start(out=x1[:, i, :], in_=x_c[H + i * 128:H + (i + 1) * 128, :]),
                      writes=[r_x1[i]], sem=s_r[i], inc=16)
            n2_stage2 = {}

            def n2_step(i):
                if i < 0:
                    return
                n2_stage2[i] = norm_to_T(9 + i, x1[:, i, :], [r_x1[i]], 128, hb2[i], r_hb2[i], 2 + (i % 2),
                                         mg, r_mg[i], i * 128, "act" if i % 2 == 0 else "dve")

            def n2_step2(i):
                if i < 0:
                    return
                n2_stage2[i]()

            for db in range(4):
                sa = take_job()
                sb = take_job()
                wA = slot(sa, 4096).rearrange("p (k c) -> p k c", k=8)
                wB = slot(sb, 4096).rearrange("p (k c) -> p k c", k=8)
                for i in range(NTOK):
                    b = next_bank() if db < 3 else i % 4
                    em.op("pe", lambda e, b=b, i=i, wA=wA, wB=wB: mm_group(
                        e, bank(b), lambda k: mg[:, k, i * 128:(i + 1) * 128],
                        lambda k: (wA if k < 8 else wB)[:, k % 8, :], KD),
                        reads=[r_slot[sa], r_slot[sb], r_mg[i]], writes=[r_bank[b]])
                    if i == NTOK - 1:
                        issue_job()
                        issue_job()
                    em.op("dve", lambda e, b=b, i=i, db=db: e.tensor_tensor(
                        out=x1[:, i, db * 512:(db + 1) * 512], in0=bank(b), in1=x1[:, i, db * 512:(db + 1) * 512], op=ALU.add),
                        reads=[r_bank[b], r_x1[i]], writes=[r_x1[i]])
                    if db == 3:
                        norm_stats(9 + i, x1[:, i, :], [r_x1[i]], 128)
                        n2_step(i - 1)
                        n2_step2(i - 2)
            n2_step(NTOK - 1)

            def ffn_views(s):
                wgt = slot(s, 2048).rearrange("p (k c) -> p k c", k=16)
                wup = wring[:, s * SLOT + 2048:s * SLOT + 4096].rearrange("p (k c) -> p k c", k=16)
                return wgt, wup

            def ffn_pe(blk, s, wgt, wup):
                o0 = blk * 512
                bg = next_bank()
                bu_ = next_bank()
                em.op("pe", lambda e: mm_group(
                    e, bank(bg), lambda k: wgt[:, k, :], lambda k: mg[:, k, o0:o0 + 512], KD),
                    reads=[r_slot[s]] + r_mg[4 * blk:4 * blk + 4], writes=[r_bank[bg]])
                em.op("pe", lambda e: mm_group(
                    e, bank(bu_), lambda k: wup[:, k, :], lambda k: mg[:, k, o0:o0 + 512], KD),
                    reads=[r_slot[s]] + r_mg[4 * blk:4 * blk + 4], writes=[r_bank[bu_]])
                return bg, bu_

            def ffn_ew(fi, blk, bg, bu_):
                o0 = blk * 512
                par = blk
                em.op("act", lambda e: e.activation(out=sg[par], in_=bank(bg), func=AF.Silu),
                      reads=[r_bank[bg]], writes=[r_sg[par]])
                em.op("dve", lambda e: e.tensor_tensor(
                    out=act[:, fi, o0:o0 + 512], in0=bank(bu_), in1=sg[par], op=ALU.mult),
                    reads=[r_bank[bu_], r_sg[par]], writes=[r_act[blk]])

            state["bank_rr"] = 0
            s_f0 = take_job()
            wgt_f0, wup_f0 = ffn_views(s_f0)
            bg_f0, bu_f0 = ffn_pe(0, s_f0, wgt_f0, wup_f0)
            n2_step2(NTOK - 2)
            n2_step2(NTOK - 1)
            ov2 = [t for r_ in r_hb2 for t in r_.r + r_.w]
            em.wait_tokens("act", ov2)
            em.wait_tokens("dve", ov2)
            ffn_ew(0, 0, bg_f0, bu_f0)
            em.op("sp", lambda e: e.dma_start(out=gb, in_=gf_d[0:1, :].partition_broadcast(128)),
                  writes=[R["gb"]], sem=s_g, inc=16)
            if debug:
                em.op("sp", lambda e: e.dma_start(out=dbg["dbg_x1"][:, :], in_=A[:, 0:16384]),
                      reads=r_x1, writes=[r_dbg], sem=s_dbg, inc=16)
                em.op("sp", lambda e: e.dma_start(out=dbg["dbg_h2T"][:, :], in_=arenaB[:, :].bitcast(BF16)),
                      reads=r_mg, writes=[r_dbg], sem=s_dbg, inc=16)

            def final_norm(i):
                if i < 0:
                    return
                par = i % 4
                col = 17 + i
                norm_rstd(col, 128)
                em.op("dve", lambda e: e.scalar_tensor_tensor(
                    out=ost[par], in0=x1[:, i, :], scalar=rstdb[:, col:col + 1], in1=gb, op0=ALU.mult, op1=ALU.mult),
                    reads=[r_x1[i], r_rstd[col], R["gb"]], writes=[r_osth[par][0]] + r_mg)
                em.op("sp", lambda e: e.dma_start(out=out_d[i * 128:(i + 1) * 128, :], in_=ost[par]),
                      reads=[r_osth[par][0]], writes=[r_outs[i]], sem=s_oh[par][0], inc=16)

            for gi in range(G):
                for fi in range(FG):
                    first = (gi == 0 and fi == 0)
                    if first:
                        s, wgt, wup = s_f0, wgt_f0, wup_f0
                    else:
                        s = take_job()
                        wgt, wup = ffn_views(s)
                    for blk in range(2):
                        if first and blk == 0:
                            continue
                        bg, bu_ = ffn_pe(blk, s, wgt, wup)
                        if blk == 1:
                            issue_job()
                        ffn_ew(fi, blk, bg, bu_)
                def down_group(db, i, s, wd):
                    b = next_bank()
                    em.op("pe", lambda e, b=b, i=i, wd=wd: mm_group(
                        e, bank(b), lambda k: act[:, k, i * 128:(i + 1) * 128], lambda k: wd[:, k, :], FG),
                        reads=[r_slot[s], r_act[i // 4]], writes=[r_bank[b]])
                    return b

                def down_add(db, i, b):
                    em.op("dve", lambda e, b=b, i=i, db=db: e.tensor_tensor(
                        out=x1[:, i, db * 512:(db + 1) * 512], in0=bank(b), in1=x1[:, i, db * 512:(db + 1) * 512],
                        op=ALU.add),
                        reads=[r_bank[b], r_x1[i]], writes=[r_x1[i]])

                last = (gi == G - 1)
                for db in range(2 if last else 4):
                    s = take_job()
                    wd = slot(s, FG * 512).rearrange("p (k c) -> p k c", k=FG)
                    for i in range(NTOK):
                        b = down_group(db, i, s, wd)
                        if i == NTOK - 1:
                            issue_job()
                        down_add(db, i, b)
                if last:
                    s2 = take_job()
                    s3 = take_job()
                    wd2 = slot(s2, FG * 512).rearrange("p (k c) -> p k c", k=FG)
                    wd3 = slot(s3, FG * 512).rearrange("p (k c) -> p k c", k=FG)
                    for i in range(NTOK):
                        b2 = down_group(2, i, s2, wd2)
                        b3 = down_group(3, i, s3, wd3)
                        down_add(2, i, b2)
                        down_add(3, i, b3)
                        norm_stats(17 + i, x1[:, i, :], [r_x1[i]], 128)
                        final_norm(i - 1)

            final_norm(NTOK - 1)
            em.wait_tokens("sp", [t for r_ in r_outs for t in r_.w] + r_dbg.w)

        @block.sync
        def _(e):
            em.begin_pass("sp", e)
            program()

        @block.scalar
        def _(e):
            em.begin_pass("act", e)
            program()

        @block.vector
        def _(e):
            em.begin_pass("dve", e)
            program()

        @block.gpsimd
        def _(e):
            em.begin_pass("pool", e)
            program()

        @block.tensor
        def _(e):
            em.begin_pass("pe", e)
            program()

    return nc


def make_in_maps(inputs):
    f = lambda a: np.ascontiguousarray(np.asarray(a, dtype=np.float32))
    x = f(inputs["x"])
    B, S, _ = x.shape
    shared = {
        "norm1_g": f(inputs["norm1_g"]).reshape(1, D),
        "wpack": pack_weights(inputs),
        "sparams": np.ascontiguousarray(np.concatenate([
            f(inputs["b_gate"]).reshape(32, 128), f(inputs["conv_w"]).reshape(24, 128),
            f(inputs["conv_b"]).reshape(8, 128), f(inputs["pool_scale"]).reshape(16, 128)], axis=0)),
        "norm2_g": f(inputs["norm2_g"]).reshape(1, D),
        "final_g": f(inputs["final_g"]).reshape(1, D),
    }
    in_maps = []
    per_seq = S // T
    for c in range(NCORES):
        b, hf = divmod(c, per_seq)
        s0 = hf * T
        xc = np.zeros((TT, D), np.float32)
        if s0 > 0:
            xc[0:H] = x[b, s0 - H:s0]
        xc[H:] = x[b, s0:s0 + T]
        pos = np.broadcast_to((s0 + 1 + np.arange(16, dtype=np.float32))[None, :], (128, 16))
        m = dict(shared)
        m["x_c"] = xc
        m["pos16"] = np.ascontiguousarray(pos, dtype=np.float32)
        in_maps.append(m)
    return in_maps, (B, S)


def kernel(**inputs):
    in_maps, (B, S) = make_in_maps(inputs)
    nc = build_nc()
    res = run_bass_kernel_spmd(nc, in_maps, core_ids=list(range(NCORES)))
    out = np.concatenate([np.asarray(r["out"]) for r in res.results], axis=0)
    return out.reshape(B, S, D).astype(np.float32)
```

```python
from contextlib import ExitStack

import numpy as np
import concourse.bass as bass
import concourse.mybir as mybir
from concourse.bass_utils import run_bass_kernel_spmd

F32 = mybir.dt.float32
BF16 = mybir.dt.bfloat16
I32 = mybir.dt.int32
AF = mybir.ActivationFunctionType
ALU = mybir.AluOpType

NCORES = 8
D = 2048
KD = 16
T = 1024
H = 16
TT = T + H
NTOK = T // 128
DI = 8192
NF = 44
G = 4
FG = NF // G
SLOT = 6144
NSLOT = 4
EPS = 1e-6
POOL_W = (2, 4, 8, 16)


JOB_SIZES = ([6144] * 8 + [6144, 6144, 4096] + [5376] * 16 + [4096] * 8
             + ([4096] * FG + [FG * 512] * 4) * G)
WTOTAL = sum(JOB_SIZES)


def pack_weights(inputs):
    f = lambda a: np.asarray(a, dtype=np.float32)
    w_in = f(inputs["w_in"]).reshape(D, DI)
    w_a_out = f(inputs["w_a_out"]).reshape(1024, D)
    w_pool = f(inputs["w_pool"]).reshape(1024, 512)
    w_o = f(inputs["w_o"]).reshape(D, D)
    w_gate = f(inputs["w_ffn_gate"]).reshape(D, NF * 128)
    w_up = f(inputs["w_ffn_up"]).reshape(D, NF * 128)
    w_down = f(inputs["w_ffn_down"]).reshape(NF * 128, D)

    def kv(a):
        return a.reshape(a.shape[0] // 128, 128, a.shape[1]).transpose(1, 0, 2)

    blocks = []
    for j in range(8):
        blocks.append(np.stack([kv(w_in[:, wi * 1024 + j * 128:wi * 1024 + (j + 1) * 128]) for wi in range(3)],
                               axis=1).reshape(128, -1))
    for q in range(3):
        n = 3 if q < 2 else 2
        blocks.append(kv(w_in[:, 3072 + q * 384:3072 + q * 384 + n * 128]).reshape(128, -1))
    for j in range(16):
        g = j // 4
        blocks.append(np.concatenate([
            np.stack([kv(w_in[:, 4096 + j * 128:4096 + (j + 1) * 128]),
                      kv(w_in[:, 6144 + j * 128:6144 + (j + 1) * 128])], axis=2).reshape(128, -1),
            kv(w_a_out[:, j * 128:(j + 1) * 128]).reshape(128, -1),
            kv(w_pool[g * 256:(g + 1) * 256, (j % 4) * 128:(j % 4 + 1) * 128]).reshape(128, -1)], axis=1))
    for db in range(4):
        for hh in range(2):
            blocks.append(kv(w_o[hh * 1024:(hh + 1) * 1024, db * 512:(db + 1) * 512]).reshape(128, -1))
    for gi in range(G):
        for fi in range(FG):
            fidx = gi * FG + fi
            blocks.append(np.concatenate([kv(w_gate[:, fidx * 128:(fidx + 1) * 128]).reshape(128, -1),
                                          kv(w_up[:, fidx * 128:(fidx + 1) * 128]).reshape(128, -1)], axis=1))
        for db in range(4):
            blocks.append(kv(w_down[gi * FG * 128:(gi + 1) * FG * 128, db * 512:(db + 1) * 512]).reshape(128, -1))
    assert [b.shape[1] for b in blocks] == JOB_SIZES
    return np.ascontiguousarray(np.concatenate(blocks, axis=1))


class Sem:
    def __init__(self, h):
        self.h = h
        self.count = 0


class Res:
    def __init__(self, name):
        self.name = name
        self.reset()

    def reset(self):
        self.w = []
        self.r = []
        self.pr = []


class Emitter:
    def __init__(self, nc):
        self.nc = nc
        self.sems = []
        self.res = []
        self.cur = None
        self.eng = None
        self.waited = {}
        self.engsem = {}

    def sem(self, h):
        s = Sem(h)
        self.sems.append(s)
        return s

    def resource(self, name):
        r = Res(name)
        self.res.append(r)
        return r

    def begin_pass(self, name, eng):
        self.cur = name
        self.eng = eng
        self.waited = {}
        for s in self.sems:
            s.count = 0
        for r in self.res:
            r.reset()

    def wait_tokens(self, engname, toks):
        if self.cur != engname:
            return
        need = {}
        for (s, v) in toks:
            if need.get(s, 0) < v:
                need[s] = v
        for s, v in need.items():
            if self.waited.get(s, 0) >= v:
                continue
            self.eng.wait_ge(s.h, v)
            self.waited[s] = v

    def op(self, engname, fn, reads=(), writes=(), sem=None, inc=1):
        toks = []
        for r in reads:
            toks += r.w
        for w in writes:
            toks += w.w + w.r + w.pr
        self.wait_tokens(engname, toks)
        ins = fn(self.eng) if self.cur == engname else None
        s = sem if sem is not None else self.engsem[engname]
        s.count += inc
        if ins is not None:
            ins.then_inc(s.h, inc)
        tok = (s, s.count)
        for r in reads:
            r.r.append(tok)
        for w in writes:
            if w.r:
                w.pr = w.r
                w.r = []
                w.w = [tok]
            else:
                w.w.append(tok)
        return tok


def build_nc(debug=False):
    nc = bass.Bass("TRN2", target_bir_lowering=False)

    def din(name, shape, dt=F32):
        return nc.dram_tensor(name, list(shape), dt, kind="ExternalInput").ap()

    x_c = din("x_c", [TT, D])
    pos16 = din("pos16", [128, 16])
    g1_d = din("norm1_g", [1, D])
    wpack = din("wpack", [128, WTOTAL])
    sparams = din("sparams", [80, 128])
    g2_d = din("norm2_g", [1, D])
    gf_d = din("final_g", [1, D])
    out_d = nc.dram_tensor("out", [T, D], F32, kind="ExternalOutput").ap()
    dbg = {}
    if debug:
        for nm, shp, dt in [("dbg_hT", [128, KD * TT], BF16), ("dbg_bu", [128, 8 * T], BF16),
                            ("dbg_pl", [128, 8 * T], BF16), ("dbg_merged", [128, KD * T], BF16),
                            ("dbg_x1", [128, NTOK * D], F32), ("dbg_h2T", [128, KD * T], BF16),
                            ("dbg_prm", [128, 80], F32)]:
            dbg[nm] = nc.dram_tensor(nm, shp, dt, kind="ExternalOutput").ap()

    with ExitStack() as es:
        def sb(name, shape, dt):
            return es.enter_context(nc.sbuf_tensor(name, shape, dt))

        def hsem(name):
            return es.enter_context(nc.semaphore(name))

        arenaA = sb("arenaA", [128, 16512], F32)
        arenaB = sb("arenaB", [128, 8192], F32)
        arenaC = sb("arenaC", [128, 12288], F32)
        wring = sb("wring", [128, NSLOT * SLOT], BF16)
        identf = sb("identf", [128, 128], F32)
        identb = sb("identb", [128, 128], BF16)
        iot = sb("iot", [128, 128], I32)
        prm_rows = sb("prm_rows", [128, 128], F32)
        prm = sb("prm", [128, 80], F32)
        ssb = sb("ssb", [128, 32], F32)
        rsb = sb("rsb", [128, 32], F32)
        rstdb = sb("rstdb", [128, 32], F32)
        pos_sb = sb("pos_sb", [128, 16], F32)
        cnt_sb = sb("cnt_sb", [128, 64], F32)
        inv_sb = sb("inv_sb", [128, 64], F32)
        tmp16 = sb("tmp16", [128, 16], F32)
        junk = sb("junk", [128, 4], F32)
        zeros_bf = sb("zeros_bf", [128, 512], BF16)
        ps = es.enter_context(nc.psum_tensor("ps", [128, 4096], F32))
        h_act = hsem("s_act"); h_dve = hsem("s_dve"); h_pe = hsem("s_pe"); h_pool = hsem("s_pool")
        h_c = hsem("s_c"); h_x0 = hsem("s_x0"); h_x1 = hsem("s_x1"); h_x2 = hsem("s_x2")
        h_w0 = hsem("s_w0"); h_w1 = hsem("s_w1"); h_w2 = hsem("s_w2"); h_w3 = hsem("s_w3")
        h_o0 = hsem("s_o0"); h_o1 = hsem("s_o1"); h_g = hsem("s_g"); h_o2 = hsem("s_o2"); h_o3 = hsem("s_o3")
        h_r0 = hsem("s_r0"); h_r1 = hsem("s_r1"); h_r2 = hsem("s_r2"); h_r3 = hsem("s_r3")
        h_r4 = hsem("s_r4"); h_r5 = hsem("s_r5"); h_r6 = hsem("s_r6"); h_r7 = hsem("s_r7")
        h_dbg = hsem("s_dbg")
        h_j0a = hsem("s_j0a"); h_j0b = hsem("s_j0b"); h_j0c = hsem("s_j0c")
        h_p = hsem("s_p")
        block = es.enter_context(nc.Block())
        em = Emitter(nc)
        em.engsem = {"act": em.sem(h_act), "dve": em.sem(h_dve), "pe": em.sem(h_pe),
                     "pool": em.sem(h_pool)}
        s_c = em.sem(h_c)
        s_x = [em.sem(h_x0), em.sem(h_x1), em.sem(h_x2)]
        s_w = [em.sem(h_w0), em.sem(h_w1), em.sem(h_w2), em.sem(h_w3)]
        s_g = em.sem(h_g)
        s_r = [em.sem(h) for h in (h_r0, h_r1, h_r2, h_r3, h_r4, h_r5, h_r6, h_r7)]
        s_oh = [[em.sem(h_)] for h_ in (h_o0, h_o1, h_o2, h_o3)]
        s_dbg = em.sem(h_dbg)
        s_j0 = [em.sem(h_j0a), em.sem(h_j0b), em.sem(h_j0c)]
        s_p = em.sem(h_p)

        A = arenaA
        hT = A[:, 0:8320].bitcast(BF16).rearrange("p (k t) -> p k t", t=TT)
        bu = A[:, 8320:12416].bitcast(BF16).rearrange("p (k t) -> p k t", t=T)
        pl = A[:, 12416:16512].bitcast(BF16).rearrange("p (k t) -> p k t", t=T)
        x1 = A[:, 0:16384].rearrange("p (i d) -> p i d", d=D)
        mg = arenaB[:, :].bitcast(BF16).rearrange("p (k t) -> p k t", t=T)
        ost = [arenaB[:, 2048 * q_:2048 * (q_ + 1)] for q_ in range(4)]
        C = arenaC
        gb = C[:, 0:2048]
        hb = [C[:, 2048:3072].bitcast(BF16), C[:, 3072:4096].bitcast(BF16)]
        sq = C[:, 4096:5120].bitcast(BF16)
        FB = 5120
        hb2 = hb + [C[:, FB + 1024 * n_:FB + 1024 * (n_ + 1)].bitcast(BF16) for n_ in range(6)]
        hbc = C[:, FB + 6144:FB + 7168].bitcast(BF16)
        xt = ([C[:, FB:FB + 2048], C[:, FB + 2048:FB + 4096], C[:, FB + 4096:FB + 6144]]
              + [arenaB[:, 2048 * n_:2048 * (n_ + 1)] for n_ in range(4)]
              + [A[:, 8320 + 2048 * n_:8320 + 2048 * (n_ + 1)] for n_ in range(2)])
        R1 = C[:, FB:FB + 1040]
        R2 = C[:, FB + 1040:FB + 2080]
        R3 = C[:, FB + 2080:FB + 3120]
        sga = [C[:, FB + 3120:FB + 3632], C[:, FB + 3632:FB + 4144]]
        sgb = [C[:, FB + 4144:FB + 4656], C[:, FB + 4656:FB + 5168]]
        m1 = C[:, FB + 5168:FB + 5680]
        m2 = C[:, FB + 5680:FB + 6192]
        act = C[:, FB:FB + 5632].bitcast(BF16).rearrange("p (k t) -> p k t", t=T)
        sg = [C[:, FB + 5632:FB + 6144], C[:, FB + 6144:FB + 6656]]

        def slot(s, n=SLOT):
            return wring[:, s * SLOT:s * SLOT + n]

        def bank(b):
            return ps[:, b * 512:(b + 1) * 512]

        def pTb(q):
            return ps[:, q * 1024:(q + 1) * 1024].bitcast(BF16).rearrange("p (k t) -> p k t", t=128)

        R = {}
        for nm in ["zeros", "iot", "identf", "identb", "prm_rows", "prm", "pos", "cnt", "inv", "tmp16",
                   "gb", "sq", "hT", "bu", "pl", "mg", "R1", "R2", "R3", "m1", "m2"]:
            R[nm] = em.resource(nm)
        r_xt = [em.resource("xt%d" % n_) for n_ in range(9)]
        r_hb = [em.resource("hb0"), em.resource("hb1")]
        r_hb2 = r_hb + [em.resource("hb2_%d" % n_) for n_ in range(6)]
        r_hbc = em.resource("hbc")
        r_sga = [em.resource("sga0"), em.resource("sga1")]
        r_sgb = [em.resource("sgb0"), em.resource("sgb1")]
        r_sg = [em.resource("sg0"), em.resource("sg1")]
        r_ost = [em.resource("ost0"), em.resource("ost1")]
        r_osth = [[em.resource("ost%d_%d" % (a_, b_)) for b_ in range(2)] for a_ in range(4)]
        r_bank = [em.resource("bank%d" % b) for b in range(8)]
        r_slot = [em.resource("slot%d" % s) for s in range(NSLOT)]
        r_x1 = [em.resource("x1_%d" % i) for i in range(NTOK)]
        r_hT = [em.resource("hT_a"), em.resource("hT_b")]
        r_j0 = [em.resource("j0_%d" % n_) for n_ in range(3)]
        r_mg = [em.resource("mg_%d" % i) for i in range(NTOK)]
        r_act = [em.resource("act0"), em.resource("act1")]
        r_ss = [em.resource("ss%d" % c) for c in range(32)]
        r_rs = [em.resource("rs%d" % c) for c in range(32)]
        r_ss2 = [em.resource("ss2_%d" % c) for c in range(32)]
        r_rstd = [em.resource("rstd%d" % c) for c in range(32)]
        r_outs = [em.resource("out%d" % i) for i in range(NTOK)]
        r_dbg = em.resource("dbg")

        NJOBS = len(JOB_SIZES)
        job_off = [0]
        for n_ in JOB_SIZES:
            job_off.append(job_off[-1] + n_)

        def program():
            state = {"next_job": 0, "bank_rr": 0}

            def issue_job():
                n = state["next_job"]
                if n >= NJOBS:
                    return
                s = n % NSLOT
                sz = JOB_SIZES[n]
                off = job_off[n]
                if n == 0:
                    for wi in (1, 2):
                        j0_part(wi)
                    state["next_job"] = 1
                    return
                em.op("pool", lambda e, s=s, sz=sz, off=off: e.dma_start(out=slot(s, sz), in_=wpack[:, off:off + sz]),
                      writes=[r_slot[s]], sem=s_w[s], inc=16)
                state["next_job"] = n + 1

            def j0_part(wi):
                em.op("pool", lambda e: e.dma_start(out=slot(0)[:, wi * 2048:(wi + 1) * 2048],
                                                    in_=wpack[:, wi * 2048:(wi + 1) * 2048]),
                      writes=[r_j0[wi]], sem=s_j0[wi], inc=16)

            job_ctr = {"n": 0}

            def take_job():
                n = job_ctr["n"]
                job_ctr["n"] = n + 1
                return n % NSLOT

            def next_bank():
                b = state["bank_rr"]
                state["bank_rr"] = (b + 1) % 8
                return b

            em.op("pool", lambda e: e.iota(iot[:, :], [[1, 128]], base=0, channel_multiplier=-1),
                  writes=[R["iot"]])
            em.op("dve", lambda e: e.tensor_single_scalar(out=identf[:, :], in_=iot[:, :], scalar=0.0,
                                                           op=ALU.is_equal),
                  reads=[R["iot"]], writes=[R["identf"]])
            em.op("dve", lambda e: e.tensor_copy(out=identb[:, :], in_=identf[:, :]),
                  reads=[R["identf"]], writes=[R["identb"]])
            em.op("dve", lambda e: e.memset(zeros_bf[:, :], 0.0), writes=[R["zeros"]])

            def warm(n):
                def f(e):
                    ins = None
                    for _ in range(n):
                        ins = e.matmul(bank(7), identb[:, :], zeros_bf[:, :], start=True, stop=True)
                    return ins
                em.op("pe", f, reads=[R["identb"], R["zeros"]], writes=[r_bank[7]])

            def load_small_params():
                em.op("sp", lambda e: e.dma_start(out=prm_rows[0:80, :], in_=sparams[:, :]),
                      writes=[R["prm_rows"]], sem=s_c, inc=16)
                em.op("sp", lambda e: e.dma_start(out=pos_sb[:, :], in_=pos16[:, :]),
                      writes=[R["pos"]], sem=s_p, inc=16)

            def small_param_compute():
                em.op("pe", lambda e: e.transpose(bank(7)[:, 0:80], prm_rows[0:80, :], identf[0:80, 0:80]),
                      reads=[R["prm_rows"], R["identf"]], writes=[r_bank[7]])
                em.op("act", lambda e: e.copy(out=prm[:, :], in_=bank(7)[:, 0:80]),
                      reads=[r_bank[7]], writes=[R["prm"]])

            def pool_count_compute():
                for g, w in enumerate(POOL_W):
                    em.op("dve", lambda e, g=g, w=w: e.tensor_scalar(out=cnt_sb[:, g * 16:(g + 1) * 16], in0=pos_sb[:, :],
                                                                     scalar1=float(w), scalar2=None, op0=ALU.min),
                          reads=[R["pos"]], writes=[R["cnt"]])
                em.op("dve", lambda e: e.reciprocal(out=inv_sb[:, :], in_=cnt_sb[:, :]),
                      reads=[R["cnt"]], writes=[R["inv"]])

            def norm_stats(col, src_ap, src_res, npart):
                em.op("act", lambda e: e.activation(out=sq[0:npart, :], in_=src_ap, func=AF.Square,
                                                    accum_out=ssb[0:npart, col:col + 1]),
                      reads=src_res, writes=[R["sq"], r_ss[col]])
                em.op("act", lambda e: e.copy(out=junk[0:npart, 0:1], in_=ssb[0:npart, col:col + 1]),
                      reads=[r_ss[col]], writes=[r_ss2[col]])
                em.op("act", lambda e: e.activation(out=rsb[0:npart, col:col + 1], in_=ssb[0:npart, col:col + 1],
                                                    func=AF.Sqrt, scale=1.0 / D, bias=EPS),
                      reads=[r_ss2[col]], writes=[r_rs[col]])

            def norm_rstd(col, npart):
                em.op("dve", lambda e: e.reciprocal(out=rstdb[0:npart, col:col + 1], in_=rsb[0:npart, col:col + 1]),
                      reads=[r_rs[col]], writes=[r_rstd[col]])

            def norm_to_T(col, src_ap, src_res, npart, hbuf, hres, q, dstT, dst_res, c0, evac):
                norm_rstd(col, npart)
                em.op("dve", lambda e: e.scalar_tensor_tensor(out=hbuf[0:npart, :], in0=src_ap,
                                                              scalar=rstdb[0:npart, col:col + 1], in1=gb[0:npart, :],
                                                              op0=ALU.mult, op1=ALU.mult),
                      reads=src_res + [r_rstd[col], R["gb"]], writes=[hres])

                def stage2():
                    pt = pTb(q)

                    def tr(e):
                        ins = None
                        for k in range(KD):
                            ins = e.transpose(pt[:, k, 0:npart], hbuf[0:npart, k * 128:(k + 1) * 128],
                                              identb[0:npart, 0:npart])
                        return ins
                    em.op("pe", tr, reads=[hres, R["identb"]], writes=[r_bank[2 * q], r_bank[2 * q + 1]])
                    if evac == "act":
                        em.op("act", lambda e: e.copy(out=dstT[:, :, c0:c0 + npart], in_=pt[:, :, 0:npart]),
                              reads=[r_bank[2 * q], r_bank[2 * q + 1]], writes=[dst_res])
                    else:
                        em.op("dve", lambda e: e.tensor_copy(out=dstT[:, :, c0:c0 + npart], in_=pt[:, :, 0:npart]),
                              reads=[r_bank[2 * q], r_bank[2 * q + 1]], writes=[dst_res])
                return stage2

            def mm_group(e, out_ap, lhs_fn, rhs_fn, nk):
                ins = None
                for k in range(nk):
                    ins = e.matmul(out_ap, lhs_fn(k), rhs_fn(k), start=(k == 0), stop=(k == nk - 1))
                return ins

            PW = lambda c: prm[:, c:c + 1]

            xsem = s_x + s_r[0:6]
            for ti in range(NTOK + 1):
                npart = H if ti == 0 else 128
                r0 = 0 if ti == 0 else H + (ti - 1) * 128
                em.op("sp", lambda e, ti=ti, npart=npart, r0=r0: e.dma_start(out=xt[ti][0:npart, :], in_=x_c[r0:r0 + npart, :]),
                      writes=[r_xt[ti]], sem=xsem[ti], inc=16)
                if ti == 0:
                    em.op("sp", lambda e: e.dma_start(out=gb, in_=g1_d[0:1, :].partition_broadcast(128)),
                          writes=[R["gb"]], sem=s_g, inc=16)
                if ti == 4:
                    load_small_params()
            em.wait_tokens("pool", [t for r_ in r_xt[0:3] for t in r_.w])
            issue_job()
            em.wait_tokens("pool", [t for r_ in r_xt for t in r_.w])
            j0_part(0)
            for _ in range(NSLOT - 1):
                issue_job()

            def phase1_tiles(tiles, qfn, hbufs, lag, nwarm=0, drain=True):
                pend = []
                for idx, ti in enumerate(tiles):
                    npart = H if ti == 0 else 128
                    r0 = 0 if ti == 0 else H + (ti - 1) * 128
                    hbuf, hres = hbufs[idx % len(hbufs)]
                    norm_stats(ti, xt[ti][0:npart, :], [r_xt[ti]], npart)
                    pend.append(norm_to_T(ti, xt[ti][0:npart, :], [r_xt[ti]], npart, hbuf, hres, qfn(ti),
                                          hT, r_hT[0 if ti <= 4 else 1], r0, "act" if ti % 2 == 1 else "dve"))
                    if len(pend) > lag:
                        pend.pop(0)()
                        if nwarm:
                            warm(nwarm)
                if not drain:
                    return pend
                for st2 in pend:
                    st2()
                    if nwarm:
                        warm(nwarm)
                return []

            def mixA_setup(j):
                s = take_job()
                wv = slot(s).rearrange("p (w k c) -> p w k c", k=16, w=3)
                bH = 3 if j == 0 else 6 + (j % 2)
                return s, wv, bH

            def mixA_pe_halo(j, s, wv, bH):
                def halo(e):
                    mm_group(e, bank(bH)[:, 0:16], lambda k: wv[:, 1, k, :], lambda k: hT[:, k, 0:16], KD)
                    return mm_group(e, bank(bH)[:, 16:32], lambda k: wv[:, 2, k, :], lambda k: hT[:, k, 0:16], KD)
                em.op("pe", halo, reads=[r_slot[s], r_hT[0]] + ([r_j0[1], r_j0[2]] if j == 0 else []), writes=[r_bank[bH]])

            def mixA_ew_halo(j, bH):
                em.op("act", lambda e: e.copy(out=R1[:, 0:16], in_=bank(bH)[:, 0:16]),
                      reads=[r_bank[bH]], writes=[R["R1"]])
                em.op("dve", lambda e: e.tensor_tensor(out=R2[:, 0:16], in0=bank(bH)[:, 16:32], in1=R1[:, 0:16],
                                                       op=ALU.mult),
                      reads=[r_bank[bH], R["R1"]], writes=[R["R2"]])

            def mixA_pe_blk(j, blk, s, wv, which=(1, 2, 0)):
                c0 = H + blk * 512
                bs = 3 * ((2 * j + blk) % 2)
                for (wi, bb) in [(1, bs), (2, bs + 1), (0, bs + 2)]:
                    if wi not in which:
                        continue
                    em.op("pe", lambda e, wi=wi, bb=bb: mm_group(
                        e, bank(bb), lambda k: wv[:, wi, k, :], lambda k: hT[:, k, c0:c0 + 512], KD),
                        reads=[r_slot[s], r_hT[blk]] + ([r_j0[wi]] if j == 0 else []), writes=[r_bank[bb]])

            def mixA_ew_blk(j, blk):
                c0 = H + blk * 512
                o0 = blk * 512
                bs = 3 * ((2 * j + blk) % 2)
                em.op("act", lambda e: e.copy(out=R1[:, c0:c0 + 512], in_=bank(bs)),
                      reads=[r_bank[bs]], writes=[R["R1"]])
                em.op("dve", lambda e: e.tensor_tensor(out=R2[:, c0:c0 + 512], in0=bank(bs + 1),
                                                       in1=R1[:, c0:c0 + 512], op=ALU.mult),
                      reads=[r_bank[bs + 1], R["R1"]], writes=[R["R2"]])
                em.op("act", lambda e: e.activation(out=R3[:, o0:o0 + 512], in_=R2[:, c0:c0 + 512],
                                                    func=AF.Identity, scale=PW(48 + j), bias=PW(56 + j)),
                      reads=[R["R2"], R["prm"]], writes=[R["R3"]])
                for (sh, pc) in [(1, 40 + j), (2, 32 + j)]:
                    em.op("dve", lambda e, sh=sh, pc=pc: e.scalar_tensor_tensor(
                        out=R3[:, o0:o0 + 512], in0=R2[:, c0 - sh:c0 - sh + 512], scalar=PW(pc),
                        in1=R3[:, o0:o0 + 512], op0=ALU.mult, op1=ALU.add),
                        reads=[R["R2"], R["R3"], R["prm"]], writes=[R["R3"]])
                em.op("dve", lambda e: e.tensor_tensor(out=bu[:, j, o0:o0 + 512], in0=bank(bs + 2),
                                                       in1=R3[:, o0:o0 + 512], op=ALU.mult),
                      reads=[r_bank[bs + 2], R["R3"]], writes=[R["bu"]])

            warm(24)
            phase1_tiles(range(0, 5), lambda ti: ti % 3, [(hb[0], r_hb[0]), (hb[1], r_hb[1]), (hbc, r_hbc)], 2, nwarm=4)
            small_param_compute()
            st2s = phase1_tiles(range(5, NTOK + 1), lambda ti: 2 + (ti % 2),
                                [(hb2[n_], r_hb2[n_]) for n_ in range(2, 6)], 4, drain=False)
            s, wv, bH = mixA_setup(0)
            mixA_pe_halo(0, s, wv, bH)
            mixA_pe_blk(0, 0, s, wv, which=(1, 2))
            for st2 in st2s:
                st2()
            mixA_pe_blk(0, 0, s, wv, which=(0,))
            em.op("sp", lambda e: e.dma_start(out=gb, in_=g2_d[0:1, :].partition_broadcast(128)),
                  writes=[R["gb"]], sem=s_g, inc=16)
            if debug:
                em.op("sp", lambda e: e.dma_start(out=dbg["dbg_prm"][:, :], in_=prm[:, :]),
                      reads=[R["prm"]], writes=[r_dbg], sem=s_dbg, inc=16)
                em.op("sp", lambda e: e.dma_start(out=dbg["dbg_hT"][:, :], in_=A[:, 0:8320].bitcast(BF16)),
                      reads=r_hT, writes=[r_dbg], sem=s_dbg, inc=16)
            ov = [t for n_ in range(2, 6) for t in r_hb2[n_].r + r_hb2[n_].w] + [t for t in r_hbc.r + r_hbc.w]
            em.wait_tokens("act", ov)
            em.wait_tokens("dve", ov)
            mixA_ew_halo(0, bH)
            mixA_ew_blk(0, 0)
            mixA_pe_blk(0, 1, s, wv)
            issue_job()
            mixA_ew_blk(0, 1)
            for j in range(1, 8):
                s, wv, bH = mixA_setup(j)
                mixA_pe_halo(j, s, wv, bH)
                mixA_ew_halo(j, bH)
                mixA_pe_blk(j, 0, s, wv)
                mixA_ew_blk(j, 0)
                mixA_pe_blk(j, 1, s, wv)
                issue_job()
                mixA_ew_blk(j, 1)

            pool_count_compute()
            for j in range(8):
                if j % 3 == 0:
                    s = take_job()
                n = 3 if j < 6 else 2
                wv = slot(s, 16 * n * 128).rearrange("p (k c) -> p k c", k=16)
                cw = (j % 3) * 128
                g = j // 2
                bH = 6 + (j % 2)
                em.op("pe", lambda e, wv=wv, cw=cw, bH=bH: mm_group(
                    e, bank(bH)[:, 0:16], lambda k: wv[:, k, cw:cw + 128], lambda k: hT[:, k, 0:16], KD),
                    reads=[r_slot[s], r_hT[0]], writes=[r_bank[bH]])
                em.op("act", lambda e, bH=bH: e.copy(out=R1[:, 0:16], in_=bank(bH)[:, 0:16]),
                      reads=[r_bank[bH]], writes=[R["R1"]])
                for blk in range(2):
                    c0 = H + blk * 512
                    bb = 3 * (j % 2) + blk
                    em.op("pe", lambda e, wv=wv, cw=cw, bb=bb, c0=c0: mm_group(
                        e, bank(bb), lambda k: wv[:, k, cw:cw + 128], lambda k: hT[:, k, c0:c0 + 512], KD),
                        reads=[r_slot[s], r_hT[blk]], writes=[r_bank[bb]])
                    if blk == 1 and (j % 3 == 2 or j == 7):
                        issue_job()
                    em.op("act", lambda e, bb=bb, c0=c0: e.copy(out=R1[:, c0:c0 + 512], in_=bank(bb)),
                          reads=[r_bank[bb]], writes=[R["R1"]])
                bufs = [(R2, R["R2"]), (R3, R["R3"])]
                cur, cur_res = R1, R["R1"]
                sh = 1
                for step in range(g + 1):
                    dst, dst_res = bufs[step % 2]
                    lo = 2 * sh - 1
                    em.op("dve", lambda e, dst=dst, cur=cur, lo=lo, sh=sh: e.tensor_tensor(
                        out=dst[:, lo:TT], in0=cur[:, lo:TT], in1=cur[:, lo - sh:TT - sh], op=ALU.add),
                        reads=[cur_res], writes=[dst_res])
                    cur, cur_res = dst, dst_res
                    sh *= 2
                w = POOL_W[g]
                em.op("dve", lambda e, cur=cur, w=w, j=j: e.scalar_tensor_tensor(
                    out=pl[:, j, :], in0=cur[:, H:TT], scalar=1.0 / w, in1=R1[:, H:TT],
                    op0=ALU.mult, op1=ALU.subtract),
                    reads=[cur_res, R["R1"]], writes=[R["pl"]])
                em.op("dve", lambda e, cur=cur, g=g: e.tensor_tensor(out=tmp16[:, :], in0=cur[:, H:H + 16],
                                                                     in1=inv_sb[:, g * 16:(g + 1) * 16], op=ALU.mult),
                      reads=[cur_res, R["inv"]], writes=[R["tmp16"]])
                em.op("dve", lambda e, j=j: e.tensor_tensor(out=pl[:, j, 0:16], in0=tmp16[:, :], in1=R1[:, H:H + 16],
                                                            op=ALU.subtract),
                      reads=[R["tmp16"], R["R1"]], writes=[R["pl"]])

            for j in range(16):
                s = take_job()
                wg_ = slot(s, 4096).rearrange("p (k w c) -> p k w c", k=16, w=2)
                wa = wring[:, s * SLOT + 4096:s * SLOT + 5120].rearrange("p (k c) -> p k c", k=8)
                wp = wring[:, s * SLOT + 5120:s * SLOT + 5376].rearrange("p (k c) -> p k c", k=2)
                g = j // 4
                for blk in range(2):
                    c0 = H + blk * 512
                    o0 = blk * 512
                    par = (2 * j + blk) % 2
                    b0 = 4 * par
                    em.op("pe", lambda e, b0=b0, wg_=wg_, c0=c0: mm_group(
                        e, bank(b0), lambda k: wg_[:, k, 0, :], lambda k: hT[:, k, c0:c0 + 512], KD),
                        reads=[r_slot[s], r_hT[blk]], writes=[r_bank[b0]])
                    em.op("pe", lambda e, b0=b0, wa=wa, o0=o0: mm_group(
                        e, bank(b0 + 1), lambda k: wa[:, k, :], lambda k: bu[:, k, o0:o0 + 512], 8),
                        reads=[r_slot[s], R["bu"]], writes=[r_bank[b0 + 1]])
                    em.op("pe", lambda e, b0=b0, wg_=wg_, c0=c0: mm_group(
                        e, bank(b0 + 2), lambda k: wg_[:, k, 1, :], lambda k: hT[:, k, c0:c0 + 512], KD),
                        reads=[r_slot[s], r_hT[blk]], writes=[r_bank[b0 + 2]])
                    em.op("pe", lambda e, b0=b0, wp=wp, o0=o0, g=g: mm_group(
                        e, bank(b0 + 3), lambda k: wp[:, k, :], lambda k: pl[:, 2 * g + k, o0:o0 + 512], 2),
                        reads=[r_slot[s], R["pl"]], writes=[r_bank[b0 + 3]])
                    if blk == 1:
                        issue_job()
                    em.op("act", lambda e, b0=b0, par=par, j=j: e.activation(out=sga[par], in_=bank(b0), func=AF.Sigmoid,
                                                                             bias=PW(j)),
                          reads=[r_bank[b0], R["prm"]], writes=[r_sga[par]])
                    em.op("act", lambda e, b0=b0, par=par, j=j: e.activation(out=sgb[par], in_=bank(b0 + 2), func=AF.Sigmoid,
                                                                             bias=PW(16 + j)),
                          reads=[r_bank[b0 + 2], R["prm"]], writes=[r_sgb[par]])
                    em.op("dve", lambda e, b0=b0, par=par: e.tensor_tensor(out=m1, in0=bank(b0 + 1), in1=sga[par], op=ALU.mult),
                          reads=[r_bank[b0 + 1], r_sga[par]], writes=[R["m1"]])
                    em.op("dve", lambda e, b0=b0, par=par, j=j: e.scalar_tensor_tensor(
                        out=m2, in0=bank(b0 + 3), scalar=PW(64 + j), in1=sgb[par], op0=ALU.mult, op1=ALU.mult),
                        reads=[r_bank[b0 + 3], r_sgb[par], R["prm"]], writes=[R["m2"]])
                    em.op("dve", lambda e, j=j, o0=o0: e.tensor_tensor(out=mg[:, j, o0:o0 + 512], in0=m1, in1=m2, op=ALU.add),
                          reads=[R["m1"], R["m2"]], writes=r_mg[4 * blk:4 * blk + 4])

            if debug:
                em.op("sp", lambda e: e.dma_start(out=dbg["dbg_bu"][:, :], in_=A[:, 8320:12416].bitcast(BF16)),
                      reads=[R["bu"]], writes=[r_dbg], sem=s_dbg, inc=16)
                em.op("sp", lambda e: e.dma_start(out=dbg["dbg_pl"][:, :], in_=A[:, 12416:16512].bitcast(BF16)),
                      reads=[R["pl"]], writes=[r_dbg], sem=s_dbg, inc=16)
                em.op("sp", lambda e: e.dma_start(out=dbg["dbg_merged"][:, :], in_=arenaB[:, :].bitcast(BF16)),
                      reads=r_mg, writes=[r_dbg], sem=s_dbg, inc=16)

            ovl = [r_hT[0], r_hT[1], R["bu"], R["pl"]] + ([r_dbg] if debug else [])
            em.wait_tokens("sp", [t for r_ in ovl for t in r_.w + r_.r + r_.pr])
            for i in range(NTOK):
                em.op("sp", lambda e, i=i: e.dma_start(out=x1[:, i, :], in_=x_c[H + i * 128:H + (i + 1) * 128, :]),
                      writes=[r_x1[i]], sem=s_r[i], inc=16)
            n2_stage2 = {}

            def n2_step(i):
                if i < 0:
                    return
                n2_stage2[i] = norm_to_T(9 + i, x1[:, i, :], [r_x1[i]], 128, hb2[i], r_hb2[i], 2 + (i % 2),
                                         mg, r_mg[i], i * 128, "act" if i % 2 == 0 else "dve")

            def n2_step2(i):
                if i < 0:
                    return
                n2_stage2[i]()

            for db in range(4):
                sa = take_job()
                sb = take_job()
                wA = slot(sa, 4096).rearrange("p (k c) -> p k c", k=8)
                wB = slot(sb, 4096).rearrange("p (k c) -> p k c", k=8)
                for i in range(NTOK):
                    b = next_bank() if db < 3 else i % 4
                    em.op("pe", lambda e, b=b, i=i, wA=wA, wB=wB: mm_group(
                        e, bank(b), lambda k: mg[:, k, i * 128:(i + 1) * 128],
                        lambda k: (wA if k < 8 else wB)[:, k % 8, :], KD),
                        reads=[r_slot[sa], r_slot[sb], r_mg[i]], writes=[r_bank[b]])
                    if i == NTOK - 1:
                        issue_job()
                        issue_job()
                    em.op("dve", lambda e, b=b, i=i, db=db: e.tensor_tensor(
                        out=x1[:, i, db * 512:(db + 1) * 512], in0=bank(b), in1=x1[:, i, db * 512:(db + 1) * 512], op=ALU.add),
                        reads=[r_bank[b], r_x1[i]], writes=[r_x1[i]])
                    if db == 3:
                        norm_stats(9 + i, x1[:, i, :], [r_x1[i]], 128)
                        n2_step(i - 1)
                        n2_step2(i - 2)
            n2_step(NTOK - 1)

            def ffn_views(s):
                wgt = slot(s, 2048).rearrange("p (k c) -> p k c", k=16)
                wup = wring[:, s * SLOT + 2048:s * SLOT + 4096].rearrange("p (k c) -> p k c", k=16)
                return wgt, wup

            def ffn_pe(blk, s, wgt, wup):
                o0 = blk * 512
                bg = next_bank()
                bu_ = next_bank()
                em.op("pe", lambda e: mm_group(
                    e, bank(bg), lambda k: wgt[:, k, :], lambda k: mg[:, k, o0:o0 + 512], KD),
                    reads=[r_slot[s]] + r_mg[4 * blk:4 * blk + 4], writes=[r_bank[bg]])
                em.op("pe", lambda e: mm_group(
                    e, bank(bu_), lambda k: wup[:, k, :], lambda k: mg[:, k, o0:o0 + 512], KD),
                    reads=[r_slot[s]] + r_mg[4 * blk:4 * blk + 4], writes=[r_bank[bu_]])
                return bg, bu_

            def ffn_ew(fi, blk, bg, bu_):
                o0 = blk * 512
                par = blk
                em.op("act", lambda e: e.activation(out=sg[par], in_=bank(bg), func=AF.Silu),
                      reads=[r_bank[bg]], writes=[r_sg[par]])
                em.op("dve", lambda e: e.tensor_tensor(
                    out=act[:, fi, o0:o0 + 512], in0=bank(bu_), in1=sg[par], op=ALU.mult),
                    reads=[r_bank[bu_], r_sg[par]], writes=[r_act[blk]])

            state["bank_rr"] = 0
            s_f0 = take_job()
            wgt_f0, wup_f0 = ffn_views(s_f0)
            bg_f0, bu_f0 = ffn_pe(0, s_f0, wgt_f0, wup_f0)
            n2_step2(NTOK - 2)
            n2_step2(NTOK - 1)
            ov2 = [t for r_ in r_hb2 for t in r_.r + r_.w]
            em.wait_tokens("act", ov2)
            em.wait_tokens("dve", ov2)
            ffn_ew(0, 0, bg_f0, bu_f0)
            em.op("sp", lambda e: e.dma_start(out=gb, in_=gf_d[0:1, :].partition_broadcast(128)),
                  writes=[R["gb"]], sem=s_g, inc=16)
            if debug:
                em.op("sp", lambda e: e.dma_start(out=dbg["dbg_x1"][:, :], in_=A[:, 0:16384]),
                      reads=r_x1, writes=[r_dbg], sem=s_dbg, inc=16)
                em.op("sp", lambda e: e.dma_start(out=dbg["dbg_h2T"][:, :], in_=arenaB[:, :].bitcast(BF16)),
                      reads=r_mg, writes=[r_dbg], sem=s_dbg, inc=16)

            def final_norm(i):
                if i < 0:
                    return
                par = i % 4
                col = 17 + i
                norm_rstd(col, 128)
                em.op("dve", lambda e: e.scalar_tensor_tensor(
                    out=ost[par], in0=x1[:, i, :], scalar=rstdb[:, col:col + 1], in1=gb, op0=ALU.mult, op1=ALU.mult),
                    reads=[r_x1[i], r_rstd[col], R["gb"]], writes=[r_osth[par][0]] + r_mg)
                em.op("sp", lambda e: e.dma_start(out=out_d[i * 128:(i + 1) * 128, :], in_=ost[par]),
                      reads=[r_osth[par][0]], writes=[r_outs[i]], sem=s_oh[par][0], inc=16)

            for gi in range(G):
                for fi in range(FG):
                    first = (gi == 0 and fi == 0)
                    if first:
                        s, wgt, wup = s_f0, wgt_f0, wup_f0
                    else:
                        s = take_job()
                        wgt, wup = ffn_views(s)
                    for blk in range(2):
                        if first and blk == 0:
                            continue
                        bg, bu_ = ffn_pe(blk, s, wgt, wup)
                        if blk == 1:
                            issue_job()
                        ffn_ew(fi, blk, bg, bu_)
                def down_group(db, i, s, wd):
                    b = next_bank()
                    em.op("pe", lambda e, b=b, i=i, wd=wd: mm_group(
                        e, bank(b), lambda k: act[:, k, i * 128:(i + 1) * 128], lambda k: wd[:, k, :], FG),
                        reads=[r_slot[s], r_act[i // 4]], writes=[r_bank[b]])
                    return b

                def down_add(db, i, b):
                    em.op("dve", lambda e, b=b, i=i, db=db: e.tensor_tensor(
                        out=x1[:, i, db * 512:(db + 1) * 512], in0=bank(b), in1=x1[:, i, db * 512:(db + 1) * 512],
                        op=ALU.add),
                        reads=[r_bank[b], r_x1[i]], writes=[r_x1[i]])

                last = (gi == G - 1)
                for db in range(2 if last else 4):
                    s = take_job()
                    wd = slot(s, FG * 512).rearrange("p (k c) -> p k c", k=FG)
                    for i in range(NTOK):
                        b = down_group(db, i, s, wd)
                        if i == NTOK - 1:
                            issue_job()
                        down_add(db, i, b)
                if last:
                    s2 = take_job()
                    s3 = take_job()
                    wd2 = slot(s2, FG * 512).rearrange("p (k c) -> p k c", k=FG)
                    wd3 = slot(s3, FG * 512).rearrange("p (k c) -> p k c", k=FG)
                    for i in range(NTOK):
                        b2 = down_group(2, i, s2, wd2)
                        b3 = down_group(3, i, s3, wd3)
                        down_add(2, i, b2)
                        down_add(3, i, b3)
                        norm_stats(17 + i, x1[:, i, :], [r_x1[i]], 128)
                        final_norm(i - 1)

            final_norm(NTOK - 1)
            em.wait_tokens("sp", [t for r_ in r_outs for t in r_.w] + r_dbg.w)

        @block.sync
        def _(e):
            em.begin_pass("sp", e)
            program()

        @block.scalar
        def _(e):
            em.begin_pass("act", e)
            program()

        @block.vector
        def _(e):
            em.begin_pass("dve", e)
            program()

        @block.gpsimd
        def _(e):
            em.begin_pass("pool", e)
            program()

        @block.tensor
        def _(e):
            em.begin_pass("pe", e)
            program()

    return nc


def make_in_maps(inputs):
    f = lambda a: np.ascontiguousarray(np.asarray(a, dtype=np.float32))
    x = f(inputs["x"])
    B, S, _ = x.shape
    shared = {
        "norm1_g": f(inputs["norm1_g"]).reshape(1, D),
        "wpack": pack_weights(inputs),
        "sparams": np.ascontiguousarray(np.concatenate([
            f(inputs["b_gate"]).reshape(32, 128), f(inputs["conv_w"]).reshape(24, 128),
            f(inputs["conv_b"]).reshape(8, 128), f(inputs["pool_scale"]).reshape(16, 128)], axis=0)),
        "norm2_g": f(inputs["norm2_g"]).reshape(1, D),
        "final_g": f(inputs["final_g"]).reshape(1, D),
    }
    in_maps = []
    per_seq = S // T
    for c in range(NCORES):
        b, hf = divmod(c, per_seq)
        s0 = hf * T
        xc = np.zeros((TT, D), np.float32)
        if s0 > 0:
            xc[0:H] = x[b, s0 - H:s0]
        xc[H:] = x[b, s0:s0 + T]
        pos = np.broadcast_to((s0 + 1 + np.arange(16, dtype=np.float32))[None, :], (128, 16))
        m = dict(shared)
        m["x_c"] = xc
        m["pos16"] = np.ascontiguousarray(pos, dtype=np.float32)
        in_maps.append(m)
    return in_maps, (B, S)


def kernel(**inputs):
    in_maps, (B, S) = make_in_maps(inputs)
    nc = build_nc()
    res = run_bass_kernel_spmd(nc, in_maps, core_ids=list(range(NCORES)))
    out = np.concatenate([np.asarray(r["out"]) for r in res.results], axis=0)
    return out.reshape(B, S, D).astype(np.float32)
```
